# Optimizing a Trainium2 kernel written in Bass

```python
import math
import jax, jax.numpy as jnp
from jax import lax
import numpy as np

D_MODEL = 1024
BATCH = 32
SEQ = 256
DEPTH = 2
DEC_BATCH = 4
DEC_SEQ = 1024
PAST_LEN = 512

GRID_W = 64
QUERY_BLOCK = 128
ROPE_THETA = 10000.0
EPS = 1e-6
DIFF_HEADS = 4
DIFF_QK = 32
DIFF_V = 64
GQA_HEADS = 4
GQA_KV_HEADS = 2
GQA_HEAD_DIM = 64
SSM_WIDTH = 256
SSM_GROUP = 16
SSM_GROUPS = SSM_WIDTH // SSM_GROUP
SSM_STATE = 64
MLA_HEADS = 4
MLA_Q_RANK = 192
MLA_KV_RANK = 128
MLA_NOPE = 64
MLA_ROPE = 32
MLA_V = 64
MIX_WIDTH = DIFF_HEADS * DIFF_V + GQA_HEADS * GQA_HEAD_DIM + SSM_WIDTH + MLA_HEADS * MLA_V
IN_SIZES = (DIFF_HEADS * 2 * DIFF_QK, DIFF_HEADS * 2 * DIFF_QK, DIFF_HEADS * DIFF_V,
            GQA_HEADS * GQA_HEAD_DIM, GQA_KV_HEADS * GQA_HEAD_DIM, GQA_KV_HEADS * GQA_HEAD_DIM,
            SSM_WIDTH, MLA_Q_RANK, MLA_KV_RANK, MLA_ROPE)
IN_COLS = sum(IN_SIZES)
D_FF = 4 * D_MODEL
N_MOD = 6

kernel_name = 'hybrid_prefix_diffusion_step'

F32 = jnp.float32


def rmsnorm(x, g):
    xf = x.astype(F32)
    y = xf * lax.rsqrt(jnp.mean(xf * xf, axis=-1, keepdims=True) + EPS)
    return (y * g.astype(F32)).astype(x.dtype)


def rope_angles(t, rot_dim):
    rows = t // GRID_W
    row = jnp.repeat(jnp.arange(rows), GRID_W).astype(F32)
    col = jnp.tile(jnp.arange(GRID_W), rows).astype(F32)
    n = rot_dim // 4
    freq = ROPE_THETA ** (-jnp.arange(n, dtype=F32) / n)
    return row[:, None] * freq, col[:, None] * freq


def apply_rope2d(x, ang_row, ang_col):
    shape = (x.shape[1],) + (1,) * (x.ndim - 3) + (ang_row.shape[-1],)
    half = x.shape[-1] // 2
    xf = x.astype(F32)

    def rot(part, ang):
        cos = jnp.cos(ang).reshape(shape)
        sin = jnp.sin(ang).reshape(shape)
        x1, x2 = jnp.split(part, 2, axis=-1)
        return jnp.concatenate([x1 * cos - x2 * sin, x1 * sin + x2 * cos], axis=-1)

    out = jnp.concatenate([rot(xf[..., :half], ang_row), rot(xf[..., half:], ang_col)], axis=-1)
    return out.astype(x.dtype)


def attend(q, k, v, scale):
    b, tq, hq, dk = q.shape
    hkv = k.shape[2]
    g = hq // hkv
    dv = v.shape[-1]
    qb = min(QUERY_BLOCK, tq)
    nb = tq // qb
    qs = (q.astype(F32) * scale).reshape(b, nb, qb, hkv, g, dk).transpose(1, 0, 2, 3, 4, 5)
    kf = k.astype(F32)
    vf = v.astype(F32)

    def block(qblk):
        s = jnp.einsum('bqhgd,bkhd->bhgqk', qblk, kf)
        p = jax.nn.softmax(s, axis=-1)
        return jnp.einsum('bhgqk,bkhd->bqhgd', p, vf)

    o = lax.map(block, qs)
    return o.transpose(1, 0, 2, 3, 4, 5).reshape(b, tq, hq, dv)


def diff_attention(q, k, v, p, lam_init):
    lam = (jnp.exp(jnp.sum(p['diff_lq1'].astype(F32) * p['diff_lk1'].astype(F32)))
           - jnp.exp(jnp.sum(p['diff_lq2'].astype(F32) * p['diff_lk2'].astype(F32))) + lam_init)
    scale = DIFF_QK ** -0.5
    o1 = attend(q[..., 0, :], k[..., 0, :], v, scale)
    o2 = attend(q[..., 1, :], k[..., 1, :], v, scale)
    o = rmsnorm(o1 - lam * o2, p['diff_subln_g']) * (1.0 - lam_init)
    return o.reshape(o.shape[0], o.shape[1], -1)


def mla_expand(ckv_n, k_rope, w_ukv):
    b, t, _ = ckv_n.shape
    kv = (ckv_n @ w_ukv).reshape(b, t, MLA_HEADS, MLA_NOPE + MLA_V)
    k_nope, v = kv[..., :MLA_NOPE], kv[..., MLA_NOPE:]
    k_r = jnp.broadcast_to(k_rope[:, :, None, :], (b, t, MLA_HEADS, MLA_ROPE)).astype(k_nope.dtype)
    return jnp.concatenate([k_nope, k_r], axis=-1), v


def _linrec(e1, e2):
    a1, b1 = e1
    a2, b2 = e2
    return a1 * a2, a2 * b1 + b2


def ssm_scan(u, a_re, a_im, log_dt, b_re, b_im, c_re, c_im, h0):
    lam = lax.complex(a_re.astype(F32), a_im.astype(F32))
    dt = jnp.exp(log_dt.astype(F32))[:, None]
    a_bar = jnp.exp(lam * dt)
    b_bar = ((a_bar - 1.0) / lam)[..., None] * lax.complex(b_re.astype(F32), b_im.astype(F32))
    c_mat = lax.complex(c_re.astype(F32), c_im.astype(F32))
    bu = jnp.einsum('gpc,btgc->btgp', b_bar, u.astype(jnp.complex64))
    bu = bu.at[:, 0].add(a_bar * h0)
    a_seq = jnp.broadcast_to(a_bar, bu.shape)
    _, h = lax.associative_scan(_linrec, (a_seq, bu), axis=1)
    y = jnp.einsum('gcp,btgp->btgc', c_mat, h).real
    return y, h[:, -1]


def ssm_mixer(u, h0_re, h0_im, p):
    b, t, _ = u.shape
    uf = u.astype(F32).reshape(b, t, SSM_GROUPS, SSM_GROUP)
    h0 = lax.complex(h0_re.astype(F32), h0_im.astype(F32))
    ys, finals = [], []
    for d in range(2):
        ud = uf if d == 0 else jnp.flip(uf, axis=1)
        y, h_last = ssm_scan(ud, p['ssm_a_re'][d], p['ssm_a_im'][d], p['ssm_log_dt'][d],
                             p['ssm_b_re'][d], p['ssm_b_im'][d], p['ssm_c_re'][d], p['ssm_c_im'][d],
                             h0[:, d])
        ys.append(y if d == 0 else jnp.flip(y, axis=1))
        finals.append(h_last)
    y = ys[0] + ys[1] + uf * p['ssm_d'].astype(F32).reshape(SSM_GROUPS, SSM_GROUP)
    g = jax.nn.gelu(y.reshape(b, t, SSM_WIDTH))
    z = g @ p['ssm_w_glu'].astype(F32)
    out = z[..., :SSM_WIDTH] * jax.nn.sigmoid(z[..., SSM_WIDTH:])
    h_fin = jnp.stack(finals, axis=1)
    return out, h_fin.real, h_fin.imag


def token_mixers(h, p, lam_init, ctx, angs):
    b, t, _ = h.shape
    idx = np.cumsum(IN_SIZES)[:-1].tolist()
    dq, dk, dv, gq, gk, gv, u, cq, ckv, kr = jnp.split(h @ p['w_in'], idx, axis=-1)
    dq = dq.reshape(b, t, DIFF_HEADS, 2, DIFF_QK)
    dk = dk.reshape(b, t, DIFF_HEADS, 2, DIFF_QK)
    dv = dv.reshape(b, t, DIFF_HEADS, DIFF_V)
    gq = rmsnorm(gq.reshape(b, t, GQA_HEADS, GQA_HEAD_DIM), p['gqa_qn_g'])
    gk = rmsnorm(gk.reshape(b, t, GQA_KV_HEADS, GQA_HEAD_DIM), p['gqa_kn_g'])
    gv = gv.reshape(b, t, GQA_KV_HEADS, GQA_HEAD_DIM)
    ckv_n = rmsnorm(ckv, p['mla_kvn_g'])
    mq = (rmsnorm(cq, p['mla_qn_g']) @ p['mla_w_uq']).reshape(b, t, MLA_HEADS, MLA_NOPE + MLA_ROPE)
    own = (dk.reshape(b, t, DIFF_HEADS, 2 * DIFF_QK), dv, gk, gv, ckv_n, kr)
    if angs is not None:
        a_diff, a_gqa, a_mla = angs
        dq = apply_rope2d(dq, *a_diff)
        dk = apply_rope2d(dk, *a_diff)
        gq = apply_rope2d(gq, *a_gqa)
        gk = apply_rope2d(gk, *a_gqa)
        mq = jnp.concatenate([mq[..., :MLA_NOPE], apply_rope2d(mq[..., MLA_NOPE:], *a_mla)], axis=-1)
        kr = apply_rope2d(kr, *a_mla)
    mk, mv = mla_expand(ckv_n, kr, p['mla_w_ukv'])
    if ctx is None:
        h0_re = jnp.zeros((b, 2, SSM_GROUPS, SSM_STATE), F32)
        h0_im = jnp.zeros((b, 2, SSM_GROUPS, SSM_STATE), F32)
    else:
        c_dk, c_dv, c_gk, c_gv, c_ckv, c_kr, h0_re, h0_im = ctx
        dk = jnp.concatenate([c_dk.reshape(b, -1, DIFF_HEADS, 2, DIFF_QK), dk], axis=1)
        dv = jnp.concatenate([c_dv, dv], axis=1)
        gk = jnp.concatenate([c_gk, gk], axis=1)
        gv = jnp.concatenate([c_gv, gv], axis=1)
        c_mk, c_mv = mla_expand(c_ckv, c_kr, p['mla_w_ukv'])
        mk = jnp.concatenate([c_mk, mk], axis=1)
        mv = jnp.concatenate([c_mv, mv], axis=1)
    o_diff = diff_attention(dq, dk, dv, p, lam_init)
    o_gqa = attend(gq, gk, gv, GQA_HEAD_DIM ** -0.5).reshape(b, t, -1)
    o_ssm, hT_re, hT_im = ssm_mixer(u, h0_re, h0_im, p)
    o_mla = attend(mq, mk, mv, (MLA_NOPE + MLA_ROPE) ** -0.5).reshape(b, t, -1)
    mixed = jnp.concatenate([o_diff, o_gqa, o_ssm, o_mla], axis=-1).astype(h.dtype)
    return mixed @ p['w_out'], own + (hT_re, hT_im)


def sq_relu_mlp(h, p):
    a = jax.nn.relu(h @ p['mlp_w1'])
    return (a * a) @ p['mlp_w2']


def adaln(cond, p):
    return jax.nn.silu(cond.astype(F32)) @ p['w_ada'] + p['b_ada']


def layer(x, mod, p, lam_init, ctx, angs):
    sh1, sc1, g1, sh2, sc2, g2 = jnp.split(mod.astype(x.dtype), N_MOD, axis=-1)
    h = rmsnorm(x, p['norm1_g']) * (1.0 + sc1) + sh1
    mix, st = token_mixers(h, p, lam_init, ctx, angs)
    x = x + g1 * mix
    h = rmsnorm(x, p['norm2_g']) * (1.0 + sc2) + sh2
    x = x + g2 * sq_relu_mlp(h, p)
    return x, st


def setup_inputs(seed: int = 0) -> dict:
    key = jax.random.key(seed)
    ks = iter(jax.random.split(key, 64))
    L, G, P, C = DEPTH, SSM_GROUPS, SSM_STATE, SSM_GROUP

    def nrm(shape, scale=1.0):
        return jax.random.normal(next(ks), shape, F32) * scale

    def gain(shape):
        return 1.0 + nrm(shape, 0.02)

    return {
        'x_prompt': nrm((BATCH, SEQ, D_MODEL)),
        'x_sample': nrm((DEC_BATCH, DEC_SEQ, D_MODEL)),
        'cache_diff_k': nrm((DEC_BATCH, L, PAST_LEN, DIFF_HEADS, 2 * DIFF_QK)),
        'cache_diff_v': nrm((DEC_BATCH, L, PAST_LEN, DIFF_HEADS, DIFF_V)),
        'cache_gqa_k': nrm((DEC_BATCH, L, PAST_LEN, GQA_KV_HEADS, GQA_HEAD_DIM)),
        'cache_gqa_v': nrm((DEC_BATCH, L, PAST_LEN, GQA_KV_HEADS, GQA_HEAD_DIM)),
        'cache_mla_ckv': nrm((DEC_BATCH, L, PAST_LEN, MLA_KV_RANK)),
        'cache_mla_krope': nrm((DEC_BATCH, L, PAST_LEN, MLA_ROPE)),
        'state_ssm_re': nrm((DEC_BATCH, L, 2, G, P), 0.3),
        'state_ssm_im': nrm((DEC_BATCH, L, 2, G, P), 0.3),
        'c': nrm((DEC_BATCH, D_MODEL)),
        'c_ctx': nrm((D_MODEL,)),
        'norm1_g': gain((L, D_MODEL)),
        'norm2_g': gain((L, D_MODEL)),
        'w_ada': nrm((L, D_MODEL, N_MOD * D_MODEL), 0.5 * D_MODEL ** -0.5),
        'b_ada': nrm((L, N_MOD * D_MODEL), 0.02),
        'w_in': nrm((L, D_MODEL, IN_COLS), D_MODEL ** -0.5),
        'w_out': nrm((L, MIX_WIDTH, D_MODEL), MIX_WIDTH ** -0.5),
        'diff_lq1': nrm((L, DIFF_QK), 0.1),
        'diff_lk1': nrm((L, DIFF_QK), 0.1),
        'diff_lq2': nrm((L, DIFF_QK), 0.1),
        'diff_lk2': nrm((L, DIFF_QK), 0.1),
        'diff_subln_g': gain((L, DIFF_V)),
        'gqa_qn_g': gain((L, GQA_HEAD_DIM)),
        'gqa_kn_g': gain((L, GQA_HEAD_DIM)),
        'ssm_a_re': -0.5 + nrm((L, 2, G, P), 0.01),
        'ssm_a_im': math.pi * jnp.arange(P, dtype=F32) + nrm((L, 2, G, P), 0.01),
        'ssm_log_dt': jax.random.uniform(next(ks), (L, 2, G), F32, math.log(1e-3), math.log(1e-1)),
        'ssm_b_re': nrm((L, 2, G, P, C), (2 * C) ** -0.5),
        'ssm_b_im': nrm((L, 2, G, P, C), (2 * C) ** -0.5),
        'ssm_c_re': nrm((L, 2, G, C, P), (2 * P) ** -0.5),
        'ssm_c_im': nrm((L, 2, G, C, P), (2 * P) ** -0.5),
        'ssm_d': nrm((L, SSM_WIDTH)),
        'ssm_w_glu': nrm((L, SSM_WIDTH, 2 * SSM_WIDTH), SSM_WIDTH ** -0.5),
        'mla_qn_g': gain((L, MLA_Q_RANK)),
        'mla_kvn_g': gain((L, MLA_KV_RANK)),
        'mla_w_uq': nrm((L, MLA_Q_RANK, MLA_HEADS * (MLA_NOPE + MLA_ROPE)), MLA_Q_RANK ** -0.5),
        'mla_w_ukv': nrm((L, MLA_KV_RANK, MLA_HEADS * (MLA_NOPE + MLA_V)), MLA_KV_RANK ** -0.5),
        'mlp_w1': nrm((L, D_MODEL, D_FF), D_MODEL ** -0.5),
        'mlp_w2': nrm((L, D_FF, D_MODEL), D_FF ** -0.5),
        'final_norm_g': gain((D_MODEL,)),
    }


def reference(x_prompt, x_sample, cache_diff_k, cache_diff_v, cache_gqa_k, cache_gqa_v, cache_mla_ckv,
              cache_mla_krope, state_ssm_re, state_ssm_im, c, c_ctx, norm1_g, norm2_g, w_ada, b_ada, w_in,
              w_out, diff_lq1, diff_lk1, diff_lq2, diff_lk2, diff_subln_g, gqa_qn_g, gqa_kn_g, ssm_a_re,
              ssm_a_im, ssm_log_dt, ssm_b_re, ssm_b_im, ssm_c_re, ssm_c_im, ssm_d, ssm_w_glu, mla_qn_g,
              mla_kvn_g, mla_w_uq, mla_w_ukv, mlp_w1, mlp_w2, final_norm_g):
    layers = [dict(norm1_g=norm1_g[l], norm2_g=norm2_g[l], w_ada=w_ada[l], b_ada=b_ada[l], w_in=w_in[l],
                   w_out=w_out[l], diff_lq1=diff_lq1[l], diff_lk1=diff_lk1[l], diff_lq2=diff_lq2[l],
                   diff_lk2=diff_lk2[l], diff_subln_g=diff_subln_g[l], gqa_qn_g=gqa_qn_g[l],
                   gqa_kn_g=gqa_kn_g[l], ssm_a_re=ssm_a_re[l], ssm_a_im=ssm_a_im[l],
                   ssm_log_dt=ssm_log_dt[l], ssm_b_re=ssm_b_re[l], ssm_b_im=ssm_b_im[l],
                   ssm_c_re=ssm_c_re[l], ssm_c_im=ssm_c_im[l], ssm_d=ssm_d[l], ssm_w_glu=ssm_w_glu[l],
                   mla_qn_g=mla_qn_g[l], mla_kvn_g=mla_kvn_g[l], mla_w_uq=mla_w_uq[l],
                   mla_w_ukv=mla_w_ukv[l], mlp_w1=mlp_w1[l], mlp_w2=mlp_w2[l])
              for l in range(DEPTH)]
    lam_inits = [0.8 - 0.6 * math.exp(-0.3 * l) for l in range(DEPTH)]

    x = x_prompt
    ctx_states = []
    for l in range(DEPTH):
        p = layers[l]
        x, st = layer(x, adaln(c_ctx, p)[None, None], p, lam_inits[l], None, None)
        ctx_states.append(st)
    y_prompt = rmsnorm(x, final_norm_g)

    t_lat = x_sample.shape[1]
    angs = (rope_angles(t_lat, DIFF_QK), rope_angles(t_lat, GQA_HEAD_DIM), rope_angles(t_lat, MLA_ROPE))
    x = x_sample
    for l in range(DEPTH):
        p = layers[l]
        ctx = (cache_diff_k[:, l], cache_diff_v[:, l], cache_gqa_k[:, l], cache_gqa_v[:, l],
               cache_mla_ckv[:, l], cache_mla_krope[:, l], state_ssm_re[:, l], state_ssm_im[:, l])
        x, _ = layer(x, adaln(c, p)[:, None], p, lam_inits[l], ctx, angs)
    y_sample = rmsnorm(x, final_norm_g)

    new_diff_k = jnp.stack([s[0] for s in ctx_states], axis=1)
    new_diff_v = jnp.stack([s[1] for s in ctx_states], axis=1)
    new_gqa_k = jnp.stack([s[2] for s in ctx_states], axis=1)
    new_gqa_v = jnp.stack([s[3] for s in ctx_states], axis=1)
    new_mla_ckv = jnp.stack([s[4] for s in ctx_states], axis=1)
    new_mla_krope = jnp.stack([s[5] for s in ctx_states], axis=1)
    new_ssm_re = jnp.stack([s[6] for s in ctx_states], axis=1)
    new_ssm_im = jnp.stack([s[7] for s in ctx_states], axis=1)
    return (y_prompt, y_sample, new_diff_k, new_diff_v, new_gqa_k, new_gqa_v, new_mla_ckv, new_mla_krope,
            new_ssm_re, new_ssm_im)
```

```python
import math
import numpy as np
import concourse.bass as bass
import concourse.mybir as mybir
from concourse.bass_utils import run_bass_kernel_spmd

F32 = mybir.dt.float32
BF16 = mybir.dt.bfloat16
I32 = mybir.dt.int32
AF = mybir.ActivationFunctionType
ALU = mybir.AluOpType
AX = mybir.AxisListType

ENGS = ("pe", "act", "dve", "pool", "sp")
D = 1024
NT = 8
EPS = 1e-6
TWO_PI = 2.0 * math.pi
IN_COLS = 1888
LAM_INIT = [0.8 - 0.6 * math.exp(-0.3 * l) for l in range(2)]


class Sched:
    def __init__(self, nc, n_dma_sems=24):
        self.nc = nc
        self.eng = dict(pe=nc.tensor, act=nc.scalar, dve=nc.vector, pool=nc.gpsimd, sp=nc.sync)
        self.ops = []
        self.last_w = {}
        self.readers = {}
        self.extra = {}
        self.window = 64
        self.reorder = True
        self.n_dma_sems = n_dma_sems

    def add(self, eng, fn, reads=(), writes=(), dma=False, cost=None, lat=0.0):
        idx = len(self.ops)
        writes = list(writes) + [k for k in reads if isinstance(k, tuple) and k and k[0] == "ps" and k not in writes]
        deps = set()
        for k in list(reads) + list(writes):
            deps |= self.extra.get(k, set())
        for k in reads:
            w = self.last_w.get(k)
            if w is not None:
                deps.add(w)
        for k in writes:
            w = self.last_w.get(k)
            if w is not None:
                deps.add(w)
            for r in self.readers.get(k, ()):
                deps.add(r)
        for k in reads:
            self.readers.setdefault(k, []).append(idx)
        for k in writes:
            self.last_w[k] = idx
            self.readers[k] = []
        deps.discard(idx)
        if cost is None:
            cost = {'pe': 0.3, 'act': 0.4, 'dve': 0.4, 'pool': 0.6, 'sp': 0.1}[eng]
        self.ops.append(dict(eng=eng, fn=fn, deps=deps, dma=dma, sig=False, cost=cost, lat=lat))
        return idx

    def alias(self, new_keys, old_keys):
        acc = set()
        for k in old_keys:
            w = self.last_w.get(k)
            if w is not None:
                acc.add(w)
            acc.update(self.readers.get(k, ()))
        for k in new_keys:
            self.extra[k] = self.extra.get(k, set()) | acc

    def dma(self, q, out, in_, reads=(), writes=(), **kw):
        n = 1
        for s_ in out.shape:
            n *= s_
        nbytes = n * 4
        cost = 0.15 if q == "sp" else 0.8
        return self.add(q, lambda h: h.dma_start(out=out, in_=in_, **kw), reads, writes, dma=True, cost=cost,
                        lat=2.0 + nbytes / 150e3)

    def schedule(self):
        import heapq
        ops = self.ops
        n = len(ops)
        succ = [[] for _ in range(n)]
        indeg = [0] * n
        for i, op in enumerate(ops):
            indeg[i] = len(op["deps"])
            for j in op["deps"]:
                succ[j].append(i)
        ready_t = [0.0] * n
        fin = [0.0] * n
        bl = [0.0] * n
        for i in range(n - 1, -1, -1):
            m = 0.0
            for k in succ[i]:
                if bl[k] > m:
                    m = bl[k]
            bl[i] = ops[i]["cost"] + ops[i]["lat"] + 0.02 + m
        heaps = {e: [] for e in ENGS}
        free = {e: 0.0 for e in ENGS}
        for i in range(n):
            if indeg[i] == 0:
                heapq.heappush(heaps[ops[i]["eng"]], (0.0, i))
        order = []
        WINDOW = self.window
        while len(order) < n:
            best = None
            for e in ENGS:
                h = heaps[e]
                if not h:
                    continue
                rt, i = h[0]
                start = max(rt, free[e])
                cand = (start, i, e)
                if best is None or cand < best:
                    best = cand
            start, i, e = best
            h = heaps[e]
            pool_ = []
            while h and h[0][0] <= start and len(pool_) < WINDOW:
                pool_.append(heapq.heappop(h))
            pick = max(pool_, key=lambda t: (bl[t[1]], -t[1]))
            for t in pool_:
                if t is not pick:
                    heapq.heappush(h, t)
            i = pick[1]
            op = ops[i]
            st_ = max(pick[0], free[e])
            op["t0"] = st_
            free[e] = st_ + op["cost"]
            fin[i] = st_ + op["cost"] + op["lat"]
            order.append(i)
            for k in succ[i]:
                same = (ops[k]["eng"] == e)
                t_ = fin[i] + (0.0 if same and e == "pe" else 0.02)
                if t_ > ready_t[k]:
                    ready_t[k] = t_
                indeg[k] -= 1
                if indeg[k] == 0:
                    heapq.heappush(heaps[ops[k]["eng"]], (ready_t[k], k))
        self.est_time = max(fin) if fin else 0.0
        return order

    def emit(self):
        nc = self.nc
        ops = self.ops
        for i, op in enumerate(ops):
            for j in op["deps"]:
                pj = ops[j]
                if pj["dma"]:
                    continue
                if pj["eng"] == "pe" and op["eng"] == "pe" and not op["dma"]:
                    continue
                pj["sig"] = True
        esem = {e: nc.alloc_semaphore(f"es_{e}") for e in ENGS}
        nd = self.n_dma_sems
        dsem = {q: [nc.alloc_semaphore(f"ds_{q}_{i}") for i in range(nd)] for q in ("sp", "pool", "act")}
        ecount = {e: 0 for e in ENGS}
        dcount = {q: [0] * nd for q in dsem}
        known = {e: {} for e in ENGS}
        dma_i = {q: 0 for q in dsem}
        nwaits = 0
        final = {}
        order = self.schedule() if self.reorder else list(range(len(ops)))
        for i in order:
            op = ops[i]
            e = op["eng"]
            h = self.eng[e]
            waits = {}
            for j in op["deps"]:
                pj = ops[j]
                if pj["dma"]:
                    key, sem, val = ("d",) + pj["dslot"], dsem[pj["dslot"][0]][pj["dslot"][1]], pj["dval"]
                elif pj["eng"] == "pe" and e == "pe" and not op["dma"]:
                    continue
                else:
                    key, sem, val = ("e", pj["eng"]), esem[pj["eng"]], pj["sigval"]
                if key not in waits or waits[key][1] < val:
                    waits[key] = (sem, val)
            if op["dma"]:
                s = dma_i[e] % nd
                dma_i[e] += 1
                if dcount[e][s] > 0:
                    key = ("d", e, s)
                    if key not in waits or waits[key][1] < dcount[e][s]:
                        waits[key] = (dsem[e][s], dcount[e][s])
            for key, (sem, val) in waits.items():
                if known[e].get(key, 0) >= val:
                    continue
                h.wait_ge(sem, val)
                known[e][key] = val
                nwaits += 1
            ins = op["fn"](h)
            if op["dma"]:
                dcount[e][s] += 16
                ins.then_inc(dsem[e][s], 16)
                op["dslot"] = (e, s)
                op["dval"] = dcount[e][s]
                final[("d", e, s)] = (dsem[e][s], dcount[e][s])
            elif op["sig"]:
                ecount[e] += 1
                ins.then_inc(esem[e], 1)
                op["sigval"] = ecount[e]
        h = self.eng["sp"]
        for key, (sem, val) in final.items():
            if known["sp"].get(key, 0) >= val:
                continue
            h.wait_ge(sem, val)
        return dict(n_ops=len(ops), n_waits=nwaits, counts=dict(ecount))


class Rot:
    def __init__(self, items):
        self.items = list(items)
        self.i = 0

    def next(self):
        v = self.items[self.i % len(self.items)]
        self.i += 1
        return v


class _Stop(Exception):
    pass


def build_program(debug=None, stop_after=None):
    try:
        return _build_program(debug, stop_after)
    except _Stop as e:
        return e.args[0]


def _build_program(debug=None, stop_after=None):
    nc = bass.Bass("TRN2", target_bir_lowering=False)
    S = Sched(nc)
    dbg_outs = {}

    def din(name, shape):
        return nc.dram_tensor(name, list(shape), F32, kind="ExternalInput")

    def dout(name, shape):
        return nc.dram_tensor(name, list(shape), F32, kind="ExternalOutput")

    xp_d = din("xp", [1024, D])
    xs_d = din("xs", [1024, D])
    cvec_d = din("cvec", [2, D])
    cdk_d = din("cdk", [2, 512, 256])
    cdv_d = din("cdv", [2, 512, 256])
    cgk_d = din("cgk", [2, 512, 128])
    cgv_d = din("cgv", [2, 512, 128])
    cckv_d = din("cckv", [2, 512, 128])
    ckr_d = din("ckr", [2, 512, 32])
    h0re_d = din("h0re", [2, 32, 64])
    h0im_d = din("h0im", [2, 32, 64])
    W = {}
    for name, shape in [
        ("norm1_g", [2, D]), ("norm2_g", [2, D]), ("w_ada", [2, D, 6 * D]), ("b_ada", [2, 6 * D]),
        ("w_in", [2, D, IN_COLS]), ("w_out", [2, D, D]),
        ("diff_lq1", [2, 32]), ("diff_lk1", [2, 32]), ("diff_lq2", [2, 32]), ("diff_lk2", [2, 32]),
        ("diff_subln_g", [2, 64]), ("gqa_qn_g", [2, 64]), ("gqa_kn_g", [2, 64]),
        ("ssm_a_re", [2, 32, 64]), ("ssm_a_im", [2, 32, 64]), ("ssm_log_dt", [2, 32]),
        ("ssm_b_re", [2, 32, 1024]), ("ssm_b_im", [2, 32, 1024]),
        ("ssm_c_re", [2, 512, 64]), ("ssm_c_im", [2, 512, 64]),
        ("ssm_d", [2, 256]), ("ssm_w_glu", [2, 256, 512]),
        ("mla_qn_g", [2, 192]), ("mla_kvn_g", [2, 128]),
        ("mla_w_uq", [2, 192, 384]), ("mla_w_ukv", [2, 128, 512]),
        ("mlp_w1", [2, D, 4 * D]), ("mlp_w2", [2, 4 * D, D]), ("final_norm_g", [D]),
    ]:
        W[name] = din(name, shape)
    yp_d = dout("yp", [1024, D])
    ys_d = dout("ys", [1024, D])
    ndk_d = dout("ndk", [4, 2, 256, 256])
    ndv_d = dout("ndv", [4, 2, 256, 256])
    ngk_d = dout("ngk", [4, 2, 256, 128])
    ngv_d = dout("ngv", [4, 2, 256, 128])
    nckv_d = dout("nckv", [4, 2, 256, 128])
    nkr_d = dout("nkr", [4, 2, 256, 32])
    nsre_d = dout("nsre", [4, 2, 32, 64])
    nsim_d = dout("nsim", [4, 2, 32, 64])
    modscr = nc.dram_tensor("modscr", [2, 2, 6 * D], F32)
    ssm_scr_b = nc.dram_tensor("ssm_scr_b", [2, 4, 128, 32 * 128], BF16)
    ssm_scr_f = nc.dram_tensor("ssm_scr_f", [2, 2, 128, 32 * 128], F32)
    ssm_scr_r = nc.dram_tensor("ssm_scr_r", [2, 128, 32], F32)
    ssm_scr_c1 = nc.dram_tensor("ssm_scr_c1", [2, 2, 128, 32], F32)

    def sb(name, shape, dt=F32):
        return nc.alloc_sbuf_tensor(name, list(shape), dt)

    PS = [nc.alloc_psum_tensor(f"ps{b}", [128, 512], F32) for b in range(8)]
    mm_rot = Rot([0, 1, 2, 3])
    tr_rot = Rot([4, 5])
    aux_rot = Rot([6, 7])

    def psf(b):
        return PS[b][:, :]

    def psb(b):
        return PS[b][:, :].bitcast(BF16)

    def pk(b):
        return ("ps", b)

    x_sb = sb("x_sb", [128, NT, D])
    modb = sb("modb", [128, 1, D])
    modcol = sb("modcol", [128, 2, 3, 8])
    hb2 = sb("hb2", [128, D], BF16)
    actT = sb("actT", [128, 8, 1024], BF16)
    tmpf = sb("tmpf", [128, D])
    hb = sb("hb", [128, D], BF16)
    junk = sb("junk", [128, D], BF16)
    stat = sb("stat", [128, 64])
    own = sb("own", [128, 1, 928])
    ident = sb("ident", [128, 128], BF16)
    identf = sb("identf", [128, 128])
    ARENA_BYTES = 104 * 1024
    arena = sb("arena", [128, ARENA_BYTES // 2], BF16)

    def carve(off, shape, dt):
        n = 1
        for s_ in shape[1:]:
            n *= s_
        esz = 2 if dt == BF16 else 4
        assert off % 4 == 0
        assert off + n * esz <= ARENA_BYTES, (off, n * esz)
        ap = arena[:, off // 2: off // 2 + n * esz // 2]
        if dt != BF16:
            ap = ap.bitcast(dt)
        if len(shape) == 2:
            return ap
        names = " ".join(f"d{i}" for i in range(len(shape) - 1))
        kw = {f"d{i}": shape[i + 1] for i in range(len(shape) - 1)}
        return ap.rearrange(f"p ({names}) -> p {names}", **kw)

    def fsz(ap):
        n = 1
        for s_ in ap.shape[1:]:
            n *= s_
        return n

    def ecost(eng, n):
        if eng == "dve":
            return 0.12 + n / 960.0
        if eng == "act":
            return 0.2 + n / 1200.0
        if eng == "pool":
            return 0.3 + n / 450.0
        return 0.3

    def E(eng, fn, reads=(), writes=(), cost=None):
        return S.add(eng, fn, reads, writes, cost=cost)

    def tt(eng, out, in0, in1, op, reads, writes):
        return E(eng, lambda h: h.tensor_tensor(out=out, in0=in0, in1=in1, op=op), reads, writes, cost=ecost(eng, fsz(out)))

    def ts(eng, out, in0, s1, s2, op0, op1, reads, writes):
        c = ecost(eng, fsz(out))
        if op1 is None:
            return E(eng, lambda h: h.tensor_scalar(out, in0, s1, None, op0=op0), reads, writes, cost=c)
        return E(eng, lambda h: h.tensor_scalar(out, in0, s1, s2, op0=op0, op1=op1), reads, writes, cost=c)

    def stt(out, in0, scalar, in1, op0, op1, reads, writes, eng="dve"):
        return E(eng, lambda h: h.scalar_tensor_tensor(out=out, in0=in0, scalar=scalar, in1=in1, op0=op0, op1=op1),
                 reads, writes, cost=ecost(eng, fsz(out)))

    def actf(out, in_, func, reads, writes, scale=None, bias=None, accum_out=None):
        kw = {}
        if scale is not None:
            kw["scale"] = scale
        if bias is not None:
            kw["bias"] = bias
        if accum_out is not None:
            kw["accum_out"] = accum_out
        return E("act", lambda h: h.activation(out=out, in_=in_, func=func, **kw), reads, writes,
                 cost=ecost("act", fsz(out)) + (0.1 if accum_out is not None else 0.0))

    def cp(eng, out, in_, reads, writes):
        c = ecost(eng, fsz(out))
        if eng == "act":
            return E("act", lambda h: h.copy(out, in_), reads, writes, cost=c)
        return E(eng, lambda h: h.tensor_copy(out, in_), reads, writes, cost=c)

    def mmcost(l, r):
        nn = fsz(r)
        f32 = (r.dtype == F32)
        return 0.035 + (nn * (4.0 if f32 else 1.0)) / 2400.0

    def mm(out, pairs, reads, writes):
        def fn(h):
            ins = None
            n = len(pairs)
            for i, (l, r) in enumerate(pairs):
                ins = h.matmul(out, lhsT=l, rhs=r, start=(i == 0), stop=(i == n - 1))
            return ins
        return E("pe", fn, reads, writes, cost=sum(mmcost(l, r) for (l, r) in pairs) + 0.1)

    def mm1(out, l, r, start, stop, reads, writes, skip=False):
        return E("pe", lambda h: h.matmul(out, lhsT=l, rhs=r, start=start, stop=stop, skip_group_check=skip), reads, writes,
                 cost=mmcost(l, r) + 0.1)

    def trs(items, reads, writes):
        def fn(h):
            ins = None
            for (o, i_, idn) in items:
                ins = h.transpose(o, i_, idn)
            return ins
        return E("pe", fn, reads, writes, cost=0.1 + sum(0.06 + (fsz(i_) if False else 128) * (2.0 if i_.dtype == F32 else 1.0) / 2400.0 for (o, i_, idn) in items))

    def memset(eng, ap, val, writes):
        return E(eng, lambda h: h.memset(ap, val), (), writes, cost=ecost(eng, fsz(ap)))

    def dbg(name, ap, shape, reads, q="sp"):
        if debug is None or name not in debug:
            return
        t = dout("dbg_" + name, shape)
        S.dma(q, t.ap(), ap, reads=reads)
        dbg_outs[name] = t

    _uid = [0]

    def uid(p):
        _uid[0] += 1
        return f"{p}{_uid[0]}"

    def rstd_chain(src, dst, mul, rk, wk):
        ts("dve", dst, src, mul, EPS, ALU.mult, ALU.add, [rk], [wk])
        actf(dst, dst, AF.Sqrt, [wk], [wk])
        E("dve", lambda h: h.reciprocal(dst, dst), [wk], [wk])

    MAGIC = 12582912.0

    def range_reduce(eng, ap, key, itmp, ikey, ftmp, fkey):
        actf(ftmp, ap, AF.Identity, [key], [fkey], scale=1.0 / TWO_PI, bias=MAGIC)
        actf(ftmp, ftmp, AF.Identity, [fkey], [fkey], bias=-MAGIC)
        stt(ap, ftmp, -TWO_PI, ap, ALU.mult, ALU.add, [fkey, key], [key], eng=eng)
        ts(eng, ap, ap, 3.1415925, -3.1415925, ALU.min, ALU.max, [key], [key])

    memset("pool", identf[:], 0.0, ["identf"])
    E("pool", lambda h: h.affine_select(out=identf[:], in_=identf[:], compare_op=ALU.not_equal, fill=1.0,
                                        base=0, pattern=[[-1, 128]], channel_multiplier=1), ["identf"], ["identf"])
    cp("dve", ident[:], identf[:], ["identf"], ["ident"])
    rpad = sb("rpad", [128, 8, 240], BF16)
    rpadf = sb("rpadf", [128, 16])
    memset("pool", rpad[:], 0.0, ["rpad"])
    for b in range(8):
        memset("pool", rpadf[:], 0.0, ["rpadf"])
        E("pool", lambda h, b=b: h.affine_select(out=rpadf[:], in_=rpadf[:], compare_op=ALU.not_equal, fill=1.0,
                                                 base=-16 * b, pattern=[[-1, 16]], channel_multiplier=1),
          ["rpadf"], ["rpadf"])
        cp("pool", rpad[:, b, 112:128], rpadf[:], ["rpadf"], ["rpad"])
    j2 = sb("j2", [128, 128])
    memset("pool", j2[:], 0.0, ["j2"])
    E("pool", lambda h: h.affine_select(out=j2[:, 64:128], in_=j2[:, 64:128], compare_op=ALU.not_equal, fill=1.0,
                                        base=0, pattern=[[-1, 64]], channel_multiplier=1), ["j2"], ["j2"])
    E("pool", lambda h: h.affine_select(out=j2[:, 0:64], in_=j2[:, 0:64], compare_op=ALU.not_equal, fill=1.0,
                                        base=-64, pattern=[[-1, 64]], channel_multiplier=1), ["j2"], ["j2"])
    epad = sb("epad", [128, 192])
    memset("pool", epad[:], 0.0, ["epad"])
    E("pool", lambda h: h.affine_select(out=epad[:, 64:128], in_=epad[:, 64:128], compare_op=ALU.not_equal, fill=1.0,
                                        base=0, pattern=[[-1, 64]], channel_multiplier=1), ["epad"], ["epad"])
    pidx_i = sb("pidx_i", [128, 4], I32)
    pidx_f = sb("pidx_f", [128, 8])
    E("pool", lambda h: h.iota(pidx_i[:, 0:1], pattern=[[0, 1]], base=0, channel_multiplier=1), (), ["pidx_i"])
    E("dve", lambda h: h.tensor_single_scalar(pidx_i[:, 1:2], pidx_i[:, 0:1], 63, ALU.bitwise_and), ["pidx_i"], ["pidx_i1"])
    E("dve", lambda h: h.tensor_single_scalar(pidx_i[:, 2:3], pidx_i[:, 0:1], 6, ALU.arith_shift_right), ["pidx_i"], ["pidx_i2"])
    E("dve", lambda h: h.tensor_single_scalar(pidx_i[:, 3:4], pidx_i[:, 0:1], 4, ALU.arith_shift_right), ["pidx_i"], ["pidx_i3"])
    cp("dve", pidx_f[:, 0:4], pidx_i[:, 0:4], ["pidx_i", "pidx_i1", "pidx_i2", "pidx_i3"], ["pidx_f"])
    ts("dve", pidx_f[:, 4:5], pidx_f[:, 2:3], 2.0, -1.0, ALU.mult, ALU.add, ["pidx_f"], ["sgnv"])
    sgnv = pidx_f[:, 4:5]

    if stop_after == "const":
        S.dma("sp", ndk_d.ap()[0, 0, 0:128, 0:128], identf[:, :], reads=["identf"])
        S.dma("sp", ndk_d.ap()[0, 0, 128:256, 0:128], j2[:, :], reads=["j2"])
        S.dma("sp", ndv_d.ap()[0, 0, 0:128, 0:8], pidx_f[:, :], reads=["pidx_f", "sgnv"])
        st = S.emit()
        return nc, dbg_outs, st
    csT = sb("csT", [128, 8, 2], BF16)
    tokf = sb("tokf", [128, 1024])
    WADA_OFF = ARENA_BYTES - 16384

    def mod_phase():
        cs = x_sb[0:2, 0, :]
        S.dma("sp", cs, cvec_d.ap(), writes=[("x", 0)])
        actf(cs, cs, AF.Silu, [("x", 0)], [("x", 0)])
        bT = tr_rot.next()
        trs([(psf(bT)[:, 2 * kc:2 * kc + 2], cs[:, kc * 128:(kc + 1) * 128], identf[0:2, 0:2]) for kc in range(8)],
            [("x", 0), "identf"], [pk(bT)])
        cp("dve", csT[:], psf(bT)[:, 0:16].rearrange("p (k t) -> p k t", t=2), [pk(bT)], ["csT"])
        wada = [carve(WADA_OFF + i * 8192, [128, 8, 512], BF16) for i in range(2)]
        bada = tokf[0:2, :].rearrange("p (b n) -> p b n", b=2)
        modrow = x_sb[0:2, 4:6, 0:512]
        BK_ = [[("tokf", 0), ("tokf", 1)], [("tokf", 2), ("tokf", 3), ("tokf", 4)]]
        ci = 0
        for l in range(2):
            for nchunk in range(12):
                bi = ci % 2
                ci += 1
                wk = ("wada", bi)
                S.dma("pool", wada[bi], W["w_ada"].ap()[l, :, nchunk * 512:(nchunk + 1) * 512].rearrange("(k p) n -> p k n", p=128),
                      writes=[wk])
                S.dma("sp", bada[:, bi, :], W["b_ada"].ap()[l:l + 1, nchunk * 512:(nchunk + 1) * 512].to_broadcast([2, 512]),
                      writes=BK_[bi])
                b = mm_rot.next()
                mm(psf(b)[0:2, :], [(csT[:, kc, :], wada[bi][:, kc, :]) for kc in range(8)], ["csT", wk], [pk(b)])
                tt("dve", modrow[:, bi, :], psf(b)[0:2, :], bada[:, bi, :], ALU.add,
                   [pk(b)] + BK_[bi], [("x", 4 + bi)])
                S.dma("sp", modscr.ap()[l, :, nchunk * 512:(nchunk + 1) * 512], modrow[:, bi, :], reads=[("x", 4 + bi)],
                      writes=["modscr"])
                yield

    modgen = mod_phase()

    def mod_step(n=1):
        for _ in range(n):
            next(modgen, None)

    cos32 = sb("cos32", [128, NT, 32])
    sin32 = sb("sin32", [128, NT, 32])
    cos64 = sb("cos64", [128, NT, 64])
    sin64 = sb("sin64", [128, NT, 64])
    rt_i = x_sb[:, 1, 0:256].bitcast(I32)
    rt_f = x_sb[:, 2, 0:256]
    rt_a = x_sb[:, 3, 0:256]
    rowf = sb("rowf", [128, NT])
    rowi = sb("rowi", [128, NT], I32)
    E("pool", lambda h: h.iota(rowi[:], pattern=[[2, NT]], base=0, channel_multiplier=0), (), ["rowi"])
    cp("dve", rowf[:], rowi[:], ["rowi"], ["rowf"])
    ts("dve", rowf[:], rowf[:], pidx_f[:, 2:3], None, ALU.add, None, ["rowf", "pidx_f"], ["rowf"])
    for (n, cs_t, sn_t) in ((8, cos32, sin32), (16, cos64, sin64)):
        fr_i = sb(f"fr_i{n}", [128, n], I32)
        fr = sb(f"fr{n}", [128, n])
        E("pool", lambda h, fr_i=fr_i, n=n: h.iota(fr_i[:], pattern=[[1, n]], base=0, channel_multiplier=0), (), [f"fr_i{n}"])
        cp("dve", fr[:], fr_i[:], [f"fr_i{n}"], [f"fr{n}"])
        actf(fr[:], fr[:], AF.Exp, [f"fr{n}"], [f"fr{n}"], scale=-math.log(10000.0) / n)
        W4 = 4 * n
        for which in ("sin", "cos"):
            ang = rt_a[:, 0:NT * 2 * n].rearrange("p (t a n) -> p t a n", t=NT, a=2)
            key = ("x", 3)
            tt("dve", ang[:, :, 0, :], rowf[:].unsqueeze(2).to_broadcast([128, NT, n]),
               fr[:].unsqueeze(1).to_broadcast([128, NT, n]), ALU.mult, ["rowf", f"fr{n}"], [key])
            tt("dve", ang[:, :, 1, :], pidx_f[:, 1:2].unsqueeze(2).to_broadcast([128, NT, n]),
               fr[:].unsqueeze(1).to_broadcast([128, NT, n]), ALU.mult, ["pidx_f", f"fr{n}", key], [key])
            flat = rt_a[:, 0:NT * 2 * n]
            if which == "cos":
                ts("dve", flat, flat, math.pi / 2, None, ALU.add, None, [key], [key])
            range_reduce("dve", flat, key, rt_i[:, 0:NT * 2 * n], ("x", 1), rt_f[:, 0:NT * 2 * n], ("x", 2))
            actf(flat, flat, AF.Sin, [key], [key])
            if which == "cos":
                dst = cs_t[:].rearrange("p t (a two n) -> p t a two n", a=2, two=2)
                for two in range(2):
                    cp("dve", dst[:, :, :, two, :], ang, [key], [f"cos{W4}"])
            else:
                dst = sn_t[:].rearrange("p t (a two n) -> p t a two n", a=2, two=2)
                cp("dve", dst[:, :, :, 0, :], ang, [key], [f"sin{W4}"])
                ts("dve", dst[:, :, :, 1, :], ang, -1.0, None, ALU.mult, None, [key], [f"sin{W4}"])

    if stop_after == "rope":
        S.dma("sp", ndk_d.ap()[0, 0, 0:128, 0:256], cos32[:].rearrange("p t w -> p (t w)"), reads=["cos32"])
        S.dma("sp", ndk_d.ap()[0, 1, 0:128, 0:256], sin32[:].rearrange("p t w -> p (t w)"), reads=["sin32"])
        st = S.emit()
        return nc, dbg_outs, st
    if stop_after == "mod":
        mod_step(100)
        st = S.emit()
        return nc, dbg_outs, st
    def _dma_nc(h, out, in_):
        return h.dma_start(out=out, in_=in_, allow_slow_non_contiguous=True)

    def ckpt(name):
        if stop_after == name:
            st = S.emit()
            raise _Stop((nc, dbg_outs, st))

    def ssm_build(l):
        P = f"sb{l}_"
        off = [0]

        def cv(shape, dt=F32):
            n = 1
            for s_ in shape[1:]:
                n *= s_
            esz = 2 if dt == BF16 else 4
            o = off[0]
            off[0] += (n * esz + 3) // 4 * 4
            assert off[0] <= WADA_OFF, off[0]
            return carve(o, shape, dt)

        if l > 0:
            S.alias([P + "all"], [k for k in S.last_w.keys() if isinstance(k, str) and k.startswith(f"sb{l-1}_")])
        ALL = [P + "all"] if l > 0 else []
        aRI = cv([128, 2, 32])
        dtt = cv([128, 32]); adt = cv([128, 32]); th = cv([128, 32])
        pim = cv([128, 5, 32, 8]); pre = cv([128, 5, 32, 8])
        cf = cv([128, 8, 32]); r8 = cv([128, 32])
        Bb = cv([128, 2, 32, 16]); Cri = cv([128, 2, 32, 16])
        mask = cv([128, 2, 128]); dcol = cv([128, 16]); phi = cv([128, 32])
        R1 = off[0]
        ld = cv([32, 2, 128])
        e5i = cv([128, 5, 32, 8], I32); e5 = cv([128, 5, 32, 8])
        mag = cv([128, 5, 32, 8]); ang = cv([128, 5, 32, 8])
        NE = 5 * 32 * 8
        itmp = cv([128, NE], I32); ftmp = cv([128, NE])
        Bsb = cv([32, 2, 1024]); Bri = cv([128, 2, 32, 16]); t1 = cv([128, 32, 16]); t2 = cv([128, 32, 16])
        Csb = cv([128, 2, 4, 64])
        mski = cv([128, 128], I32); mskf = cv([128, 128])
        early_keys = [P + k for k in ("ld", "e5i", "e5", "mag", "ang", "itmp", "ftmp", "Bsb", "Bri", "t1", "t2", "Csb", "mski", "mskf")]

        for half in range(2):
            S.dma("sp", ld[0:32, 0, half * 64:(half + 1) * 64], W["ssm_a_re"].ap()[l], reads=ALL, writes=[P + "ld"])
            S.dma("sp", ld[0:32, 1, half * 64:(half + 1) * 64], W["ssm_a_im"].ap()[l], reads=ALL, writes=[P + "ld"])
        bT_ = tr_rot.next()
        trs([(psf(bT_)[:, ri * 32:(ri + 1) * 32], ld[0:32, ri, :], identf[0:32, 0:32]) for ri in range(2)],
            [P + "ld", "identf"], [pk(bT_)])
        cp("dve", aRI.rearrange("p r g -> p (r g)"), psf(bT_)[:, 0:64], [pk(bT_)] + ALL, [P + "aRI"])
        aRe = aRI[:, 0, :]
        aIm = aRI[:, 1, :]
        S.dma("sp", dtt, W["ssm_log_dt"].ap()[l:l + 1, :].to_broadcast([128, 32]), reads=ALL, writes=[P + "dt"])
        actf(dtt, dtt, AF.Exp, [P + "dt"], [P + "dt"])
        tt("dve", adt, aRe, dtt, ALU.mult, [P + "aRI", P + "dt"] + ALL, [P + "adt"])
        tt("dve", th, aIm, dtt, ALU.mult, [P + "aRI", P + "dt"] + ALL, [P + "th"])
        specs = [(0, 0, 0, -1), (0, 1, 0, 1), (1, 0, 0, 1), (1, 1, 0, -1),
                 (2, 0, 7, -1), (2, 1, 0, 1), (3, 0, 1, 1), (3, 1, 8, -1), (4, 0, 1, 0), (4, 1, 1, 0)]
        for (slot, d_, base, step) in specs:
            E("pool", lambda h, slot=slot, d_=d_, base=base, step=step:
              h.iota(e5i[:, slot, d_ * 16:(d_ + 1) * 16, :], pattern=[[0, 16], [step, 8]], base=base, channel_multiplier=0),
              ALL, [P + "e5i"])
        fl = lambda a: a.rearrange("p a g s -> p (a g s)")
        cp("dve", fl(e5), fl(e5i), [P + "e5i"] + ALL, [P + "e5"])
        for a in range(5):
            tt("dve", mag[:, a], adt.unsqueeze(2).to_broadcast([128, 32, 8]), e5[:, a], ALU.mult,
               [P + "adt", P + "e5"] + ALL, [P + "mag"])
            tt("dve", ang[:, a], th.unsqueeze(2).to_broadcast([128, 32, 8]), e5[:, a], ALU.mult,
               [P + "th", P + "e5"] + ALL, [P + "ang"])
        actf(fl(mag), fl(mag), AF.Exp, [P + "mag"], [P + "mag"])
        ts("dve", fl(pre), fl(ang), math.pi / 2, None, ALU.add, None, [P + "ang"] + ALL, [P + "pre"])
        range_reduce("dve", fl(ang), P + "ang", itmp, P + "itmp", ftmp, P + "ftmp")
        actf(fl(pim), fl(ang), AF.Sin, [P + "ang"] + ALL, [P + "pim"])
        range_reduce("dve", fl(pre), P + "pre", itmp, P + "itmp", ftmp, P + "ftmp")
        actf(fl(pre), fl(pre), AF.Sin, [P + "pre"], [P + "pre"])
        tt("dve", fl(pim), fl(pim), fl(mag), ALU.mult, [P + "pim", P + "mag"], [P + "pim"])
        tt("dve", fl(pre), fl(pre), fl(mag), ALU.mult, [P + "pre", P + "mag"], [P + "pre"])
        dbg(f"pow_re{l}", fl(pre), [128, NE], [P + "pre"])
        dbg(f"pow_im{l}", fl(pim), [128, NE], [P + "pim"])
        ckpt("sb1")
        mod_step(2)
        abr = pre[:, 4, :, 0]
        abi = pim[:, 4, :, 0]
        K = P + "cf"
        ts("dve", cf[:, 0], abr, -1.0, None, ALU.add, None, [P + "pre"] + ALL, [K + "0"])
        tt("dve", cf[:, 1], cf[:, 0], aRe, ALU.mult, [K + "0", P + "aRI"] + ALL, [K + "1"])
        tt("dve", cf[:, 2], abi, aIm, ALU.mult, [P + "pim", P + "aRI"] + ALL, [K + "2"])
        tt("dve", cf[:, 1], cf[:, 1], cf[:, 2], ALU.add, [K + "1", K + "2"], [K + "1"])
        tt("dve", cf[:, 2], abi, aRe, ALU.mult, [P + "pim", P + "aRI", K + "1"], [K + "2"])
        tt("dve", cf[:, 3], cf[:, 0], aIm, ALU.mult, [K + "0", P + "aRI"] + ALL, [K + "3"])
        tt("dve", cf[:, 2], cf[:, 2], cf[:, 3], ALU.subtract, [K + "2", K + "3"], [K + "2"])
        tt("dve", cf[:, 3], aRe, aRe, ALU.mult, [P + "aRI", K + "2"], [K + "3"])
        tt("dve", cf[:, 4], aIm, aIm, ALU.mult, [P + "aRI"] + ALL, [K + "4"])
        tt("dve", cf[:, 3], cf[:, 3], cf[:, 4], ALU.add, [K + "3", K + "4"], [K + "3"])
        E("dve", lambda h: h.reciprocal(cf[:, 3], cf[:, 3]), [K + "3"], [K + "3"])
        tt("dve", cf[:, 5], cf[:, 1], cf[:, 3], ALU.mult, [K + "1", K + "3"] + ALL, [K + "5"])
        tt("dve", cf[:, 6], cf[:, 2], cf[:, 3], ALU.mult, [K + "2", K + "3"] + ALL, [K + "6"])
        actf(r8, adt, AF.Exp, [P + "adt"] + ALL, [P + "r8"], scale=8.0)
        S.dma("sp", ssm_scr_r.ap()[l], r8, reads=[P + "r8"], writes=[("scr_r", l)])
        ckpt("sb2")
        S.dma("sp", Bsb[0:32, 0, :], W["ssm_b_re"].ap()[l], reads=ALL, writes=[P + "Bsb"])
        S.dma("sp", Bsb[0:32, 1, :], W["ssm_b_im"].ap()[l], reads=ALL, writes=[P + "Bsb"])
        for ri in range(2):
            b_ = mm_rot.next()
            trs([(psf(b_)[0:64, c * 32:(c + 1) * 32], Bsb[0:32, ri, :].rearrange("g (p c) -> g c p", c=16)[:, c, :],
                  identf[0:32, 0:32]) for c in range(16)], [P + "Bsb", "identf"], [pk(b_)])
            cp("dve", Bri[0:64, ri].rearrange("p g c -> p c g"), psf(b_)[0:64, :].rearrange("p (c g) -> p c g", c=16),
               [pk(b_)] + ALL, [P + "Bri"])
        cre = cf[0:64, 5, :].unsqueeze(2).to_broadcast([64, 32, 16])
        cim = cf[0:64, 6, :].unsqueeze(2).to_broadcast([64, 32, 16])
        tt("dve", t1[0:64], cre, Bri[0:64, 0], ALU.mult, [K + "5", P + "Bri"] + ALL, [P + "t1"])
        tt("dve", t2[0:64], cim, Bri[0:64, 1], ALU.mult, [K + "6", P + "Bri"] + ALL, [P + "t2"])
        tt("dve", Bb[0:64, 0], t1[0:64], t2[0:64], ALU.subtract, [P + "t1", P + "t2"] + ALL, [P + "Bb0"])
        tt("dve", t1[0:64], cre, Bri[0:64, 1], ALU.mult, [K + "5", P + "Bri", P + "Bb0"], [P + "t1"])
        tt("dve", t2[0:64], cim, Bri[0:64, 0], ALU.mult, [K + "6", P + "Bri", P + "Bb0"], [P + "t2"])
        tt("dve", Bb[0:64, 1], t1[0:64], t2[0:64], ALU.add, [P + "t1", P + "t2"] + ALL, [P + "Bb1"])
        ckpt("sb3")
        S.dma("sp", Csb[:, 0], W["ssm_c_re"].ap()[l].rearrange("(j r) p -> r j p", r=128), reads=ALL, writes=[P + "Csb"])
        S.dma("sp", Csb[:, 1], W["ssm_c_im"].ap()[l].rearrange("(j r) p -> r j p", r=128), reads=ALL, writes=[P + "Csb"])
        for ri in range(2):
            b_ = mm_rot.next()
            trs([(psf(b_)[0:64, j * 128:(j + 1) * 128], Csb[:, ri, j, :], identf[:, :]) for j in range(4)],
                [P + "Csb", "identf"], [pk(b_)])
            cp("dve", Cri[0:64, ri].rearrange("p g c -> p (g c)"), psf(b_)[0:64, :], [pk(b_)] + ALL, [P + "Cri"])
        ckpt("sb4")
        E("pool", lambda h: h.iota(mski, pattern=[[1, 128]], base=0, channel_multiplier=0), ALL, [P + "mski"])
        E("dve", lambda h: h.tensor_single_scalar(mski, mski, 4, ALU.arith_shift_right), [P + "mski"], [P + "mski"])
        cp("dve", mskf, mski, [P + "mski"] + ALL, [P + "mskf"])
        ts("dve", mask[:, 0, :], mskf, pidx_f[:, 3:4], None, ALU.is_ge, None, [P + "mskf", "pidx_f"] + ALL, [P + "mask"])
        ts("dve", mask[:, 1, :], mskf, pidx_f[:, 3:4], None, ALU.is_le, None, [P + "mskf", "pidx_f"], [P + "mask"])
        for s_ in range(8):
            S.add("sp", lambda h, s_=s_: _dma_nc(h, dcol[s_ * 16:(s_ + 1) * 16, :],
                                                W["ssm_d"].ap()[l].rearrange("(g c) -> c g", c=16)),
                  ALL, [P + "dcol"], dma=True)
        ckpt("sb5")
        ts("dve", phi, th, 8.0, None, ALU.mult, None, [P + "th"] + ALL, [P + "phi"])
        range_reduce("dve", phi, P + "phi", itmp[:, 0:32], P + "itmp", ftmp[:, 0:32], P + "ftmp")
        mod_step(2)
        ckpt("sb6")
        S.alias([P + "late"], early_keys)
        LATE = [P + "late"]
        off[0] = R1
        GB = 8
        Mst = cv([128, GB, 128], BF16); PSst = cv([128, GB, 128], BF16)
        PSWst = cv([128, GB, 128], BF16); Qst = cv([128, GB, 128], BF16)
        Xre = cv([128, GB, 8, 16]); nXim = cv([128, GB, 8, 16]); Zre = cv([128, GB, 8, 16]); Zim = cv([128, GB, 8, 16])
        Pre = cv([128, GB, 8, 16]); Pim = cv([128, GB, 8, 16]); Qre = cv([128, GB, 8, 16]); nQim = cv([128, GB, 8, 16])
        ta = cv([128, GB, 8, 16]); tb = cv([128, GB, 8, 16])
        mtmp = cv([128, 128])
        kidx_i = cv([128, 128], I32); kidx = cv([128, 128])
        tab = cv([128, 8, 128]); tabo = cv([128, 8, 128])
        it2 = cv([128, 1024], I32); ft2 = cv([128, 1024])

        def cmul(slot, mat, mkeys, blk, out_re, out_im, neg_im, kout):
            g0, g1 = blk * GB, blk * GB + GB
            shp = [64, GB, 8, 16]
            pr = pre[0:64, slot, g0:g1, :].unsqueeze(3).to_broadcast(shp)
            pi = pim[0:64, slot, g0:g1, :].unsqueeze(3).to_broadcast(shp)
            mr = mat[0:64, 0, g0:g1, :].unsqueeze(2).to_broadcast(shp)
            mi = mat[0:64, 1, g0:g1, :].unsqueeze(2).to_broadcast(shp)
            rk = [P + "pre", P + "pim"] + mkeys + LATE
            tt("dve", ta[0:64], pr, mr, ALU.mult, rk, [P + "ta"])
            tt("pool", tb[0:64], pi, mi, ALU.mult, rk, [P + "tb"])
            tt("dve", out_re[0:64], ta[0:64], tb[0:64], ALU.subtract, [P + "ta", P + "tb"] + LATE, [kout + "r"])
            tt("dve", ta[0:64], pr, mi, ALU.mult, rk, [P + "ta"])
            tt("pool", tb[0:64], pi, mr, ALU.mult, rk, [P + "tb"])
            if neg_im:
                stt(out_im[0:64], ta[0:64], -1.0, tb[0:64], ALU.mult, ALU.subtract, [P + "ta", P + "tb"] + LATE, [kout + "i"])
            else:
                tt("dve", out_im[0:64], ta[0:64], tb[0:64], ALU.add, [P + "ta", P + "tb"] + LATE, [kout + "i"])

        BK = [P + "Bb0", P + "Bb1"]
        CK = [P + "Cri"]
        for blk in range(32 // GB):
            d_ = (blk * GB) // 16
            cmul(0, Bb, BK, blk, Xre, nXim, True, P + "X")
            cmul(1, Cri, CK, blk, Zre, Zim, False, P + "Z")
            cmul(2, Bb, BK, blk, Pre, Pim, False, P + "P")
            cmul(3, Cri, CK, blk, Qre, nQim, True, P + "Q")
            ckpt("sb6a")
            for gi in range(GB):
                g = (blk * GB + gi) % 16
                f = lambda a, gi=gi: a[0:64, gi].rearrange("p s c -> p (s c)")
                b_ = mm_rot.next()
                mm(psf(b_)[:, 0:128], [(f(Xre), f(Zre)), (f(nXim), f(Zim))],
                   [P + "Xr", P + "Xi", P + "Zr", P + "Zi"], [pk(b_)])
                if d_ == 0:
                    tt("dve", mtmp, psf(b_)[:, 0:128], mask[:, 0, :], ALU.mult, [pk(b_), P + "mask"] + LATE, [P + "mtmp"])
                    stt(Mst[:, gi, :], identf[:, :], dcol[:, g:g + 1], mtmp, ALU.mult, ALU.add,
                        [P + "mtmp", P + "dcol", "identf"] + LATE, [P + "Mst"])
                else:
                    tt("dve", Mst[:, gi, :], psf(b_)[:, 0:128], mask[:, 1, :], ALU.mult, [pk(b_), P + "mask"] + LATE, [P + "Mst"])
                ckpt("sb6b")
                b_ = mm_rot.next()
                trs([(psf(b_)[:, 0:64], f(Pre), identf[0:64, 0:64]), (psf(b_)[:, 64:128], f(Pim), identf[0:64, 0:64])],
                    [P + "Pr", P + "Pi", "identf"], [pk(b_)])
                cp("act", PSst[:, gi, :], psf(b_)[:, 0:128], [pk(b_)] + LATE, [P + "PSst"])
                cp("dve", PSWst[:, gi, :].rearrange("p (r q) -> p r q", r=2),
                   psf(b_)[:, 0:128].rearrange("p (r q) -> p r q", r=2)[:, ::-1, :], [pk(b_)] + LATE, [P + "PSWst"])
                ckpt("sb6c")
                b_ = mm_rot.next()
                mm(psf(b_)[:, 0:128], [(epad[0:64, 64:192], f(Qre)), (epad[0:64, 0:128], f(nQim))],
                   [P + "Qr", P + "Qi", "epad"], [pk(b_)])
                cp("act", Qst[:, gi, :], psf(b_)[:, 0:128], [pk(b_)] + LATE, [P + "Qst"])
                ckpt("sb6d")
            for idx, st_ in enumerate((Mst, PSst, PSWst, Qst)):
                S.dma("sp", ssm_scr_b.ap()[l, idx, :, blk * GB * 128:(blk + 1) * GB * 128], st_.rearrange("p g m -> p (g m)"),
                      reads=[P + ["Mst", "PSst", "PSWst", "Qst"][idx]], writes=[("scr_b", l)])
            ckpt("sb6e%d" % blk)
            mod_step(1)
        ckpt("sb7")
        E("pool", lambda h: h.iota(kidx_i, pattern=[[1, 128]], base=0, channel_multiplier=0), LATE, [P + "kidx_i"])
        cp("dve", kidx, kidx_i, [P + "kidx_i"] + LATE, [P + "kidx"])
        for blk in range(4):
            for which in range(2):
                tk = P + "tab"
                tt("dve", tab, phi[:, blk * 8:(blk + 1) * 8].unsqueeze(2).to_broadcast([128, 8, 128]),
                   kidx.unsqueeze(1).to_broadcast([128, 8, 128]), ALU.mult, [P + "phi", P + "kidx"] + LATE, [tk])
                tv = tab.rearrange("p g k -> p (g k)")
                tvo = tabo.rearrange("p g k -> p (g k)")
                if which == 0:
                    ts("dve", tv, tv, math.pi / 2, None, ALU.add, None, [tk], [tk])
                range_reduce("dve", tv, tk, it2, P + "it2", ft2, P + "ft2")
                if which == 0:
                    actf(tvo, tv, AF.Sin, [tk] + LATE, [P + "tabo"])
                else:
                    actf(tv, tv, AF.Sin, [tk], [tk])
                    ts("dve", tvo, tv, sgnv, None, ALU.mult, None, [tk, "sgnv"] + LATE, [P + "tabo"])
                S.dma("sp", ssm_scr_f.ap()[l, which, :, blk * 1024:(blk + 1) * 1024], tvo,
                      reads=[P + "tabo"], writes=[("scr_f", l)])
                S.add("sp", lambda h, blk=blk, which=which: _dma_nc(h, ssm_scr_c1.ap()[l, which, :, blk * 8:(blk + 1) * 8], tabo[:, :, 1]),
                      [P + "tabo"], [("scr_c1", l)], dma=True, cost=0.15, lat=2.5)
            mod_step(1)

    for l in range(2):
        ssm_build(l)
    mod_step(100)
    setup_keys = list(S.last_w.keys())

    if stop_after == "setup":
        st = S.emit()
        return nc, dbg_outs, st

    A_WIN = 0
    o = 30208
    dqT = carve(o, [128, 4, 1024], BF16); o += 4 * 1024 * 2
    dkT = carve(o, [128, 4, 1536], BF16); o += 4 * 1536 * 2
    dV = carve(o, [128, 12, 4, 65], BF16); o += 12 * 4 * 65 * 2
    o = (o + 3) // 4 * 4
    gqT = carve(o, [128, 4, 1024], BF16); o += 4 * 1024 * 2
    gkT = carve(o, [128, 2, 1536], BF16); o += 2 * 1536 * 2
    gV = carve(o, [128, 12, 2, 65], BF16); o += 12 * 2 * 65 * 2
    o = (o + 3) // 4 * 4
    mqT = carve(o, [128, 4, 1024], BF16); o += 4 * 1024 * 2
    mkT = carve(o, [128, 4, 1536], BF16); o += 4 * 1536 * 2
    mV = carve(o, [128, 12, 4, 65], BF16); o += 12 * 4 * 65 * 2
    o = (o + 3) // 4 * 4
    A_ATT_END = o
    uT = carve(o, [128, 2, 1024], BF16)
    cache_sb = carve(o, [128, 2, 928], BF16)
    n_g = carve(o, [128, D], F32)
    o += 4096
    assert o <= ARENA_BYTES, o
    mixed = carve(0, [128, NT, 1024], BF16)
    w_in_sb = carve(A_WIN, [128, 8, IN_COLS], BF16)
    UK = "uslot"
    ATT_KEYS = ([("dqT", t) for t in range(NT)] + [("dkT", t) for t in range(12)] + [("dV", t) for t in range(12)] +
                [("gqT", t) for t in range(NT)] + [("gkT", t) for t in range(12)] + [("gV", t) for t in range(12)] +
                [("mqT", t) for t in range(NT)] + [("mkT", t) for t in range(12)] + [("mV", t) for t in range(12)])
    MIX_KEYS = [("mixed", t) for t in range(NT)] + [("mixed_s", t) for t in range(NT)]
    TOKF = [("tokf", k) for k in range(5)]
    HB = [("hb", 0), ("hbq", 0), ("hbq", 1), ("hbk", 0), ("hb", 4)]

    gq_g = sb("gq_g", [128, 64]); gk_g = sb("gk_g", [128, 64]); sub_g = sb("sub_g", [128, 64])
    mq_g = sb("mq_g", [128, 192]); mkv_g = sb("mkv_g", [128, 128])
    lamt = sb("lamt", [128, 4, 32]); lams = sb("lams", [128, 8])
    w_uq = sb("w_uq", [128, 2, 384], BF16)
    w_ukv = sb("w_ukv", [128, 512], BF16)
    w_glu = sb("w_glu", [128, 2, 512], BF16)
    pT = [sb(f"pT{i}", [128, 512], BF16) for i in range(4)]
    pT_rot = Rot([0, 1, 2, 3])
    tokb = sb("tokb", [128, 1024], BF16)
    h0t = sb("h0t", [128, 2, 32])

    def st_(c0, c1=None):
        return stat[:, c0:(c1 if c1 is not None else c0 + 1)]

    def load_layer_params(l, job):
        for (t, nm, n) in ((gq_g, "gqa_qn_g", 64), (gk_g, "gqa_kn_g", 64), (sub_g, "diff_subln_g", 64),
                           (mq_g, "mla_qn_g", 192), (mkv_g, "mla_kvn_g", 128)):
            S.dma("sp", t[:, 0:n], W[nm].ap()[l:l + 1, :].to_broadcast([128, n]), writes=[nm])
        ts("dve", sub_g[:], sub_g[:], 1.0 - LAM_INIT[l], None, ALU.mult, None, ["diff_subln_g"], ["diff_subln_g"])
        for i, nm in enumerate(("diff_lq1", "diff_lk1", "diff_lq2", "diff_lk2")):
            S.dma("sp", lamt[:, i, :], W[nm].ap()[l:l + 1, :].to_broadcast([128, 32]), writes=[("lamt", i)])
        tt("dve", lamt[:, 0, :], lamt[:, 0, :], lamt[:, 1, :], ALU.mult, [("lamt", 0), ("lamt", 1)], [("lamt", 0)])
        tt("dve", lamt[:, 2, :], lamt[:, 2, :], lamt[:, 3, :], ALU.mult, [("lamt", 2), ("lamt", 3)], [("lamt", 2)])
        E("dve", lambda h: h.tensor_reduce(out=lams[:, 0:1], in_=lamt[:, 0, :], axis=AX.X, op=ALU.add), [("lamt", 0)], ["lams"])
        E("dve", lambda h: h.tensor_reduce(out=lams[:, 1:2], in_=lamt[:, 2, :], axis=AX.X, op=ALU.add), [("lamt", 2)], ["lams"])
        actf(lams[:, 2:4], lams[:, 0:2], AF.Exp, ["lams"], ["lams"])
        tt("dve", lams[:, 4:5], lams[:, 2:3], lams[:, 3:4], ALU.subtract, ["lams"], ["lams"])
        ts("dve", lams[:, 5:6], lams[:, 4:5], -1.0, -LAM_INIT[l], ALU.mult, ALU.add, ["lams"], ["lams"])
        S.dma("pool", w_uq[:, 0, :], W["mla_w_uq"].ap()[l, 0:128, :], writes=["w_uq"])
        S.dma("pool", w_uq[0:64, 1, :], W["mla_w_uq"].ap()[l, 128:192, :], writes=["w_uq"])
        S.dma("pool", w_ukv[:], W["mla_w_ukv"].ap()[l], writes=["w_ukv"])
        S.dma("pool", w_glu[:], W["ssm_w_glu"].ap()[l].rearrange("(k p) n -> p k n", p=128), writes=["w_glu"])

    def AK(i):
        return [("actT", i, kc) for kc in range(8)]

    def load_mod(l, cond, which):
        mc = modcol[:, which]
        MK = ("modcol", which)
        S.dma("sp", modb[:, 0, :],
              modscr.ap()[l, cond:cond + 1, (which * 3 + 2) * D:(which * 3 + 3) * D].to_broadcast([128, D]),
              reads=["modscr"], writes=[("modb", 2)])
        nm = "norm1_g" if which == 0 else "norm2_g"
        srcs = [modscr.ap()[l, cond, (which * 3 + 0) * D:(which * 3 + 1) * D],
                modscr.ap()[l, cond, (which * 3 + 1) * D:(which * 3 + 2) * D],
                W[nm].ap()[l, :]]
        for i_, src in enumerate(srcs):
            S.add("sp", lambda h, i_=i_, src=src: _dma_nc(h, mc[:, i_, :], src.rearrange("(k p) -> p k", p=128)),
                  ["modscr"], [(MK, i_)], dma=True, cost=0.15, lat=4.0)
        stt(mc[:, 1, :], mc[:, 1, :], 1.0, mc[:, 2, :], ALU.add, ALU.mult, [(MK, 1), (MK, 2)], [(MK, 1)])

    def norm_mod_transpose(i, which):
        xk = ("x", i)
        mc = modcol[:, which]
        MK = ("modcol", which)
        par = i % 2
        hbx = [hb, hb2][par]
        HK = HB if par == 0 else ["hb2"]
        sc0, sc1 = 60 + 2 * par, 61 + 2 * par
        actf(hbx[:], x_sb[:, i, :], AF.Square, [xk], [("st", sc0)] + HK, accum_out=st_(sc0))
        rstd_chain(st_(sc0), st_(sc1), 1.0 / D, ("st", sc0), ("st", sc1))
        ts("dve", hbx[:], x_sb[:, i, :], st_(sc1), None, ALU.mult, None, [xk, ("st", sc1)], HK)
        b_ = tr_rot.next()
        trs([(psb(b_)[:, kc * 128:(kc + 1) * 128], hbx[:, kc * 128:(kc + 1) * 128], ident[:, :]) for kc in range(8)],
            HK + ["ident"], [pk(b_)])
        for kc in range(8):
            o_ap = actT[:, kc, i * 128:(i + 1) * 128]
            i_ap = psb(b_)[:, kc * 128:(kc + 1) * 128]
            if kc % 2 == 0:
                E("act", lambda h, o_ap=o_ap, i_ap=i_ap, kc=kc: h.activation(out=o_ap, in_=i_ap, func=AF.Identity,
                                                                             scale=mc[:, 1, kc:kc + 1], bias=mc[:, 0, kc:kc + 1]),
                  [pk(b_), (MK, 0), (MK, 1)], [("actT", i, kc)], cost=0.5)
            else:
                ts("dve", o_ap, i_ap, mc[:, 1, kc:kc + 1], mc[:, 0, kc:kc + 1], ALU.mult, ALU.add,
                   [pk(b_), (MK, 0), (MK, 1)], [("actT", i, kc)])

    def rope(src, src_keys, dst, dst_keys, nh, n, cos_t, sin_t, i):
        Wd = 4 * n
        t1 = tokf[:, 0:nh * Wd]
        t2 = tmpf[:, 0:nh * Wd]
        tt("dve", t1.rearrange("p (h w) -> p h w", h=nh), src.rearrange("p (h w) -> p h w", h=nh),
           cos_t[:, i, :].unsqueeze(1).to_broadcast([128, nh, Wd]), ALU.mult, src_keys + ["cos%d" % Wd], [("tokf", 0)])
        tt("dve", t2.rearrange("p (h w) -> p h w", h=nh), src.rearrange("p (h w) -> p h w", h=nh),
           sin_t[:, i, :].unsqueeze(1).to_broadcast([128, nh, Wd]), ALU.mult, src_keys + ["sin%d" % Wd], ["tmpf"])
        v = lambda a: a.rearrange("p (ha two n) -> p ha two n", two=2, n=n)
        tt("dve", v(dst), v(t1), v(t2)[:, :, ::-1, :], ALU.add, [("tokf", 0), "tmpf"], dst_keys)

    def head_rms(src, nh, hd, g_tile, gkey, dst, rkeys, wkeys, scol):
        actf(tmpf[:, 0:nh * hd], src, AF.Square, rkeys, ["tmpf"])
        E("dve", lambda h: h.tensor_reduce(out=st_(scol, scol + nh), in_=tmpf[:, 0:nh * hd].rearrange("p (h d) -> p h d", h=nh),
                                           axis=AX.X, op=ALU.add), ["tmpf"], [("st", scol)])
        rstd_chain(st_(scol, scol + nh), st_(scol + 8, scol + 8 + nh), 1.0 / hd, ("st", scol), ("st", scol + 8))
        tt("dve", dst.rearrange("p (h d) -> p h d", h=nh), src.rearrange("p (h d) -> p h d", h=nh),
           st_(scol + 8, scol + 8 + nh).unsqueeze(2).to_broadcast([128, nh, hd]), ALU.mult, rkeys + [("st", scol + 8)], wkeys)
        tt("dve", dst.rearrange("p (h d) -> p h d", h=nh), dst.rearrange("p (h d) -> p h d", h=nh),
           g_tile[:, 0:hd].unsqueeze(1).to_broadcast([128, nh, hd]), ALU.mult, wkeys + [gkey], wkeys)

    def kv_expand(kt, kr_src, kr_keys):
        kcol = slice(kt * 128, (kt + 1) * 128)
        bb = aux_rot.next()
        mm(psf(bb)[:, 0:512], [(tokb[:, 256:384], w_ukv[:, :])], [("tokb", 1), "w_ukv"], [pk(bb)])
        mktok = hb[:, 576:960].rearrange("p (h w) -> p h w", h=4)
        kvp = psf(bb)[:, 0:512].rearrange("p (h w) -> p h w", h=4)
        cp("act", mktok[:, :, 0:64], kvp[:, :, 0:64], [pk(bb)], [("hbk", 0)])
        cp("dve", mV[:, kt, :, 0:64], kvp[:, :, 64:128], [pk(bb)], [("mV", kt)])
        cp("dve", mktok[:, :, 64:96], kr_src.unsqueeze(1).to_broadcast([128, 4, 32]), kr_keys + [("hbk", 0)], [("hbk", 0)])
        bt = tr_rot.next()
        trs([(psb(bt)[0:96, h * 128:(h + 1) * 128], hb[:, 576 + h * 96:576 + (h + 1) * 96], ident[:, :]) for h in range(4)],
            [("hbk", 0), "ident"], [pk(bt)])
        cp("act", mkT[0:96, :, kcol], psb(bt)[0:96, 0:512].rearrange("p (h t) -> p h t", h=4), [pk(bt)], [("mkT", kt)])

    def in_proj_tile(job, l, i, kt):
        rp = job["rope"]
        OW = "own"
        ow = own[:, 0, :]
        bounds = [0, 512, 1024, 1536, IN_COLS]
        banks = []
        for c in range(4):
            b_ = mm_rot.next()
            banks.append(b_)
            n0, n1 = bounds[c], bounds[c + 1]
            mm(psf(b_)[:, 0:n1 - n0], [(actT[:, kc, i * 128:(i + 1) * 128], w_in_sb[:, kc, n0:n1]) for kc in range(8)],
               AK(i) + ["w_in"], [pk(b_)])
        b0, b1, b2, b3 = banks
        tcol = slice(i * 128, (i + 1) * 128)
        kcol = slice(kt * 128, (kt + 1) * 128)
        if rp:
            rope(psf(b0)[:, 0:256], [pk(b0)], tokb[:, 0:256], [("tokb", 0)], 8, 8, cos32, sin32, i)
            rope(psf(b0)[:, 256:512], [pk(b0)], tokb[:, 256:512], [("tokb", 1)], 8, 8, cos32, sin32, i)
        else:
            cp("act", tokb[:, 0:256], psf(b0)[:, 0:256], [pk(b0)], [("tokb", 0)])
            cp("dve", tokb[:, 256:512], psf(b0)[:, 256:512], [pk(b0)], [("tokb", 1)])
        if job["caches_out"]:
            cp("act", ow[:, 0:256], psf(b0)[:, 256:512], [pk(b0)], [OW])
        bt = tr_rot.next()
        trs([(psb(bt)[0:64, j * 128:(j + 1) * 128], tokb[:, j * 64:(j + 1) * 64], ident[:, :]) for j in range(8)],
            [("tokb", 0), ("tokb", 1), "ident"], [pk(bt)])
        cp("act", dqT[0:64, :, tcol], psb(bt)[0:64, 0:512].rearrange("p (h t) -> p h t", h=4), [pk(bt)], [("dqT", i)])
        cp("act", dkT[0:64, :, kcol], psb(bt)[0:64, 512:1024].rearrange("p (h t) -> p h t", h=4), [pk(bt)], [("dkT", kt)])
        if job["caches_out"]:
            cp("act", ow[:, 256:512], psf(b1)[:, 0:256], [pk(b1)], [OW])
        cp("act", dV[:, kt, :, 0:64], psf(b1)[:, 0:256].rearrange("p (h d) -> p h d", h=4), [pk(b1)], [("dV", kt)])
        head_rms(psf(b1)[:, 256:512], 4, 64, gq_g, "gqa_qn_g", tokf[:, 256:512], [pk(b1)], [("tokf", 1)], 8)
        if rp:
            rope(tokf[:, 256:512], [("tokf", 1)], tokb[:, 512:768], [("tokb", 2)], 4, 16, cos64, sin64, i)
        else:
            cp("dve", tokb[:, 512:768], tokf[:, 256:512], [("tokf", 1)], [("tokb", 2)])
        head_rms(psf(b2)[:, 0:128], 2, 64, gk_g, "gqa_kn_g", ow[:, 512:640], [pk(b2)], [OW], 24)
        if rp:
            rope(ow[:, 512:640], [OW], tokb[:, 768:896], [("tokb", 3)], 2, 16, cos64, sin64, i)
        else:
            cp("dve", tokb[:, 768:896], ow[:, 512:640], [OW], [("tokb", 3)])
        if job["caches_out"]:
            cp("act", ow[:, 640:768], psf(b2)[:, 128:256], [pk(b2)], [OW])
        cp("act", gV[:, kt, :, 0:64], psf(b2)[:, 128:256].rearrange("p (h d) -> p h d", h=2), [pk(b2)], [("gV", kt)])
        bt = tr_rot.next()
        trs([(psb(bt)[0:64, j * 128:(j + 1) * 128], tokb[:, 512 + j * 64:512 + (j + 1) * 64], ident[:, :]) for j in range(6)],
            [("tokb", 2), ("tokb", 3), "ident"], [pk(bt)])
        cp("act", gqT[0:64, :, tcol], psb(bt)[0:64, 0:512].rearrange("p (h t) -> p h t", h=4), [pk(bt)], [("gqT", i)])
        cp("act", gkT[0:64, :, kcol], psb(bt)[0:64, 512:768].rearrange("p (h t) -> p h t", h=2), [pk(bt)], [("gkT", kt)])
        cp("act", hb[:, 0:256], psf(b2)[:, 256:512], [pk(b2)], [("hb", 0)])
        bt = tr_rot.next()
        trs([(psb(bt)[:, j * 128:(j + 1) * 128], hb[:, j * 128:(j + 1) * 128], ident[:, :]) for j in range(2)],
            [("hb", 0), "ident"], [pk(bt)])
        cp("act", uT[:, :, tcol], psb(bt)[:, 0:256].rearrange("p (k t) -> p k t", k=2), [pk(bt)], [UK])
        actf(junk[:, 0:192], psf(b3)[:, 0:192], AF.Square, [pk(b3)], [("st", 40)], accum_out=st_(40))
        rstd_chain(st_(40), st_(41), 1.0 / 192, ("st", 40), ("st", 41))
        stt(hb[:, 256:448], psf(b3)[:, 0:192], st_(41), mq_g[:, 0:192], ALU.mult, ALU.mult,
            [pk(b3), ("st", 41), "mla_qn_g"], [("hbq", 0)])
        actf(junk[:, 256:384], psf(b3)[:, 192:320], AF.Square, [pk(b3)], [("st", 42)], accum_out=st_(42))
        rstd_chain(st_(42), st_(43), 1.0 / 128, ("st", 42), ("st", 43))
        stt(ow[:, 768:896], psf(b3)[:, 192:320], st_(43), mkv_g[:, 0:128], ALU.mult, ALU.mult,
            [pk(b3), ("st", 43), "mla_kvn_g"], [OW])
        cp("dve", hb[:, 448:576], ow[:, 768:896], [OW], [("hbq", 1)])
        cp("act", ow[:, 896:928], psf(b3)[:, 320:352], [pk(b3)], [OW])
        bt = tr_rot.next()
        trs([(psb(bt)[:, 0:128], hb[:, 256:384], ident[:, :]), (psb(bt)[0:64, 128:256], hb[:, 384:448], ident[:, :]),
             (psb(bt)[:, 256:384], hb[:, 448:576], ident[:, :])], [("hbq", 0), ("hbq", 1), "ident"], [pk(bt)])
        cp("act", tokb[:, 0:128], psb(bt)[:, 0:128], [pk(bt)], [("tokb", 0)])
        cp("act", tokb[0:64, 128:256], psb(bt)[0:64, 128:256], [pk(bt)], [("tokb", 0)])
        cp("dve", tokb[:, 256:384], psb(bt)[:, 256:384], [pk(bt)], [("tokb", 1)])
        ba = aux_rot.next()
        mm(psf(ba)[:, 0:384], [(tokb[:, 0:128], w_uq[:, 0, :]), (tokb[0:64, 128:256], w_uq[0:64, 1, :])],
           [("tokb", 0), "w_uq"], [pk(ba)])
        mqtok = tokb[:, 512:896].rearrange("p (h w) -> p h w", h=4)
        mqp = psf(ba)[:, 0:384].rearrange("p (h w) -> p h w", h=4)
        cp("act", mqtok[:, :, 0:64], mqp[:, :, 0:64], [pk(ba)], [("tokb", 2), ("tokb", 3)])
        if rp:
            cp("dve", tokf[:, 768:896].rearrange("p (h w) -> p h w", h=4), mqp[:, :, 64:96], [pk(ba)], [("tokf", 3)])
            rope(tokf[:, 768:896], [("tokf", 3)], tokf[:, 896:1024], [("tokf", 4)], 4, 8, cos32, sin32, i)
            cp("dve", mqtok[:, :, 64:96], tokf[:, 896:1024].rearrange("p (h w) -> p h w", h=4), [("tokf", 4)],
               [("tokb", 2), ("tokb", 3)])
        else:
            cp("dve", mqtok[:, :, 64:96], mqp[:, :, 64:96], [pk(ba)], [("tokb", 2), ("tokb", 3)])
        bt = tr_rot.next()
        trs([(psb(bt)[0:96, h * 128:(h + 1) * 128], tokb[:, 512 + h * 96:512 + (h + 1) * 96], ident[:, :]) for h in range(4)],
            [("tokb", 2), ("tokb", 3), "ident"], [pk(bt)])
        cp("act", mqT[0:96, :, tcol], psb(bt)[0:96, 0:512].rearrange("p (h t) -> p h t", h=4), [pk(bt)], [("mqT", i)])
        if rp:
            rope(ow[:, 896:928], [OW], tokf[:, 896:928], [("tokf", 4)], 1, 8, cos32, sin32, i)
            kv_expand(kt, tokf[:, 896:928], [("tokf", 4)])
        else:
            kv_expand(kt, ow[:, 896:928], [OW])
        if job["caches_out"]:
            sq = i // 2
            t0 = (i % 2) * 128
            for (dst, c0, c1) in ((ndk_d, 0, 256), (ndv_d, 256, 512), (ngk_d, 512, 640), (ngv_d, 640, 768),
                                  (nckv_d, 768, 896), (nkr_d, 896, 928)):
                S.dma("sp", dst.ap()[sq, l, t0:t0 + 128, :], ow[:, c0:c1], reads=[OW])

    def past_tiles(job, l):
        for half in range(2):
            for (src, c0, c1) in ((cdk_d, 0, 256), (cdv_d, 256, 512), (cgk_d, 512, 640), (cgv_d, 640, 768),
                                  (cckv_d, 768, 896), (ckr_d, 896, 928)):
                S.dma("pool", cache_sb[:, :, c0:c1], src.ap()[l, half * 256:(half + 1) * 256, :].rearrange("(j p) n -> p j n", p=128),
                      writes=[UK])
            for kk in range(2):
                kt = half * 2 + kk
                kcol = slice(kt * 128, (kt + 1) * 128)
                cs_ = cache_sb[:, kk, :]
                bt = tr_rot.next()
                trs([(psb(bt)[0:64, j * 128:(j + 1) * 128], cs_[:, j * 64:(j + 1) * 64], ident[:, :]) for j in range(4)] +
                    [(psb(bt)[0:64, (4 + j) * 128:(5 + j) * 128], cs_[:, 512 + j * 64:512 + (j + 1) * 64], ident[:, :]) for j in range(2)] +
                    [(psb(bt)[:, 768:896], cs_[:, 768:896], ident[:, :])], [UK, "ident"], [pk(bt)])
                cp("act", dkT[0:64, :, kcol], psb(bt)[0:64, 0:512].rearrange("p (h t) -> p h t", h=4), [pk(bt)], [("dkT", kt)])
                cp("dve", gkT[0:64, :, kcol], psb(bt)[0:64, 512:768].rearrange("p (h t) -> p h t", h=2), [pk(bt)], [("gkT", kt)])
                cp("act", tokb[:, 256:384], psb(bt)[:, 768:896], [pk(bt)], [("tokb", 1)])
                cp("dve", dV[:, kt, :, 0:64], cs_[:, 256:512].rearrange("p (h d) -> p h d", h=4), [UK], [("dV", kt)])
                cp("dve", gV[:, kt, :, 0:64], cs_[:, 640:768].rearrange("p (h d) -> p h d", h=2), [UK], [("gV", kt)])
                kv_expand(kt, cs_[:, 896:928], [UK])

    def attention(job, l):
        nseq, Ts, past = job["nseq"], job["Ts"], job["past"]
        nk = (past + Ts) // 128
        qb = min(512, Ts)
        nqb = Ts // qb
        nsub = qb // 128
        o1 = tokf[:, 0:256].rearrange("p (s d) -> p s d", s=4)
        o2 = tokf[:, 256:512].rearrange("p (s d) -> p s d", s=4)
        att_s_rot = Rot([0, 1])
        att_o_rot = Rot([2, 3])
        for sq in range(nseq):
            tile0 = sq * (Ts // 128)
            kt0 = 0 if past else tile0
            items = []
            for h_ in range(4):
                for qi_ in range(nqb):
                    for j_ in range(2):
                        items.append((2 * h_ + j_, qi_))
            for hs_ in range(8, 16):
                for qi_ in range(nqb):
                    items.append((hs_, qi_))
            for (hs, qi) in items:
                if hs < 8:
                    kind, h, j = "d", hs // 2, hs % 2
                    QT, KT, V, vh, r0, r1, scale = dqT, dkT, dV, h, 32 * j, 32 * j + 32, 32 ** -0.5
                    qh, kh = h, h
                elif hs < 12:
                    kind, h, j = "g", hs - 8, 0
                    QT, KT, V, vh, r0, r1, scale = gqT, gkT, gV, h // 2, 0, 64, 64 ** -0.5
                    qh, kh = h, h // 2
                else:
                    kind, h, j = "m", hs - 12, 0
                    QT, KT, V, vh, r0, r1, scale = mqT, mkT, mV, h, 0, 96, 96 ** -0.5
                    qh, kh = h, h
                qkey = {"d": "dqT", "g": "gqT", "m": "mqT"}[kind]
                kkey = {"d": "dkT", "g": "gkT", "m": "mkT"}[kind]
                vkey = {"d": "dV", "g": "gV", "m": "mV"}[kind]
                if True:
                    q0 = tile0 * 128 + qi * qb
                    bo = att_o_rot.next()
                    ops_ = psf(bo)[:, 0:nsub * 65].rearrange("p (s d) -> p s d", s=nsub)
                    qtiles = [tile0 + qi * nsub + s_ for s_ in range(nsub)]
                    for kk in range(nk):
                        kt = kt0 + kk
                        bs = att_s_rot.next()
                        mm(psf(bs)[:, 0:qb], [(KT[r0:r1, kh, kt * 128:(kt + 1) * 128], QT[r0:r1, qh, q0:q0 + qb])],
                           [(kkey, kt)] + [(qkey, t_) for t_ in qtiles], [pk(bs)])
                        pi = pT_rot.next()
                        actf(pT[pi][:, 0:qb], psf(bs)[:, 0:qb], AF.Exp, [pk(bs)], [("pT", pi)], scale=scale)

                        def pv(hh, pi=pi, kt=kt, kk=kk, ops_=ops_, V=V, vh=vh):
                            ins = None
                            for s_ in range(nsub):
                                ins = hh.matmul(ops_[:, s_, :], lhsT=pT[pi][:, s_ * 128:(s_ + 1) * 128], rhs=V[:, kt, vh, :],
                                                start=(kk == 0 and s_ == 0), stop=(kk == nk - 1), skip_group_check=True)
                            return ins
                        E("pe", pv, [("pT", pi), (vkey, kt)], [pk(bo)], cost=0.1 + nsub * 0.09)
                    E("dve", lambda h_, ops_=ops_: h_.reciprocal(st_(48, 48 + nsub), ops_[:, :, 64]), [pk(bo)], [("st", 48)])
                    rec = st_(48, 48 + nsub).unsqueeze(2).to_broadcast([128, nsub, 64])
                    mk_ = [("mixed", t_) for t_ in qtiles]
                    if kind == "d":
                        dst = o1 if j == 0 else o2
                        tt("dve", dst[:, 0:nsub, :], ops_[:, :, 0:64], rec, ALU.mult, [pk(bo), ("st", 48)], [("tokf", j)])
                        if j == 1:
                            dif = tokf[:, 512:768].rearrange("p (s d) -> p s d", s=4)[:, 0:nsub, :]
                            stt(dif, o2[:, 0:nsub, :], lams[:, 5:6], o1[:, 0:nsub, :], ALU.mult, ALU.add,
                                [("tokf", 0), ("tokf", 1), "lams"], [("tokf", 2)])
                            sqv = tmpf[:, 0:nsub * 64].rearrange("p (s d) -> p s d", s=nsub)
                            tt("pool", sqv, dif, dif, ALU.mult, [("tokf", 2)], ["tmpf"])
                            E("dve", lambda h_, sqv=sqv: h_.tensor_reduce(out=st_(52, 52 + nsub), in_=sqv, axis=AX.X, op=ALU.add),
                              ["tmpf"], [("st", 52)])
                            rstd_chain(st_(52, 52 + nsub), st_(56, 56 + nsub), 1.0 / 64, ("st", 52), ("st", 56))
                            tt("dve", dif, dif, st_(56, 56 + nsub).unsqueeze(2).to_broadcast([128, nsub, 64]), ALU.mult,
                               [("tokf", 2), ("st", 56)], [("tokf", 2)])
                            mdst = mixed[:, qtiles[0]:qtiles[0] + nsub, h * 64:(h + 1) * 64]
                            tt("dve", mdst, dif, sub_g[:, 0:64].unsqueeze(1).to_broadcast([128, nsub, 64]), ALU.mult,
                               [("tokf", 2), "diff_subln_g"], mk_)
                    else:
                        c0 = (256 if kind == "g" else 768) + h * 64
                        mdst = mixed[:, qtiles[0]:qtiles[0] + nsub, c0:c0 + 64]
                        tt("dve", mdst, ops_[:, :, 0:64], rec, ALU.mult, [pk(bo), ("st", 48)], mk_)

    def ssm_run(job, l):
        nm = job["name"]
        nseq, Ts = job["nseq"], job["Ts"]
        nch = Ts // 8
        P = f"sr{l}{nm}_"
        RK = P + "region"
        o_ = [16384]
        o2_ = [A_ATT_END + 4096]

        def cv(shape, dt=F32, tail=False):
            n = 1
            for s_ in shape[1:]:
                n *= s_
            esz = 2 if dt == BF16 else 4
            sz = (n * esz + 3) // 4 * 4
            if tail:
                oo = o2_[0]
                o2_[0] += sz
                assert o2_[0] <= ARENA_BYTES, o2_[0]
            else:
                oo = o_[0]
                o_[0] += sz
                assert o_[0] <= 30208, o_[0]
            return carve(oo, shape, dt)

        prm = []
        for i_ in range(2):
            prm.append(dict(M=cv([128, 2, 128], BF16), PS=cv([128, 2, 128], BF16), PSW=cv([128, 2, 128], BF16),
                            Q=cv([128, 2, 128], BF16), CC=cv([128, 2, 128]), SS=cv([128, 2, 128])))
        Ug = [cv([128, 128], BF16) for _ in range(2)]
        Wt = [cv([128, 128]) for _ in range(2)]
        T1 = [cv([128, 128]) for _ in range(2)]
        Gt = [cv([128, 128]) for _ in range(2)]
        Hb = [cv([128, nseq, nch + 1], BF16) for _ in range(4)]
        Yg = [cv([128, 128], BF16) for _ in range(2)]
        Hfin = cv([128, nseq, 32], tail=True)
        g0 = cv([128, 32], tail=True)
        g0t = cv([128, 32], tail=True)
        r8 = cv([128, 32], tail=True)
        c1 = cv([128, 2, 32], tail=True)
        h0b = cv([128, 32], BF16, tail=True)
        ygT = [tokb[:, :], hb[:, :]]
        YGK = [[("tokb", k) for k in range(4)], HB]
        sgl = tmpf[:, 0:256]
        S.dma("sp", r8, ssm_scr_r.ap()[l], reads=[("scr_r", l), RK], writes=[P + "r8"])

        def load_prm(g):
            gs = g % 2
            pr = prm[gs]
            for idx, kk_ in enumerate(("M", "PS", "PSW", "Q")):
                S.dma("sp", pr[kk_], ssm_scr_b.ap()[l, idx].rearrange("p (d g m) -> p d g m", d=2, g=16)[:, :, g, :],
                      reads=[("scr_b", l), RK], writes=[(P + "prm" + kk_, gs)])
            for idx, kk_ in enumerate(("CC", "SS")):
                S.dma("sp", pr[kk_], ssm_scr_f.ap()[l, idx].rearrange("p (d g m) -> p d g m", d=2, g=16)[:, :, g, :],
                      reads=[("scr_f", l), RK], writes=[(P + "prm" + kk_, gs)])

        if job["past"]:
            ld = tokf[0:32, 0:256].rearrange("p (r q) -> p r q", r=2)
            LDK = [("tokf", 0)]
            S.dma("sp", ld[:, 0, 0:64], h0re_d.ap()[l], writes=LDK)
            S.dma("sp", ld[:, 0, 64:128], h0im_d.ap()[l], writes=LDK)
            S.dma("sp", ld[:, 1, 0:64], h0im_d.ap()[l], writes=LDK)
            S.dma("sp", ld[:, 1, 64:128], h0re_d.ap()[l], writes=LDK)
            S.dma("sp", c1, ssm_scr_c1.ap()[l].rearrange("w p g -> p w g"), reads=[("scr_c1", l), RK], writes=[P + "c1"])
            bT_ = 4
            trs([(psf(bT_)[:, r * 32:(r + 1) * 32], ld[:, r, :], identf[0:32, 0:32]) for r in range(2)],
                LDK + ["identf", RK], [pk(bT_)])
            cp("dve", h0t[:].rearrange("p r g -> p (r g)"), psf(bT_)[:, 0:64], [pk(bT_)], ["h0t"])
            cp("dve", h0b, h0t[:, 0, :], ["h0t", RK], [P + "h0b"])
            tt("dve", g0, c1[:, 0, :], h0t[:, 0, :], ALU.mult, [P + "c1", "h0t", RK], [P + "g0"])
            tt("dve", g0t, c1[:, 1, :], h0t[:, 1, :], ALU.mult, [P + "c1", "h0t", RK], [P + "g0t"])
            tt("dve", g0, g0, g0t, ALU.add, [P + "g0", P + "g0t"], [P + "g0"])
        NC_ = nseq * nch
        ybanks = {0: (6, 7), 1: (6, 7)}
        s_rots = {0: Rot([4]), 1: Rot([4])}
        y_rot = Rot([5])
        v3 = lambda a: a[:, 0:NC_].rearrange("p (q k) -> p q k", q=nseq)
        load_prm(0)
        for g in range(16):
            ch, gl = g // 8, g % 8
            ui = g % 2
            gs = g % 2
            pr = prm[gs]
            PKf = lambda nm_, gs=gs: (P + "prm" + nm_, gs)
            s_rot = s_rots[ch]
            if g + 1 < 16:
                load_prm(g + 1)
            bu = s_rot.next()
            mm(psf(bu)[:, 0:NC_], [(rpad[:, gl, 112 - 16 * s_:240 - 16 * s_],
                                    uT[:, ch, :].rearrange("p (k s) -> p s k", s=8)[:, s_, :]) for s_ in range(8)],
               [UK, "rpad"], [pk(bu)])
            cp("act", Ug[ui], psf(bu)[:, 0:NC_], [pk(bu), RK], [(P + "Ug", ui)])
            by = y_rot.next()
            for d_ in range(2):
                dg = d_ * 16 + g
                wi = d_
                bs_ = s_rot.next()

                def ssw(hh, bs_=bs_, d_=d_, ui=ui, pr=pr):
                    hh.matmul(psf(bs_)[:, 0:128], lhsT=pr["PS"][:, d_, :], rhs=Ug[ui], start=True, stop=True)
                    return hh.matmul(psf(bs_)[:, 128:256], lhsT=pr["PSW"][:, d_, :], rhs=Ug[ui], start=True, stop=True)
                E("pe", ssw, [PKf("PS"), PKf("PSW"), (P + "Ug", ui)], [pk(bs_)], cost=0.35)

                def tabv(tab, d_=d_):
                    t_ = tab[:, d_, 0:nch]
                    if d_ == 1:
                        t_ = t_[:, ::-1]
                    return t_.unsqueeze(1).to_broadcast([128, nseq, nch])

                tt("dve", v3(Wt[wi]), v3(psf(bs_)), tabv(pr["CC"]), ALU.mult, [pk(bs_), PKf("CC"), RK], [(P + "Wt", wi)])
                tt("dve", v3(T1[wi]), psf(bs_)[:, 128:256].rearrange("p (q k) -> p q k", q=nseq), tabv(pr["SS"]), ALU.mult,
                   [pk(bs_), PKf("SS"), RK], [(P + "T1", wi)])
                tt("pool", Wt[wi][:, 0:NC_], Wt[wi][:, 0:NC_], T1[wi][:, 0:NC_], ALU.subtract,
                   [(P + "Wt", wi), (P + "T1", wi)], [(P + "Wt", wi)])
                for sq in range(nseq):
                    wv = Wt[wi][:, sq * nch:(sq + 1) * nch]
                    gv = Gt[wi][:, sq * nch:(sq + 1) * nch]
                    if d_ == 1:
                        wv = wv[:, ::-1]
                        gv = gv[:, ::-1]
                    init = g0[:, dg:dg + 1] if job["past"] else 0.0
                    rk = [(P + "Wt", wi), P + "r8", RK] + ([P + "g0"] if job["past"] else [])
                    E("dve", lambda h_, wv=wv, gv=gv, init=init, dg=dg:
                      h_.tensor_tensor_scan(out=gv, data0=r8[:, dg:dg + 1].to_broadcast([128, nch]), data1=wv,
                                            initial=init, op0=ALU.mult, op1=ALU.add), rk, [(P + "Gt", wi)],
                      cost=0.15 + 2 * nch / 960.0)
                bg = s_rot.next()
                mm(psf(bg)[:, 0:NC_], [(j2[:, :], Gt[wi][:, 0:NC_])], [(P + "Gt", wi), "j2"], [pk(bg)])
                tt("dve", v3(Wt[wi]), v3(Gt[wi]), tabv(pr["CC"]), ALU.mult, [(P + "Gt", wi), PKf("CC")], [(P + "Wt", wi)])
                tt("dve", v3(T1[wi]), v3(psf(bg)), tabv(pr["SS"]), ALU.mult, [pk(bg), PKf("SS")], [(P + "T1", wi)])
                hbi = d_ * 2 + (g % 2)
                HBt = Hb[hbi]
                if d_ == 0:
                    hdst, hinit, hrhs = HBt[:, :, 1:nch + 1], HBt[:, :, 0], HBt[:, :, 0:nch]
                else:
                    hdst, hinit, hrhs = HBt[:, :, 0:nch], HBt[:, :, nch], HBt[:, :, 1:nch + 1]
                tt("pool", hdst, v3(Wt[wi]), v3(T1[wi]), ALU.add, [(P + "Wt", wi), (P + "T1", wi), RK], [(P + "Hb", hbi)])
                if job["past"]:
                    cp("dve", hinit, h0b[:, dg:dg + 1].to_broadcast([128, nseq]), [P + "h0b", RK], [(P + "Hb", hbi)])
                else:
                    memset("dve", hinit, 0.0, [(P + "Hb", hbi)])
                if job["caches_out"]:
                    col = nch - 1 if d_ == 0 else 0
                    fv = v3(Wt[wi])[:, :, col]
                    tv = v3(T1[wi])[:, :, col]
                    tt("dve", Hfin[:, :, dg], fv, tv, ALU.add, [(P + "Wt", wi), (P + "T1", wi), RK], [P + "Hfin"])
                mm1(psf(by)[:, 0:NC_], pr["M"][:, d_, :], Ug[ui], d_ == 0, False, [PKf("M"), (P + "Ug", ui)], [pk(by)])
                mm1(v3(psf(by)), pr["Q"][:, d_, :], hrhs, False, d_ == 1, [PKf("Q"), (P + "Hb", hbi)], [pk(by)])
            actf(Yg[ui], psf(by)[:, 0:NC_], AF.Gelu_apprx_tanh, [pk(by), RK], [(P + "Yg", ui)])
            yb = ybanks[ch]
            for tau in range(8):
                pb_ = yb[tau // 4]
                outp = psf(pb_)[:, (tau % 4) * 128:(tau % 4 + 1) * 128]
                mm1(outp, rpad[:, tau, 112 - 16 * gl:240 - 16 * gl], Yg[ui], gl == 0 and tau % 4 == 0, gl == 7,
                    [(P + "Yg", ui), "rpad"], [pk(pb_)], skip=True)
            if gl == 7:
                for half in range(2):
                    pb_ = yb[half]
                    dstv = ygT[ch].rearrange("p (k s) -> p s k", s=8)[:, half * 4:(half + 1) * 4, :]
                    cp("act", dstv, psf(pb_)[:, :].rearrange("p (s k) -> p s k", s=4), [pk(pb_)], YGK[ch])
        for i in range(NT):
            bz = [4, 5][i % 2]
            mm(psf(bz)[:, 0:512], [(ygT[kc][:, i * 128:(i + 1) * 128], w_glu[:, kc, :]) for kc in range(2)],
               YGK[0] + YGK[1] + ["w_glu"], [pk(bz)])
            actf(sgl, psf(bz)[:, 256:512], AF.Sigmoid, [pk(bz)], ["tmpf"])
            tt("dve", mixed[:, i, 512:768], psf(bz)[:, 0:256], sgl, ALU.mult, [pk(bz), "tmpf", RK], [("mixed_s", i)])
        if job["caches_out"]:
            bT_ = 4
            trs([(psf(bT_)[:, 0:128], Hfin.rearrange("p q g -> p (q g)"), identf[:, :])], [P + "Hfin", "identf"], [pk(bT_)])
            cp("dve", tokf[:, 0:128], psf(bT_)[:, 0:128], [pk(bT_)], [("tokf", 0)])
            for sq in range(nseq):
                S.dma("sp", nsre_d.ap()[sq, l], tokf[sq * 32:(sq + 1) * 32, 0:64], reads=[("tokf", 0)])
                S.dma("sp", nsim_d.ap()[sq, l], tokf[sq * 32:(sq + 1) * 32, 64:128], reads=[("tokf", 0)])
        return [RK] + [(P + "prm" + n_, i_) for n_ in ("M", "PS", "PSW", "Q", "CC", "SS") for i_ in range(2)] + [P + "r8", P + "c1", P + "Hfin", P + "h0b", P + "g0", P + "g0t"] + \
               [(P + k, i_) for k in ("Ug", "Wt", "T1", "Gt", "Yg") for i_ in range(2)] + [(P + "Hb", i_) for i_ in range(4)]

    def out_proj(job, l, ssm_keys):
        nm = job["name"]
        P = f"op{l}{nm}_"
        w_out_sb = carve(30208, [128, 8, 1024], BF16)
        S.alias(["w_out"], ATT_KEYS)
        S.dma("pool", w_out_sb, W["w_out"].ap()[l].rearrange("(k p) n -> p k n", p=128), writes=["w_out"])
        for i in range(NT):
            b_ = tr_rot.next()
            trs([(psb(b_)[:, kc * 128:(kc + 1) * 128], mixed[:, i, kc * 128:(kc + 1) * 128], ident[:, :]) for kc in range(8)],
                [("mixed", i), ("mixed_s", i), "ident"], [pk(b_)])
            cp("act", actT[:, :, i * 128:(i + 1) * 128], psb(b_)[:, :].rearrange("p (k t) -> p k t", k=8), [pk(b_)],
               AK(i))
        for i in range(NT):
            for nh in range(2):
                b_ = mm_rot.next()
                mm(psf(b_)[:, :], [(actT[:, kc, i * 128:(i + 1) * 128], w_out_sb[:, kc, nh * 512:(nh + 1) * 512]) for kc in range(8)],
                   AK(i) + ["w_out"], [pk(b_)])
                tb_, tk_ = [(tmpf, ["tmpf"]), (tokf, [("tokf", 0), ("tokf", 1)])][nh]
                tt("dve", tb_[:, 0:512], psf(b_)[:, :], modb[:, 0, nh * 512:(nh + 1) * 512], ALU.mult, [pk(b_), ("modb", 2)], tk_)
                tt("dve", x_sb[:, i, nh * 512:(nh + 1) * 512], x_sb[:, i, nh * 512:(nh + 1) * 512], tb_[:, 0:512], ALU.add,
                   tk_ + [("x", i)], [("x", i)])

    def mlp(job, l):
        nm = job["name"]
        P = f"ml{l}{nm}_"
        aT = carve(0, [128, 32, 1024], BF16)
        w1b = [carve(65536 + i * 4096, [128, 8, 256], BF16) for i in range(2)]
        w2b = [carve(65536 + 8192 + i * 16384, [128, 32, 256], BF16) for i in range(2)]
        RK = P + "region"
        S.alias([RK], ["w_out", "w_in", UK] + MIX_KEYS + ATT_KEYS + list(job.get("ssm_keys", [])))
        for jb in range(16):
            bi = jb % 2
            S.dma("pool", w1b[bi], W["mlp_w1"].ap()[l, :, jb * 256:(jb + 1) * 256].rearrange("(k p) n -> p k n", p=128),
                  reads=[RK], writes=[(P + "w1", bi)])
            for hc in range(2):
                j = jb * 2 + hc
                for tb in range(2):
                    b_ = mm_rot.next()
                    mm(psf(b_)[:, :], [(w1b[bi][:, kc, hc * 128:(hc + 1) * 128], actT[:, kc, tb * 512:(tb + 1) * 512]) for kc in range(8)],
                       [(P + "w1", bi)] + [k_ for t in range(tb * 4, tb * 4 + 4) for k_ in AK(t)], [pk(b_)])
                    pi = pT_rot.next()
                    actf(pT[pi][:, :], psf(b_)[:, :], AF.Relu, [pk(b_)], [("pT", pi)])
                    tt("dve", aT[:, j, tb * 512:(tb + 1) * 512], pT[pi][:, :], pT[pi][:, :], ALU.mult, [("pT", pi), RK], [(P + "aT", tb)])
        for q in range(4):
            bi = q % 2
            S.dma("pool", w2b[bi], W["mlp_w2"].ap()[l, :, q * 256:(q + 1) * 256].rearrange("(j p) n -> p j n", p=128),
                  reads=[RK], writes=[(P + "w2", bi)])
            for i in range(NT):
                b_ = mm_rot.next()
                mm(psf(b_)[:, 0:256], [(aT[:, j, i * 128:(i + 1) * 128], w2b[bi][:, j, :]) for j in range(32)],
                   [(P + "aT", i // 4), (P + "w2", bi)], [pk(b_)])
                tb_, tk_ = [(tmpf, ["tmpf"]), (tokf, [("tokf", 0)])][i % 2]
                tt("dve", tb_[:, 0:256], psf(b_)[:, 0:256], modb[:, 0, q * 256:(q + 1) * 256], ALU.mult, [pk(b_), ("modb", 2)], tk_)
                tt("dve", x_sb[:, i, q * 256:(q + 1) * 256], x_sb[:, i, q * 256:(q + 1) * 256], tb_[:, 0:256], ALU.add,
                   tk_ + [("x", i)], [("x", i)])
        return [RK, (P + "aT", 0), (P + "aT", 1), (P + "w1", 0), (P + "w1", 1), (P + "w2", 0), (P + "w2", 1)]

    def run_job(job, prev_keys):
        S.alias(["w_in", UK] + ATT_KEYS + MIX_KEYS, prev_keys)
        for i in range(NT):
            S.dma("sp", x_sb[:, i, :], job["x_d"].ap()[i * 128:(i + 1) * 128, :], writes=[("x", i)])
        mlp_keys = None
        for l in range(2):
            if mlp_keys is not None:
                S.alias(["w_in", UK] + ATT_KEYS + MIX_KEYS, mlp_keys)
            S.dma("pool", w_in_sb, W["w_in"].ap()[l].rearrange("(k p) n -> p k n", p=128), writes=["w_in"])
            load_layer_params(l, job)
            load_mod(l, job["cond"], 0)
            for (t_, key) in ((dV, "dV"), (gV, "gV"), (mV, "mV")):
                for kt in range(12):
                    memset("pool", t_[:, kt, :, 64:65], 1.0, [(key, kt)])
            for i in range(NT):
                norm_mod_transpose(i, 0)
            if job["past"]:
                past_tiles(job, l)
            for i in range(NT):
                kt = (job["past"] // 128 + i) if job["past"] else i
                in_proj_tile(job, l, i, kt)
            if stop_after == "inproj":
                return
            S.alias(MIX_KEYS, ["w_in"])
            S.alias([f"sr{l}{job['name']}_region"], ["w_in", UK] + ATT_KEYS)
            attention(job, l)
            if stop_after == "attn":
                return
            ssm_keys = ssm_run(job, l)
            dbg(f"mixed{job['name']}{l}", mixed.rearrange("p t c -> p (t c)"), [128, NT * 1024], MIX_KEYS, q="pool")
            if stop_after == "ssm":
                return
            job["ssm_keys"] = ssm_keys
            out_proj(job, l, ssm_keys)
            load_mod(l, job["cond"], 1)
            for i in range(NT):
                norm_mod_transpose(i, 1)
            mlp_keys = mlp(job, l)
        S.alias([UK], mlp_keys)
        S.dma("sp", n_g, W["final_norm_g"].ap().unsqueeze(0).to_broadcast([128, D]), writes=[UK])
        for i in range(NT):
            xk = ("x", i)
            actf(junk[:], x_sb[:, i, :], AF.Square, [xk], [("st", 0)], accum_out=st_(0))
            rstd_chain(st_(0), st_(1), 1.0 / D, ("st", 0), ("st", 1))
            stt(tmpf[:], x_sb[:, i, :], st_(1), n_g, ALU.mult, ALU.mult, [xk, ("st", 1), UK], ["tmpf"])
            S.dma("sp", job["y_d"].ap()[i * 128:(i + 1) * 128, :], tmpf[:], reads=["tmpf"])
        return mlp_keys + [UK]

    jobP = dict(name="P", nseq=4, Ts=256, past=0, rope=False, cond=0, x_d=xp_d, y_d=yp_d, caches_out=True)
    jobS = dict(name="S", nseq=1, Ts=1024, past=512, rope=True, cond=1, x_d=xs_d, y_d=ys_d, caches_out=False)
    arena_setup_keys = [k for k in setup_keys if (isinstance(k, str) and k.startswith("sb")) or (isinstance(k, tuple) and k[0] == "wada")]
    kP = run_job(jobP, arena_setup_keys)
    if stop_after is None or stop_after == "all":
        run_job(jobS, kP)
    elif stop_after == "sample_inproj":
        pass
    st = S.emit()
    return nc, dbg_outs, st


_CACHE = {}


def _get_program():
    if "nc" not in _CACHE:
        nc, _, st = build_program()
        _CACHE["nc"] = nc
    return _CACHE["nc"]


def make_in_maps(inputs):
    f = lambda a: np.ascontiguousarray(np.asarray(a, dtype=np.float32))
    x_prompt = f(inputs["x_prompt"])
    x_sample = f(inputs["x_sample"])
    wnames = ["norm1_g", "norm2_g", "w_ada", "b_ada", "w_in", "w_out", "diff_lq1", "diff_lk1", "diff_lq2", "diff_lk2",
              "diff_subln_g", "gqa_qn_g", "gqa_kn_g", "ssm_log_dt", "ssm_d", "ssm_w_glu", "mla_qn_g", "mla_kvn_g",
              "mla_w_uq", "mla_w_ukv", "mlp_w1", "mlp_w2", "final_norm_g"]
    shared = {n: f(inputs[n]) for n in wnames}
    shared["ssm_a_re"] = f(inputs["ssm_a_re"]).reshape(2, 32, 64)
    shared["ssm_a_im"] = f(inputs["ssm_a_im"]).reshape(2, 32, 64)
    shared["ssm_log_dt"] = f(inputs["ssm_log_dt"]).reshape(2, 32)
    shared["ssm_b_re"] = f(inputs["ssm_b_re"]).reshape(2, 32, 1024)
    shared["ssm_b_im"] = f(inputs["ssm_b_im"]).reshape(2, 32, 1024)
    shared["ssm_c_re"] = f(inputs["ssm_c_re"]).reshape(2, 512, 64)
    shared["ssm_c_im"] = f(inputs["ssm_c_im"]).reshape(2, 512, 64)
    c = f(inputs["c"])
    c_ctx = f(inputs["c_ctx"])
    maps = []
    for core in range(8):
        b = core // 2
        m = dict(shared)
        m["xp"] = x_prompt[4 * core:4 * core + 4].reshape(1024, D)
        m["xs"] = x_sample[b]
        m["cvec"] = np.stack([c_ctx, c[b]], axis=0)
        m["cdk"] = f(inputs["cache_diff_k"])[b].reshape(2, 512, 256)
        m["cdv"] = f(inputs["cache_diff_v"])[b].reshape(2, 512, 256)
        m["cgk"] = f(inputs["cache_gqa_k"])[b].reshape(2, 512, 128)
        m["cgv"] = f(inputs["cache_gqa_v"])[b].reshape(2, 512, 128)
        m["cckv"] = f(inputs["cache_mla_ckv"])[b].reshape(2, 512, 128)
        m["ckr"] = f(inputs["cache_mla_krope"])[b].reshape(2, 512, 32)
        m["h0re"] = f(inputs["state_ssm_re"])[b].reshape(2, 32, 64)
        m["h0im"] = f(inputs["state_ssm_im"])[b].reshape(2, 32, 64)
        maps.append(m)
    return maps


def assemble(results):
    y_prompt = np.concatenate([r["yp"].reshape(4, 256, D) for r in results], axis=0)
    y_sample = np.stack([np.concatenate([results[2 * b]["ys"][0:512], results[2 * b + 1]["ys"][512:1024]], axis=0)
                         for b in range(4)], axis=0)
    cat = lambda k, shp: np.concatenate([r[k].reshape(shp) for r in results], axis=0)
    ndk = cat("ndk", (4, 2, 256, 4, 64))
    ndv = cat("ndv", (4, 2, 256, 4, 64))
    ngk = cat("ngk", (4, 2, 256, 2, 64))
    ngv = cat("ngv", (4, 2, 256, 2, 64))
    nckv = cat("nckv", (4, 2, 256, 128))
    nkr = cat("nkr", (4, 2, 256, 32))
    nsre = cat("nsre", (4, 2, 2, 16, 64))
    nsim = cat("nsim", (4, 2, 2, 16, 64))
    outs = (y_prompt, y_sample, ndk, ndv, ngk, ngv, nckv, nkr, nsre, nsim)
    return tuple(np.ascontiguousarray(o, dtype=np.float32) for o in outs)


def kernel(**inputs):
    nc = _get_program()
    in_maps = make_in_maps(inputs)
    res = run_bass_kernel_spmd(nc, in_maps, core_ids=list(range(8)))
    return assemble(res.results)
```

```python
import math
import numpy as np
import concourse.bass as bass
import concourse.mybir as mybir
from concourse.bass_utils import run_bass_kernel_spmd

F32 = mybir.dt.float32
BF16 = mybir.dt.bfloat16
I32 = mybir.dt.int32
AF = mybir.ActivationFunctionType
ALU = mybir.AluOpType
AX = mybir.AxisListType

ENGS = ("pe", "act", "dve", "pool", "sp")
D = 1024
NT = 8
EPS = 1e-6
TWO_PI = 2.0 * math.pi
IN_COLS = 1888
LAM_INIT = [0.8 - 0.6 * math.exp(-0.3 * l) for l in range(2)]


class Sched:
    def __init__(self, nc, n_dma_sems=24):
        self.nc = nc
        self.eng = dict(pe=nc.tensor, act=nc.scalar, dve=nc.vector, pool=nc.gpsimd, sp=nc.sync)
        self.ops = []
        self.last_w = {}
        self.readers = {}
        self.extra = {}
        self.window = 64
        self.reorder = True
        self.n_dma_sems = n_dma_sems

    def add(self, eng, fn, reads=(), writes=(), dma=False, cost=None, lat=0.0):
        idx = len(self.ops)
        writes = list(writes) + [k for k in reads if isinstance(k, tuple) and k and k[0] == "ps" and k not in writes]
        deps = set()
        for k in list(reads) + list(writes):
            deps |= self.extra.get(k, set())
        for k in reads:
            w = self.last_w.get(k)
            if w is not None:
                deps.add(w)
        for k in writes:
            w = self.last_w.get(k)
            if w is not None:
                deps.add(w)
            for r in self.readers.get(k, ()):
                deps.add(r)
        for k in reads:
            self.readers.setdefault(k, []).append(idx)
        for k in writes:
            self.last_w[k] = idx
            self.readers[k] = []
        deps.discard(idx)
        if cost is None:
            cost = {'pe': 0.3, 'act': 0.4, 'dve': 0.4, 'pool': 0.6, 'sp': 0.1}[eng]
        self.ops.append(dict(eng=eng, fn=fn, deps=deps, dma=dma, sig=False, cost=cost, lat=lat))
        return idx

    def alias(self, new_keys, old_keys):
        acc = set()
        for k in old_keys:
            w = self.last_w.get(k)
            if w is not None:
                acc.add(w)
            acc.update(self.readers.get(k, ()))
        for k in new_keys:
            self.extra[k] = self.extra.get(k, set()) | acc

    def dma(self, q, out, in_, reads=(), writes=(), **kw):
        n = 1
        for s_ in out.shape:
            n *= s_
        nbytes = n * 4
        cost = 0.15 if q == "sp" else 0.8
        return self.add(q, lambda h: h.dma_start(out=out, in_=in_, **kw), reads, writes, dma=True, cost=cost,
                        lat=2.0 + nbytes / 150e3)

    def schedule(self):
        import heapq
        ops = self.ops
        n = len(ops)
        succ = [[] for _ in range(n)]
        indeg = [0] * n
        for i, op in enumerate(ops):
            indeg[i] = len(op["deps"])
            for j in op["deps"]:
                succ[j].append(i)
        ready_t = [0.0] * n
        fin = [0.0] * n
        bl = [0.0] * n
        for i in range(n - 1, -1, -1):
            m = 0.0
            for k in succ[i]:
                if bl[k] > m:
                    m = bl[k]
            bl[i] = ops[i]["cost"] + ops[i]["lat"] + 0.15 + m
        heaps = {e: [] for e in ENGS}
        free = {e: 0.0 for e in ENGS}
        for i in range(n):
            if indeg[i] == 0:
                heapq.heappush(heaps[ops[i]["eng"]], (0.0, i))
        order = []
        WINDOW = self.window
        while len(order) < n:
            best = None
            for e in ENGS:
                h = heaps[e]
                if not h:
                    continue
                rt, i = h[0]
                start = max(rt, free[e])
                cand = (start, i, e)
                if best is None or cand < best:
                    best = cand
            start, i, e = best
            h = heaps[e]
            pool_ = []
            while h and h[0][0] <= start and len(pool_) < WINDOW:
                pool_.append(heapq.heappop(h))
            pick = max(pool_, key=lambda t: (bl[t[1]], -t[1]))
            for t in pool_:
                if t is not pick:
                    heapq.heappush(h, t)
            i = pick[1]
            op = ops[i]
            st_ = max(pick[0], free[e])
            op["t0"] = st_
            free[e] = st_ + op["cost"]
            fin[i] = st_ + op["cost"] + op["lat"]
            order.append(i)
            for k in succ[i]:
                same = (ops[k]["eng"] == e)
                t_ = fin[i] + (0.05 if same and e == "pe" else 0.15)
                if t_ > ready_t[k]:
                    ready_t[k] = t_
                indeg[k] -= 1
                if indeg[k] == 0:
                    heapq.heappush(heaps[ops[k]["eng"]], (ready_t[k], k))
        self.est_time = max(fin) if fin else 0.0
        return order

    def emit(self):
        nc = self.nc
        ops = self.ops
        for i, op in enumerate(ops):
            for j in op["deps"]:
                pj = ops[j]
                if pj["dma"]:
                    continue
                if pj["eng"] == "pe" and op["eng"] == "pe" and not op["dma"]:
                    continue
                pj["sig"] = True
        esem = {e: nc.alloc_semaphore(f"es_{e}") for e in ENGS}
        nd = self.n_dma_sems
        dsem = {q: [nc.alloc_semaphore(f"ds_{q}_{i}") for i in range(nd)] for q in ("sp", "pool", "act")}
        ecount = {e: 0 for e in ENGS}
        dcount = {q: [0] * nd for q in dsem}
        known = {e: {} for e in ENGS}
        dma_i = {q: 0 for q in dsem}
        nwaits = 0
        final = {}
        order = self.schedule() if self.reorder else list(range(len(ops)))
        for i in order:
            op = ops[i]
            e = op["eng"]
            h = self.eng[e]
            waits = {}
            for j in op["deps"]:
                pj = ops[j]
                if pj["dma"]:
                    key, sem, val = ("d",) + pj["dslot"], dsem[pj["dslot"][0]][pj["dslot"][1]], pj["dval"]
                elif pj["eng"] == "pe" and e == "pe" and not op["dma"]:
                    continue
                else:
                    key, sem, val = ("e", pj["eng"]), esem[pj["eng"]], pj["sigval"]
                if key not in waits or waits[key][1] < val:
                    waits[key] = (sem, val)
            if op["dma"]:
                s = dma_i[e] % nd
                dma_i[e] += 1
                if dcount[e][s] > 0:
                    key = ("d", e, s)
                    if key not in waits or waits[key][1] < dcount[e][s]:
                        waits[key] = (dsem[e][s], dcount[e][s])
            for key, (sem, val) in waits.items():
                if known[e].get(key, 0) >= val:
                    continue
                h.wait_ge(sem, val)
                known[e][key] = val
                nwaits += 1
            ins = op["fn"](h)
            if op["dma"]:
                dcount[e][s] += 16
                ins.then_inc(dsem[e][s], 16)
                op["dslot"] = (e, s)
                op["dval"] = dcount[e][s]
                final[("d", e, s)] = (dsem[e][s], dcount[e][s])
            elif op["sig"]:
                ecount[e] += 1
                ins.then_inc(esem[e], 1)
                op["sigval"] = ecount[e]
        h = self.eng["sp"]
        for key, (sem, val) in final.items():
            if known["sp"].get(key, 0) >= val:
                continue
            h.wait_ge(sem, val)
        return dict(n_ops=len(ops), n_waits=nwaits, counts=dict(ecount))


class Rot:
    def __init__(self, items):
        self.items = list(items)
        self.i = 0

    def next(self):
        v = self.items[self.i % len(self.items)]
        self.i += 1
        return v


class _Stop(Exception):
    pass


def build_program(debug=None, stop_after=None):
    try:
        return _build_program(debug, stop_after)
    except _Stop as e:
        return e.args[0]


def _build_program(debug=None, stop_after=None):
    nc = bass.Bass("TRN2", target_bir_lowering=False)
    S = Sched(nc)
    dbg_outs = {}

    def din(name, shape):
        return nc.dram_tensor(name, list(shape), F32, kind="ExternalInput")

    def dout(name, shape):
        return nc.dram_tensor(name, list(shape), F32, kind="ExternalOutput")

    xp_d = din("xp", [1024, D])
    xs_d = din("xs", [1024, D])
    cvec_d = din("cvec", [2, D])
    cdk_d = din("cdk", [2, 512, 256])
    cdv_d = din("cdv", [2, 512, 256])
    cgk_d = din("cgk", [2, 512, 128])
    cgv_d = din("cgv", [2, 512, 128])
    cckv_d = din("cckv", [2, 512, 128])
    ckr_d = din("ckr", [2, 512, 32])
    h0re_d = din("h0re", [2, 32, 64])
    h0im_d = din("h0im", [2, 32, 64])
    W = {}
    for name, shape in [
        ("norm1_g", [2, D]), ("norm2_g", [2, D]), ("w_ada", [2, D, 6 * D]), ("b_ada", [2, 6 * D]),
        ("w_in", [2, D, IN_COLS]), ("w_out", [2, D, D]),
        ("diff_lq1", [2, 32]), ("diff_lk1", [2, 32]), ("diff_lq2", [2, 32]), ("diff_lk2", [2, 32]),
        ("diff_subln_g", [2, 64]), ("gqa_qn_g", [2, 64]), ("gqa_kn_g", [2, 64]),
        ("ssm_a_re", [2, 32, 64]), ("ssm_a_im", [2, 32, 64]), ("ssm_log_dt", [2, 32]),
        ("ssm_b_re", [2, 32, 1024]), ("ssm_b_im", [2, 32, 1024]),
        ("ssm_c_re", [2, 512, 64]), ("ssm_c_im", [2, 512, 64]),
        ("ssm_d", [2, 256]), ("ssm_w_glu", [2, 256, 512]),
        ("mla_qn_g", [2, 192]), ("mla_kvn_g", [2, 128]),
        ("mla_w_uq", [2, 192, 384]), ("mla_w_ukv", [2, 128, 512]),
        ("mlp_w1", [2, D, 4 * D]), ("mlp_w2", [2, 4 * D, D]), ("final_norm_g", [D]),
    ]:
        W[name] = din(name, shape)
    yp_d = dout("yp", [1024, D])
    ys_d = dout("ys", [1024, D])
    ndk_d = dout("ndk", [4, 2, 256, 256])
    ndv_d = dout("ndv", [4, 2, 256, 256])
    ngk_d = dout("ngk", [4, 2, 256, 128])
    ngv_d = dout("ngv", [4, 2, 256, 128])
    nckv_d = dout("nckv", [4, 2, 256, 128])
    nkr_d = dout("nkr", [4, 2, 256, 32])
    nsre_d = dout("nsre", [4, 2, 32, 64])
    nsim_d = dout("nsim", [4, 2, 32, 64])
    modscr = nc.dram_tensor("modscr", [2, 2, 6 * D], F32)
    ssm_scr_b = nc.dram_tensor("ssm_scr_b", [2, 4, 128, 32 * 128], BF16)
    ssm_scr_f = nc.dram_tensor("ssm_scr_f", [2, 2, 128, 32 * 128], F32)
    ssm_scr_r = nc.dram_tensor("ssm_scr_r", [2, 128, 32], F32)
    ssm_scr_c1 = nc.dram_tensor("ssm_scr_c1", [2, 2, 128, 32], F32)

    def sb(name, shape, dt=F32):
        return nc.alloc_sbuf_tensor(name, list(shape), dt)

    PS = [nc.alloc_psum_tensor(f"ps{b}", [128, 512], F32) for b in range(8)]
    mm_rot = Rot([0, 1, 2, 3])
    tr_rot = Rot([4, 5])
    aux_rot = Rot([6, 7])

    def psf(b):
        return PS[b][:, :]

    def psb(b):
        return PS[b][:, :].bitcast(BF16)

    def pk(b):
        return ("ps", b)

    x_sb = sb("x_sb", [128, NT, D])
    modb = sb("modb", [128, 1, D])
    modcol = sb("modcol", [128, 2, 3, 8])
    hb2 = sb("hb2", [128, D], BF16)
    actT = sb("actT", [128, 8, 1024], BF16)
    tmpf = sb("tmpf", [128, D])
    hb = sb("hb", [128, D], BF16)
    junk = sb("junk", [128, D], BF16)
    stat = sb("stat", [128, 64])
    own = sb("own", [128, 1, 928])
    ident = sb("ident", [128, 128], BF16)
    identf = sb("identf", [128, 128])
    ARENA_BYTES = 104 * 1024
    arena = sb("arena", [128, ARENA_BYTES // 2], BF16)

    def carve(off, shape, dt):
        n = 1
        for s_ in shape[1:]:
            n *= s_
        esz = 2 if dt == BF16 else 4
        assert off % 4 == 0
        assert off + n * esz <= ARENA_BYTES, (off, n * esz)
        ap = arena[:, off // 2: off // 2 + n * esz // 2]
        if dt != BF16:
            ap = ap.bitcast(dt)
        if len(shape) == 2:
            return ap
        names = " ".join(f"d{i}" for i in range(len(shape) - 1))
        kw = {f"d{i}": shape[i + 1] for i in range(len(shape) - 1)}
        return ap.rearrange(f"p ({names}) -> p {names}", **kw)

    def fsz(ap):
        n = 1
        for s_ in ap.shape[1:]:
            n *= s_
        return n

    def ecost(eng, n):
        if eng == "dve":
            return 0.12 + n / 960.0
        if eng == "act":
            return 0.2 + n / 1200.0
        if eng == "pool":
            return 0.3 + n / 450.0
        return 0.3

    def E(eng, fn, reads=(), writes=(), cost=None):
        return S.add(eng, fn, reads, writes, cost=cost)

    def tt(eng, out, in0, in1, op, reads, writes):
        return E(eng, lambda h: h.tensor_tensor(out=out, in0=in0, in1=in1, op=op), reads, writes, cost=ecost(eng, fsz(out)))

    def ts(eng, out, in0, s1, s2, op0, op1, reads, writes):
        c = ecost(eng, fsz(out))
        if op1 is None:
            return E(eng, lambda h: h.tensor_scalar(out, in0, s1, None, op0=op0), reads, writes, cost=c)
        return E(eng, lambda h: h.tensor_scalar(out, in0, s1, s2, op0=op0, op1=op1), reads, writes, cost=c)

    def stt(out, in0, scalar, in1, op0, op1, reads, writes, eng="dve"):
        return E(eng, lambda h: h.scalar_tensor_tensor(out=out, in0=in0, scalar=scalar, in1=in1, op0=op0, op1=op1),
                 reads, writes, cost=ecost(eng, fsz(out)))

    def actf(out, in_, func, reads, writes, scale=None, bias=None, accum_out=None):
        kw = {}
        if scale is not None:
            kw["scale"] = scale
        if bias is not None:
            kw["bias"] = bias
        if accum_out is not None:
            kw["accum_out"] = accum_out
        return E("act", lambda h: h.activation(out=out, in_=in_, func=func, **kw), reads, writes,
                 cost=ecost("act", fsz(out)) + (0.1 if accum_out is not None else 0.0))

    def cp(eng, out, in_, reads, writes):
        c = ecost(eng, fsz(out))
        if eng == "act":
            return E("act", lambda h: h.copy(out, in_), reads, writes, cost=c)
        return E(eng, lambda h: h.tensor_copy(out, in_), reads, writes, cost=c)

    def mmcost(l, r):
        nn = fsz(r)
        f32 = (r.dtype == F32)
        return 0.035 + (nn * (4.0 if f32 else 1.0)) / 2400.0

    def mm(out, pairs, reads, writes):
        def fn(h):
            ins = None
            n = len(pairs)
            for i, (l, r) in enumerate(pairs):
                ins = h.matmul(out, lhsT=l, rhs=r, start=(i == 0), stop=(i == n - 1))
            return ins
        return E("pe", fn, reads, writes, cost=sum(mmcost(l, r) for (l, r) in pairs) + 0.1)

    def mm1(out, l, r, start, stop, reads, writes, skip=False):
        return E("pe", lambda h: h.matmul(out, lhsT=l, rhs=r, start=start, stop=stop, skip_group_check=skip), reads, writes,
                 cost=mmcost(l, r) + 0.1)

    def trs(items, reads, writes):
        def fn(h):
            ins = None
            for (o, i_, idn) in items:
                ins = h.transpose(o, i_, idn)
            return ins
        return E("pe", fn, reads, writes, cost=0.1 + sum(0.06 + (fsz(i_) if False else 128) * (2.0 if i_.dtype == F32 else 1.0) / 2400.0 for (o, i_, idn) in items))

    def memset(eng, ap, val, writes):
        return E(eng, lambda h: h.memset(ap, val), (), writes, cost=ecost(eng, fsz(ap)))

    def dbg(name, ap, shape, reads, q="sp"):
        if debug is None or name not in debug:
            return
        t = dout("dbg_" + name, shape)
        S.dma(q, t.ap(), ap, reads=reads)
        dbg_outs[name] = t

    _uid = [0]

    def uid(p):
        _uid[0] += 1
        return f"{p}{_uid[0]}"

    def rstd_chain(src, dst, mul, rk, wk):
        ts("dve", dst, src, mul, EPS, ALU.mult, ALU.add, [rk], [wk])
        actf(dst, dst, AF.Sqrt, [wk], [wk])
        E("dve", lambda h: h.reciprocal(dst, dst), [wk], [wk])

    MAGIC = 12582912.0

    def range_reduce(eng, ap, key, itmp, ikey, ftmp, fkey):
        actf(ftmp, ap, AF.Identity, [key], [fkey], scale=1.0 / TWO_PI, bias=MAGIC)
        actf(ftmp, ftmp, AF.Identity, [fkey], [fkey], bias=-MAGIC)
        stt(ap, ftmp, -TWO_PI, ap, ALU.mult, ALU.add, [fkey, key], [key], eng=eng)
        ts(eng, ap, ap, 3.1415925, -3.1415925, ALU.min, ALU.max, [key], [key])

    memset("pool", identf[:], 0.0, ["identf"])
    E("pool", lambda h: h.affine_select(out=identf[:], in_=identf[:], compare_op=ALU.not_equal, fill=1.0,
                                        base=0, pattern=[[-1, 128]], channel_multiplier=1), ["identf"], ["identf"])
    cp("dve", ident[:], identf[:], ["identf"], ["ident"])
    rpad = sb("rpad", [128, 8, 240], BF16)
    rpadf = sb("rpadf", [128, 16])
    memset("pool", rpad[:], 0.0, ["rpad"])
    for b in range(8):
        memset("pool", rpadf[:], 0.0, ["rpadf"])
        E("pool", lambda h, b=b: h.affine_select(out=rpadf[:], in_=rpadf[:], compare_op=ALU.not_equal, fill=1.0,
                                                 base=-16 * b, pattern=[[-1, 16]], channel_multiplier=1),
          ["rpadf"], ["rpadf"])
        cp("pool", rpad[:, b, 112:128], rpadf[:], ["rpadf"], ["rpad"])
    j2 = sb("j2", [128, 128])
    memset("pool", j2[:], 0.0, ["j2"])
    E("pool", lambda h: h.affine_select(out=j2[:, 64:128], in_=j2[:, 64:128], compare_op=ALU.not_equal, fill=1.0,
                                        base=0, pattern=[[-1, 64]], channel_multiplier=1), ["j2"], ["j2"])
    E("pool", lambda h: h.affine_select(out=j2[:, 0:64], in_=j2[:, 0:64], compare_op=ALU.not_equal, fill=1.0,
                                        base=-64, pattern=[[-1, 64]], channel_multiplier=1), ["j2"], ["j2"])
    epad = sb("epad", [128, 192])
    memset("pool", epad[:], 0.0, ["epad"])
    E("pool", lambda h: h.affine_select(out=epad[:, 64:128], in_=epad[:, 64:128], compare_op=ALU.not_equal, fill=1.0,
                                        base=0, pattern=[[-1, 64]], channel_multiplier=1), ["epad"], ["epad"])
    pidx_i = sb("pidx_i", [128, 4], I32)
    pidx_f = sb("pidx_f", [128, 8])
    E("pool", lambda h: h.iota(pidx_i[:, 0:1], pattern=[[0, 1]], base=0, channel_multiplier=1), (), ["pidx_i"])
    E("dve", lambda h: h.tensor_single_scalar(pidx_i[:, 1:2], pidx_i[:, 0:1], 63, ALU.bitwise_and), ["pidx_i"], ["pidx_i1"])
    E("dve", lambda h: h.tensor_single_scalar(pidx_i[:, 2:3], pidx_i[:, 0:1], 6, ALU.arith_shift_right), ["pidx_i"], ["pidx_i2"])
    E("dve", lambda h: h.tensor_single_scalar(pidx_i[:, 3:4], pidx_i[:, 0:1], 4, ALU.arith_shift_right), ["pidx_i"], ["pidx_i3"])
    cp("dve", pidx_f[:, 0:4], pidx_i[:, 0:4], ["pidx_i", "pidx_i1", "pidx_i2", "pidx_i3"], ["pidx_f"])
    ts("dve", pidx_f[:, 4:5], pidx_f[:, 2:3], 2.0, -1.0, ALU.mult, ALU.add, ["pidx_f"], ["sgnv"])
    sgnv = pidx_f[:, 4:5]

    if stop_after == "const":
        S.dma("sp", ndk_d.ap()[0, 0, 0:128, 0:128], identf[:, :], reads=["identf"])
        S.dma("sp", ndk_d.ap()[0, 0, 128:256, 0:128], j2[:, :], reads=["j2"])
        S.dma("sp", ndv_d.ap()[0, 0, 0:128, 0:8], pidx_f[:, :], reads=["pidx_f", "sgnv"])
        st = S.emit()
        return nc, dbg_outs, st
    csT = sb("csT", [128, 8, 2], BF16)
    tokf = sb("tokf", [128, 1024])
    WADA_OFF = ARENA_BYTES - 16384

    def mod_phase():
        cs = x_sb[0:2, 0, :]
        S.dma("sp", cs, cvec_d.ap(), writes=[("x", 0)])
        actf(cs, cs, AF.Silu, [("x", 0)], [("x", 0)])
        bT = tr_rot.next()
        trs([(psf(bT)[:, 2 * kc:2 * kc + 2], cs[:, kc * 128:(kc + 1) * 128], identf[0:2, 0:2]) for kc in range(8)],
            [("x", 0), "identf"], [pk(bT)])
        cp("dve", csT[:], psf(bT)[:, 0:16].rearrange("p (k t) -> p k t", t=2), [pk(bT)], ["csT"])
        wada = [carve(WADA_OFF + i * 8192, [128, 8, 512], BF16) for i in range(2)]
        bada = tokf[0:2, :].rearrange("p (b n) -> p b n", b=2)
        modrow = x_sb[0:2, 4:6, 0:512]
        BK_ = [[("tokf", 0), ("tokf", 1)], [("tokf", 2), ("tokf", 3), ("tokf", 4)]]
        ci = 0
        for l in range(2):
            for nchunk in range(12):
                bi = ci % 2
                ci += 1
                wk = ("wada", bi)
                S.dma("pool", wada[bi], W["w_ada"].ap()[l, :, nchunk * 512:(nchunk + 1) * 512].rearrange("(k p) n -> p k n", p=128),
                      writes=[wk])
                S.dma("sp", bada[:, bi, :], W["b_ada"].ap()[l:l + 1, nchunk * 512:(nchunk + 1) * 512].to_broadcast([2, 512]),
                      writes=BK_[bi])
                b = mm_rot.next()
                mm(psf(b)[0:2, :], [(csT[:, kc, :], wada[bi][:, kc, :]) for kc in range(8)], ["csT", wk], [pk(b)])
                tt("dve", modrow[:, bi, :], psf(b)[0:2, :], bada[:, bi, :], ALU.add,
                   [pk(b)] + BK_[bi], [("x", 4 + bi)])
                S.dma("sp", modscr.ap()[l, :, nchunk * 512:(nchunk + 1) * 512], modrow[:, bi, :], reads=[("x", 4 + bi)],
                      writes=["modscr"])
                yield

    modgen = mod_phase()

    def mod_step(n=1):
        for _ in range(n):
            next(modgen, None)

    cos32 = sb("cos32", [128, NT, 32])
    sin32 = sb("sin32", [128, NT, 32])
    cos64 = sb("cos64", [128, NT, 64])
    sin64 = sb("sin64", [128, NT, 64])
    rt_i = x_sb[:, 1, 0:256].bitcast(I32)
    rt_f = x_sb[:, 2, 0:256]
    rt_a = x_sb[:, 3, 0:256]
    rowf = sb("rowf", [128, NT])
    rowi = sb("rowi", [128, NT], I32)
    E("pool", lambda h: h.iota(rowi[:], pattern=[[2, NT]], base=0, channel_multiplier=0), (), ["rowi"])
    cp("dve", rowf[:], rowi[:], ["rowi"], ["rowf"])
    ts("dve", rowf[:], rowf[:], pidx_f[:, 2:3], None, ALU.add, None, ["rowf", "pidx_f"], ["rowf"])
    for (n, cs_t, sn_t) in ((8, cos32, sin32), (16, cos64, sin64)):
        fr_i = sb(f"fr_i{n}", [128, n], I32)
        fr = sb(f"fr{n}", [128, n])
        E("pool", lambda h, fr_i=fr_i, n=n: h.iota(fr_i[:], pattern=[[1, n]], base=0, channel_multiplier=0), (), [f"fr_i{n}"])
        cp("dve", fr[:], fr_i[:], [f"fr_i{n}"], [f"fr{n}"])
        actf(fr[:], fr[:], AF.Exp, [f"fr{n}"], [f"fr{n}"], scale=-math.log(10000.0) / n)
        W4 = 4 * n
        for which in ("sin", "cos"):
            ang = rt_a[:, 0:NT * 2 * n].rearrange("p (t a n) -> p t a n", t=NT, a=2)
            key = ("x", 3)
            tt("dve", ang[:, :, 0, :], rowf[:].unsqueeze(2).to_broadcast([128, NT, n]),
               fr[:].unsqueeze(1).to_broadcast([128, NT, n]), ALU.mult, ["rowf", f"fr{n}"], [key])
            tt("dve", ang[:, :, 1, :], pidx_f[:, 1:2].unsqueeze(2).to_broadcast([128, NT, n]),
               fr[:].unsqueeze(1).to_broadcast([128, NT, n]), ALU.mult, ["pidx_f", f"fr{n}", key], [key])
            flat = rt_a[:, 0:NT * 2 * n]
            if which == "cos":
                ts("dve", flat, flat, math.pi / 2, None, ALU.add, None, [key], [key])
            range_reduce("dve", flat, key, rt_i[:, 0:NT * 2 * n], ("x", 1), rt_f[:, 0:NT * 2 * n], ("x", 2))
            actf(flat, flat, AF.Sin, [key], [key])
            if which == "cos":
                dst = cs_t[:].rearrange("p t (a two n) -> p t a two n", a=2, two=2)
                for two in range(2):
                    cp("dve", dst[:, :, :, two, :], ang, [key], [f"cos{W4}"])
            else:
                dst = sn_t[:].rearrange("p t (a two n) -> p t a two n", a=2, two=2)
                cp("dve", dst[:, :, :, 0, :], ang, [key], [f"sin{W4}"])
                ts("dve", dst[:, :, :, 1, :], ang, -1.0, None, ALU.mult, None, [key], [f"sin{W4}"])

    if stop_after == "rope":
        S.dma("sp", ndk_d.ap()[0, 0, 0:128, 0:256], cos32[:].rearrange("p t w -> p (t w)"), reads=["cos32"])
        S.dma("sp", ndk_d.ap()[0, 1, 0:128, 0:256], sin32[:].rearrange("p t w -> p (t w)"), reads=["sin32"])
        st = S.emit()
        return nc, dbg_outs, st
    if stop_after == "mod":
        mod_step(100)
        st = S.emit()
        return nc, dbg_outs, st
    def _dma_nc(h, out, in_):
        return h.dma_start(out=out, in_=in_, allow_slow_non_contiguous=True)

    def ckpt(name):
        if stop_after == name:
            st = S.emit()
            raise _Stop((nc, dbg_outs, st))

    def ssm_build(l):
        P = f"sb{l}_"
        off = [0]

        def cv(shape, dt=F32):
            n = 1
            for s_ in shape[1:]:
                n *= s_
            esz = 2 if dt == BF16 else 4
            o = off[0]
            off[0] += (n * esz + 3) // 4 * 4
            assert off[0] <= WADA_OFF, off[0]
            return carve(o, shape, dt)

        if l > 0:
            S.alias([P + "all"], [k for k in S.last_w.keys() if isinstance(k, str) and k.startswith(f"sb{l-1}_")])
        ALL = [P + "all"] if l > 0 else []
        aRI = cv([128, 2, 32])
        dtt = cv([128, 32]); adt = cv([128, 32]); th = cv([128, 32])
        pim = cv([128, 5, 32, 8]); pre = cv([128, 5, 32, 8])
        cf = cv([128, 8, 32]); r8 = cv([128, 32])
        Bb = cv([128, 2, 32, 16]); Cri = cv([128, 2, 32, 16])
        mask = cv([128, 2, 128]); dcol = cv([128, 16]); phi = cv([128, 32])
        R1 = off[0]
        ld = cv([32, 2, 128])
        e5i = cv([128, 5, 32, 8], I32); e5 = cv([128, 5, 32, 8])
        mag = cv([128, 5, 32, 8]); ang = cv([128, 5, 32, 8])
        NE = 5 * 32 * 8
        itmp = cv([128, NE], I32); ftmp = cv([128, NE])
        Bsb = cv([32, 2, 1024]); Bri = cv([128, 2, 32, 16]); t1 = cv([128, 32, 16]); t2 = cv([128, 32, 16])
        Csb = cv([128, 2, 4, 64])
        mski = cv([128, 128], I32); mskf = cv([128, 128])
        early_keys = [P + k for k in ("ld", "e5i", "e5", "mag", "ang", "itmp", "ftmp", "Bsb", "Bri", "t1", "t2", "Csb", "mski", "mskf")]

        for half in range(2):
            S.dma("sp", ld[0:32, 0, half * 64:(half + 1) * 64], W["ssm_a_re"].ap()[l], reads=ALL, writes=[P + "ld"])
            S.dma("sp", ld[0:32, 1, half * 64:(half + 1) * 64], W["ssm_a_im"].ap()[l], reads=ALL, writes=[P + "ld"])
        bT_ = tr_rot.next()
        trs([(psf(bT_)[:, ri * 32:(ri + 1) * 32], ld[0:32, ri, :], identf[0:32, 0:32]) for ri in range(2)],
            [P + "ld", "identf"], [pk(bT_)])
        cp("dve", aRI.rearrange("p r g -> p (r g)"), psf(bT_)[:, 0:64], [pk(bT_)] + ALL, [P + "aRI"])
        aRe = aRI[:, 0, :]
        aIm = aRI[:, 1, :]
        S.dma("sp", dtt, W["ssm_log_dt"].ap()[l:l + 1, :].to_broadcast([128, 32]), reads=ALL, writes=[P + "dt"])
        actf(dtt, dtt, AF.Exp, [P + "dt"], [P + "dt"])
        tt("dve", adt, aRe, dtt, ALU.mult, [P + "aRI", P + "dt"] + ALL, [P + "adt"])
        tt("dve", th, aIm, dtt, ALU.mult, [P + "aRI", P + "dt"] + ALL, [P + "th"])
        specs = [(0, 0, 0, -1), (0, 1, 0, 1), (1, 0, 0, 1), (1, 1, 0, -1),
                 (2, 0, 7, -1), (2, 1, 0, 1), (3, 0, 1, 1), (3, 1, 8, -1), (4, 0, 1, 0), (4, 1, 1, 0)]
        for (slot, d_, base, step) in specs:
            E("pool", lambda h, slot=slot, d_=d_, base=base, step=step:
              h.iota(e5i[:, slot, d_ * 16:(d_ + 1) * 16, :], pattern=[[0, 16], [step, 8]], base=base, channel_multiplier=0),
              ALL, [P + "e5i"])
        fl = lambda a: a.rearrange("p a g s -> p (a g s)")
        cp("dve", fl(e5), fl(e5i), [P + "e5i"] + ALL, [P + "e5"])
        for a in range(5):
            tt("dve", mag[:, a], adt.unsqueeze(2).to_broadcast([128, 32, 8]), e5[:, a], ALU.mult,
               [P + "adt", P + "e5"] + ALL, [P + "mag"])
            tt("dve", ang[:, a], th.unsqueeze(2).to_broadcast([128, 32, 8]), e5[:, a], ALU.mult,
               [P + "th", P + "e5"] + ALL, [P + "ang"])
        actf(fl(mag), fl(mag), AF.Exp, [P + "mag"], [P + "mag"])
        ts("dve", fl(pre), fl(ang), math.pi / 2, None, ALU.add, None, [P + "ang"] + ALL, [P + "pre"])
        range_reduce("dve", fl(ang), P + "ang", itmp, P + "itmp", ftmp, P + "ftmp")
        actf(fl(pim), fl(ang), AF.Sin, [P + "ang"] + ALL, [P + "pim"])
        range_reduce("dve", fl(pre), P + "pre", itmp, P + "itmp", ftmp, P + "ftmp")
        actf(fl(pre), fl(pre), AF.Sin, [P + "pre"], [P + "pre"])
        tt("dve", fl(pim), fl(pim), fl(mag), ALU.mult, [P + "pim", P + "mag"], [P + "pim"])
        tt("dve", fl(pre), fl(pre), fl(mag), ALU.mult, [P + "pre", P + "mag"], [P + "pre"])
        dbg(f"pow_re{l}", fl(pre), [128, NE], [P + "pre"])
        dbg(f"pow_im{l}", fl(pim), [128, NE], [P + "pim"])
        ckpt("sb1")
        mod_step(2)
        abr = pre[:, 4, :, 0]
        abi = pim[:, 4, :, 0]
        K = P + "cf"
        ts("dve", cf[:, 0], abr, -1.0, None, ALU.add, None, [P + "pre"] + ALL, [K + "0"])
        tt("dve", cf[:, 1], cf[:, 0], aRe, ALU.mult, [K + "0", P + "aRI"] + ALL, [K + "1"])
        tt("dve", cf[:, 2], abi, aIm, ALU.mult, [P + "pim", P + "aRI"] + ALL, [K + "2"])
        tt("dve", cf[:, 1], cf[:, 1], cf[:, 2], ALU.add, [K + "1", K + "2"], [K + "1"])
        tt("dve", cf[:, 2], abi, aRe, ALU.mult, [P + "pim", P + "aRI", K + "1"], [K + "2"])
        tt("dve", cf[:, 3], cf[:, 0], aIm, ALU.mult, [K + "0", P + "aRI"] + ALL, [K + "3"])
        tt("dve", cf[:, 2], cf[:, 2], cf[:, 3], ALU.subtract, [K + "2", K + "3"], [K + "2"])
        tt("dve", cf[:, 3], aRe, aRe, ALU.mult, [P + "aRI", K + "2"], [K + "3"])
        tt("dve", cf[:, 4], aIm, aIm, ALU.mult, [P + "aRI"] + ALL, [K + "4"])
        tt("dve", cf[:, 3], cf[:, 3], cf[:, 4], ALU.add, [K + "3", K + "4"], [K + "3"])
        E("dve", lambda h: h.reciprocal(cf[:, 3], cf[:, 3]), [K + "3"], [K + "3"])
        tt("dve", cf[:, 5], cf[:, 1], cf[:, 3], ALU.mult, [K + "1", K + "3"] + ALL, [K + "5"])
        tt("dve", cf[:, 6], cf[:, 2], cf[:, 3], ALU.mult, [K + "2", K + "3"] + ALL, [K + "6"])
        actf(r8, adt, AF.Exp, [P + "adt"] + ALL, [P + "r8"], scale=8.0)
        S.dma("sp", ssm_scr_r.ap()[l], r8, reads=[P + "r8"], writes=[("scr_r", l)])
        ckpt("sb2")
        S.dma("sp", Bsb[0:32, 0, :], W["ssm_b_re"].ap()[l], reads=ALL, writes=[P + "Bsb"])
        S.dma("sp", Bsb[0:32, 1, :], W["ssm_b_im"].ap()[l], reads=ALL, writes=[P + "Bsb"])
        for ri in range(2):
            b_ = mm_rot.next()
            trs([(psf(b_)[0:64, c * 32:(c + 1) * 32], Bsb[0:32, ri, :].rearrange("g (p c) -> g c p", c=16)[:, c, :],
                  identf[0:32, 0:32]) for c in range(16)], [P + "Bsb", "identf"], [pk(b_)])
            cp("dve", Bri[0:64, ri].rearrange("p g c -> p c g"), psf(b_)[0:64, :].rearrange("p (c g) -> p c g", c=16),
               [pk(b_)] + ALL, [P + "Bri"])
        cre = cf[0:64, 5, :].unsqueeze(2).to_broadcast([64, 32, 16])
        cim = cf[0:64, 6, :].unsqueeze(2).to_broadcast([64, 32, 16])
        tt("dve", t1[0:64], cre, Bri[0:64, 0], ALU.mult, [K + "5", P + "Bri"] + ALL, [P + "t1"])
        tt("dve", t2[0:64], cim, Bri[0:64, 1], ALU.mult, [K + "6", P + "Bri"] + ALL, [P + "t2"])
        tt("dve", Bb[0:64, 0], t1[0:64], t2[0:64], ALU.subtract, [P + "t1", P + "t2"] + ALL, [P + "Bb0"])
        tt("dve", t1[0:64], cre, Bri[0:64, 1], ALU.mult, [K + "5", P + "Bri", P + "Bb0"], [P + "t1"])
        tt("dve", t2[0:64], cim, Bri[0:64, 0], ALU.mult, [K + "6", P + "Bri", P + "Bb0"], [P + "t2"])
        tt("dve", Bb[0:64, 1], t1[0:64], t2[0:64], ALU.add, [P + "t1", P + "t2"] + ALL, [P + "Bb1"])
        ckpt("sb3")
        S.dma("sp", Csb[:, 0], W["ssm_c_re"].ap()[l].rearrange("(j r) p -> r j p", r=128), reads=ALL, writes=[P + "Csb"])
        S.dma("sp", Csb[:, 1], W["ssm_c_im"].ap()[l].rearrange("(j r) p -> r j p", r=128), reads=ALL, writes=[P + "Csb"])
        for ri in range(2):
            b_ = mm_rot.next()
            trs([(psf(b_)[0:64, j * 128:(j + 1) * 128], Csb[:, ri, j, :], identf[:, :]) for j in range(4)],
                [P + "Csb", "identf"], [pk(b_)])
            cp("dve", Cri[0:64, ri].rearrange("p g c -> p (g c)"), psf(b_)[0:64, :], [pk(b_)] + ALL, [P + "Cri"])
        ckpt("sb4")
        E("pool", lambda h: h.iota(mski, pattern=[[1, 128]], base=0, channel_multiplier=0), ALL, [P + "mski"])
        E("dve", lambda h: h.tensor_single_scalar(mski, mski, 4, ALU.arith_shift_right), [P + "mski"], [P + "mski"])
        cp("dve", mskf, mski, [P + "mski"] + ALL, [P + "mskf"])
        ts("dve", mask[:, 0, :], mskf, pidx_f[:, 3:4], None, ALU.is_ge, None, [P + "mskf", "pidx_f"] + ALL, [P + "mask"])
        ts("dve", mask[:, 1, :], mskf, pidx_f[:, 3:4], None, ALU.is_le, None, [P + "mskf", "pidx_f"], [P + "mask"])
        for s_ in range(8):
            S.add("sp", lambda h, s_=s_: _dma_nc(h, dcol[s_ * 16:(s_ + 1) * 16, :],
                                                W["ssm_d"].ap()[l].rearrange("(g c) -> c g", c=16)),
                  ALL, [P + "dcol"], dma=True)
        ckpt("sb5")
        ts("dve", phi, th, 8.0, None, ALU.mult, None, [P + "th"] + ALL, [P + "phi"])
        range_reduce("dve", phi, P + "phi", itmp[:, 0:32], P + "itmp", ftmp[:, 0:32], P + "ftmp")
        mod_step(2)
        ckpt("sb6")
        S.alias([P + "late"], early_keys)
        LATE = [P + "late"]
        off[0] = R1
        GB = 8
        Mst = cv([128, GB, 128], BF16); PSst = cv([128, GB, 128], BF16)
        PSWst = cv([128, GB, 128], BF16); Qst = cv([128, GB, 128], BF16)
        Xre = cv([128, GB, 8, 16]); nXim = cv([128, GB, 8, 16]); Zre = cv([128, GB, 8, 16]); Zim = cv([128, GB, 8, 16])
        Pre = cv([128, GB, 8, 16]); Pim = cv([128, GB, 8, 16]); Qre = cv([128, GB, 8, 16]); nQim = cv([128, GB, 8, 16])
        ta = cv([128, GB, 8, 16]); tb = cv([128, GB, 8, 16])
        mtmp = cv([128, 128])
        kidx_i = cv([128, 128], I32); kidx = cv([128, 128])
        tab = cv([128, 8, 128]); tabo = cv([128, 8, 128])
        it2 = cv([128, 1024], I32); ft2 = cv([128, 1024])

        def cmul(slot, mat, mkeys, blk, out_re, out_im, neg_im, kout):
            g0, g1 = blk * GB, blk * GB + GB
            shp = [64, GB, 8, 16]
            pr = pre[0:64, slot, g0:g1, :].unsqueeze(3).to_broadcast(shp)
            pi = pim[0:64, slot, g0:g1, :].unsqueeze(3).to_broadcast(shp)
            mr = mat[0:64, 0, g0:g1, :].unsqueeze(2).to_broadcast(shp)
            mi = mat[0:64, 1, g0:g1, :].unsqueeze(2).to_broadcast(shp)
            rk = [P + "pre", P + "pim"] + mkeys + LATE
            tt("dve", ta[0:64], pr, mr, ALU.mult, rk, [P + "ta"])
            tt("pool", tb[0:64], pi, mi, ALU.mult, rk, [P + "tb"])
            tt("dve", out_re[0:64], ta[0:64], tb[0:64], ALU.subtract, [P + "ta", P + "tb"] + LATE, [kout + "r"])
            tt("dve", ta[0:64], pr, mi, ALU.mult, rk, [P + "ta"])
            tt("pool", tb[0:64], pi, mr, ALU.mult, rk, [P + "tb"])
            if neg_im:
                stt(out_im[0:64], ta[0:64], -1.0, tb[0:64], ALU.mult, ALU.subtract, [P + "ta", P + "tb"] + LATE, [kout + "i"])
            else:
                tt("dve", out_im[0:64], ta[0:64], tb[0:64], ALU.add, [P + "ta", P + "tb"] + LATE, [kout + "i"])

        BK = [P + "Bb0", P + "Bb1"]
        CK = [P + "Cri"]
        for blk in range(32 // GB):
            d_ = (blk * GB) // 16
            cmul(0, Bb, BK, blk, Xre, nXim, True, P + "X")
            cmul(1, Cri, CK, blk, Zre, Zim, False, P + "Z")
            cmul(2, Bb, BK, blk, Pre, Pim, False, P + "P")
            cmul(3, Cri, CK, blk, Qre, nQim, True, P + "Q")
            ckpt("sb6a")
            for gi in range(GB):
                g = (blk * GB + gi) % 16
                f = lambda a, gi=gi: a[0:64, gi].rearrange("p s c -> p (s c)")
                b_ = mm_rot.next()
                mm(psf(b_)[:, 0:128], [(f(Xre), f(Zre)), (f(nXim), f(Zim))],
                   [P + "Xr", P + "Xi", P + "Zr", P + "Zi"], [pk(b_)])
                if d_ == 0:
                    tt("dve", mtmp, psf(b_)[:, 0:128], mask[:, 0, :], ALU.mult, [pk(b_), P + "mask"] + LATE, [P + "mtmp"])
                    stt(Mst[:, gi, :], identf[:, :], dcol[:, g:g + 1], mtmp, ALU.mult, ALU.add,
                        [P + "mtmp", P + "dcol", "identf"] + LATE, [P + "Mst"])
                else:
                    tt("dve", Mst[:, gi, :], psf(b_)[:, 0:128], mask[:, 1, :], ALU.mult, [pk(b_), P + "mask"] + LATE, [P + "Mst"])
                ckpt("sb6b")
                b_ = mm_rot.next()
                trs([(psf(b_)[:, 0:64], f(Pre), identf[0:64, 0:64]), (psf(b_)[:, 64:128], f(Pim), identf[0:64, 0:64])],
                    [P + "Pr", P + "Pi", "identf"], [pk(b_)])
                cp("act", PSst[:, gi, :], psf(b_)[:, 0:128], [pk(b_)] + LATE, [P + "PSst"])
                cp("dve", PSWst[:, gi, :].rearrange("p (r q) -> p r q", r=2),
                   psf(b_)[:, 0:128].rearrange("p (r q) -> p r q", r=2)[:, ::-1, :], [pk(b_)] + LATE, [P + "PSWst"])
                ckpt("sb6c")
                b_ = mm_rot.next()
                mm(psf(b_)[:, 0:128], [(epad[0:64, 64:192], f(Qre)), (epad[0:64, 0:128], f(nQim))],
                   [P + "Qr", P + "Qi", "epad"], [pk(b_)])
                cp("act", Qst[:, gi, :], psf(b_)[:, 0:128], [pk(b_)] + LATE, [P + "Qst"])
                ckpt("sb6d")
            for idx, st_ in enumerate((Mst, PSst, PSWst, Qst)):
                S.dma("sp", ssm_scr_b.ap()[l, idx, :, blk * GB * 128:(blk + 1) * GB * 128], st_.rearrange("p g m -> p (g m)"),
                      reads=[P + ["Mst", "PSst", "PSWst", "Qst"][idx]], writes=[("scr_b", l)])
            ckpt("sb6e%d" % blk)
            mod_step(1)
        ckpt("sb7")
        E("pool", lambda h: h.iota(kidx_i, pattern=[[1, 128]], base=0, channel_multiplier=0), LATE, [P + "kidx_i"])
        cp("dve", kidx, kidx_i, [P + "kidx_i"] + LATE, [P + "kidx"])
        for blk in range(4):
            for which in range(2):
                tk = P + "tab"
                tt("dve", tab, phi[:, blk * 8:(blk + 1) * 8].unsqueeze(2).to_broadcast([128, 8, 128]),
                   kidx.unsqueeze(1).to_broadcast([128, 8, 128]), ALU.mult, [P + "phi", P + "kidx"] + LATE, [tk])
                tv = tab.rearrange("p g k -> p (g k)")
                tvo = tabo.rearrange("p g k -> p (g k)")
                if which == 0:
                    ts("dve", tv, tv, math.pi / 2, None, ALU.add, None, [tk], [tk])
                range_reduce("dve", tv, tk, it2, P + "it2", ft2, P + "ft2")
                if which == 0:
                    actf(tvo, tv, AF.Sin, [tk] + LATE, [P + "tabo"])
                else:
                    actf(tv, tv, AF.Sin, [tk], [tk])
                    ts("dve", tvo, tv, sgnv, None, ALU.mult, None, [tk, "sgnv"] + LATE, [P + "tabo"])
                S.dma("sp", ssm_scr_f.ap()[l, which, :, blk * 1024:(blk + 1) * 1024], tvo,
                      reads=[P + "tabo"], writes=[("scr_f", l)])
                S.add("sp", lambda h, blk=blk, which=which: _dma_nc(h, ssm_scr_c1.ap()[l, which, :, blk * 8:(blk + 1) * 8], tabo[:, :, 1]),
                      [P + "tabo"], [("scr_c1", l)], dma=True, cost=0.15, lat=2.5)
            mod_step(1)

    for l in range(2):
        ssm_build(l)
    mod_step(100)
    setup_keys = list(S.last_w.keys())

    if stop_after == "setup":
        st = S.emit()
        return nc, dbg_outs, st

    A_WIN = 0
    o = 30208
    dqT = carve(o, [128, 4, 1024], BF16); o += 4 * 1024 * 2
    dkT = carve(o, [128, 4, 1536], BF16); o += 4 * 1536 * 2
    dV = carve(o, [128, 12, 4, 65], BF16); o += 12 * 4 * 65 * 2
    o = (o + 3) // 4 * 4
    gqT = carve(o, [128, 4, 1024], BF16); o += 4 * 1024 * 2
    gkT = carve(o, [128, 2, 1536], BF16); o += 2 * 1536 * 2
    gV = carve(o, [128, 12, 2, 65], BF16); o += 12 * 2 * 65 * 2
    o = (o + 3) // 4 * 4
    mqT = carve(o, [128, 4, 1024], BF16); o += 4 * 1024 * 2
    mkT = carve(o, [128, 4, 1536], BF16); o += 4 * 1536 * 2
    mV = carve(o, [128, 12, 4, 65], BF16); o += 12 * 4 * 65 * 2
    o = (o + 3) // 4 * 4
    A_ATT_END = o
    uT = carve(o, [128, 2, 1024], BF16)
    cache_sb = carve(o, [128, 2, 928], BF16)
    n_g = carve(o, [128, D], F32)
    o += 4096
    assert o <= ARENA_BYTES, o
    mixed = carve(0, [128, NT, 1024], BF16)
    w_in_sb = carve(A_WIN, [128, 8, IN_COLS], BF16)
    UK = "uslot"
    ATT_KEYS = ([("dqT", t) for t in range(NT)] + [("dkT", t) for t in range(12)] + [("dV", t) for t in range(12)] +
                [("gqT", t) for t in range(NT)] + [("gkT", t) for t in range(12)] + [("gV", t) for t in range(12)] +
                [("mqT", t) for t in range(NT)] + [("mkT", t) for t in range(12)] + [("mV", t) for t in range(12)])
    MIX_KEYS = [("mixed", t) for t in range(NT)] + [("mixed_s", t) for t in range(NT)]
    TOKF = [("tokf", k) for k in range(5)]
    HB = [("hb", 0), ("hbq", 0), ("hbq", 1), ("hbk", 0), ("hb", 4)]

    gq_g = sb("gq_g", [128, 64]); gk_g = sb("gk_g", [128, 64]); sub_g = sb("sub_g", [128, 64])
    mq_g = sb("mq_g", [128, 192]); mkv_g = sb("mkv_g", [128, 128])
    lamt = sb("lamt", [128, 4, 32]); lams = sb("lams", [128, 8])
    w_uq = sb("w_uq", [128, 2, 384], BF16)
    w_ukv = sb("w_ukv", [128, 512], BF16)
    w_glu = sb("w_glu", [128, 2, 512], BF16)
    pT = [sb(f"pT{i}", [128, 512], BF16) for i in range(4)]
    pT_rot = Rot([0, 1, 2, 3])
    tokb = sb("tokb", [128, 1024], BF16)
    h0t = sb("h0t", [128, 2, 32])

    def st_(c0, c1=None):
        return stat[:, c0:(c1 if c1 is not None else c0 + 1)]

    def load_layer_params(l, job):
        for (t, nm, n) in ((gq_g, "gqa_qn_g", 64), (gk_g, "gqa_kn_g", 64), (sub_g, "diff_subln_g", 64),
                           (mq_g, "mla_qn_g", 192), (mkv_g, "mla_kvn_g", 128)):
            S.dma("sp", t[:, 0:n], W[nm].ap()[l:l + 1, :].to_broadcast([128, n]), writes=[nm])
        ts("dve", sub_g[:], sub_g[:], 1.0 - LAM_INIT[l], None, ALU.mult, None, ["diff_subln_g"], ["diff_subln_g"])
        for i, nm in enumerate(("diff_lq1", "diff_lk1", "diff_lq2", "diff_lk2")):
            S.dma("sp", lamt[:, i, :], W[nm].ap()[l:l + 1, :].to_broadcast([128, 32]), writes=[("lamt", i)])
        tt("dve", lamt[:, 0, :], lamt[:, 0, :], lamt[:, 1, :], ALU.mult, [("lamt", 0), ("lamt", 1)], [("lamt", 0)])
        tt("dve", lamt[:, 2, :], lamt[:, 2, :], lamt[:, 3, :], ALU.mult, [("lamt", 2), ("lamt", 3)], [("lamt", 2)])
        E("dve", lambda h: h.tensor_reduce(out=lams[:, 0:1], in_=lamt[:, 0, :], axis=AX.X, op=ALU.add), [("lamt", 0)], ["lams"])
        E("dve", lambda h: h.tensor_reduce(out=lams[:, 1:2], in_=lamt[:, 2, :], axis=AX.X, op=ALU.add), [("lamt", 2)], ["lams"])
        actf(lams[:, 2:4], lams[:, 0:2], AF.Exp, ["lams"], ["lams"])
        tt("dve", lams[:, 4:5], lams[:, 2:3], lams[:, 3:4], ALU.subtract, ["lams"], ["lams"])
        ts("dve", lams[:, 5:6], lams[:, 4:5], -1.0, -LAM_INIT[l], ALU.mult, ALU.add, ["lams"], ["lams"])
        S.dma("pool", w_uq[:, 0, :], W["mla_w_uq"].ap()[l, 0:128, :], writes=["w_uq"])
        S.dma("pool", w_uq[0:64, 1, :], W["mla_w_uq"].ap()[l, 128:192, :], writes=["w_uq"])
        S.dma("pool", w_ukv[:], W["mla_w_ukv"].ap()[l], writes=["w_ukv"])
        S.dma("pool", w_glu[:], W["ssm_w_glu"].ap()[l].rearrange("(k p) n -> p k n", p=128), writes=["w_glu"])

    def AK(i):
        return [("actT", i, kc) for kc in range(8)]

    def load_mod(l, cond, which):
        mc = modcol[:, which]
        MK = ("modcol", which)
        S.dma("sp", modb[:, 0, :],
              modscr.ap()[l, cond:cond + 1, (which * 3 + 2) * D:(which * 3 + 3) * D].to_broadcast([128, D]),
              reads=["modscr"], writes=[("modb", 2)])
        nm = "norm1_g" if which == 0 else "norm2_g"
        srcs = [modscr.ap()[l, cond, (which * 3 + 0) * D:(which * 3 + 1) * D],
                modscr.ap()[l, cond, (which * 3 + 1) * D:(which * 3 + 2) * D],
                W[nm].ap()[l, :]]
        for i_, src in enumerate(srcs):
            S.add("sp", lambda h, i_=i_, src=src: _dma_nc(h, mc[:, i_, :], src.rearrange("(k p) -> p k", p=128)),
                  ["modscr"], [(MK, i_)], dma=True, cost=0.15, lat=4.0)
        stt(mc[:, 1, :], mc[:, 1, :], 1.0, mc[:, 2, :], ALU.add, ALU.mult, [(MK, 1), (MK, 2)], [(MK, 1)])

    def norm_mod_transpose(i, which):
        xk = ("x", i)
        mc = modcol[:, which]
        MK = ("modcol", which)
        par = i % 2
        hbx = [hb, hb2][par]
        HK = HB if par == 0 else ["hb2"]
        sc0, sc1 = 60 + 2 * par, 61 + 2 * par
        actf(hbx[:], x_sb[:, i, :], AF.Square, [xk], [("st", sc0)] + HK, accum_out=st_(sc0))
        rstd_chain(st_(sc0), st_(sc1), 1.0 / D, ("st", sc0), ("st", sc1))
        ts("dve", hbx[:], x_sb[:, i, :], st_(sc1), None, ALU.mult, None, [xk, ("st", sc1)], HK)
        b_ = tr_rot.next()
        trs([(psb(b_)[:, kc * 128:(kc + 1) * 128], hbx[:, kc * 128:(kc + 1) * 128], ident[:, :]) for kc in range(8)],
            HK + ["ident"], [pk(b_)])
        for kc in range(8):
            o_ap = actT[:, kc, i * 128:(i + 1) * 128]
            i_ap = psb(b_)[:, kc * 128:(kc + 1) * 128]
            if kc % 2 == 0:
                E("act", lambda h, o_ap=o_ap, i_ap=i_ap, kc=kc: h.activation(out=o_ap, in_=i_ap, func=AF.Identity,
                                                                             scale=mc[:, 1, kc:kc + 1], bias=mc[:, 0, kc:kc + 1]),
                  [pk(b_), (MK, 0), (MK, 1)], [("actT", i, kc)], cost=0.5)
            else:
                ts("dve", o_ap, i_ap, mc[:, 1, kc:kc + 1], mc[:, 0, kc:kc + 1], ALU.mult, ALU.add,
                   [pk(b_), (MK, 0), (MK, 1)], [("actT", i, kc)])

    def rope(src, src_keys, dst, dst_keys, nh, n, cos_t, sin_t, i):
        Wd = 4 * n
        t1 = tokf[:, 0:nh * Wd]
        t2 = tmpf[:, 0:nh * Wd]
        tt("dve", t1.rearrange("p (h w) -> p h w", h=nh), src.rearrange("p (h w) -> p h w", h=nh),
           cos_t[:, i, :].unsqueeze(1).to_broadcast([128, nh, Wd]), ALU.mult, src_keys + ["cos%d" % Wd], [("tokf", 0)])
        tt("dve", t2.rearrange("p (h w) -> p h w", h=nh), src.rearrange("p (h w) -> p h w", h=nh),
           sin_t[:, i, :].unsqueeze(1).to_broadcast([128, nh, Wd]), ALU.mult, src_keys + ["sin%d" % Wd], ["tmpf"])
        v = lambda a: a.rearrange("p (ha two n) -> p ha two n", two=2, n=n)
        tt("dve", v(dst), v(t1), v(t2)[:, :, ::-1, :], ALU.add, [("tokf", 0), "tmpf"], dst_keys)

    def head_rms(src, nh, hd, g_tile, gkey, dst, rkeys, wkeys, scol):
        actf(tmpf[:, 0:nh * hd], src, AF.Square, rkeys, ["tmpf"])
        E("dve", lambda h: h.tensor_reduce(out=st_(scol, scol + nh), in_=tmpf[:, 0:nh * hd].rearrange("p (h d) -> p h d", h=nh),
                                           axis=AX.X, op=ALU.add), ["tmpf"], [("st", scol)])
        rstd_chain(st_(scol, scol + nh), st_(scol + 8, scol + 8 + nh), 1.0 / hd, ("st", scol), ("st", scol + 8))
        tt("dve", dst.rearrange("p (h d) -> p h d", h=nh), src.rearrange("p (h d) -> p h d", h=nh),
           st_(scol + 8, scol + 8 + nh).unsqueeze(2).to_broadcast([128, nh, hd]), ALU.mult, rkeys + [("st", scol + 8)], wkeys)
        tt("dve", dst.rearrange("p (h d) -> p h d", h=nh), dst.rearrange("p (h d) -> p h d", h=nh),
           g_tile[:, 0:hd].unsqueeze(1).to_broadcast([128, nh, hd]), ALU.mult, wkeys + [gkey], wkeys)

    def kv_expand(kt, kr_src, kr_keys):
        kcol = slice(kt * 128, (kt + 1) * 128)
        bb = aux_rot.next()
        mm(psf(bb)[:, 0:512], [(tokb[:, 256:384], w_ukv[:, :])], [("tokb", 1), "w_ukv"], [pk(bb)])
        mktok = hb[:, 576:960].rearrange("p (h w) -> p h w", h=4)
        kvp = psf(bb)[:, 0:512].rearrange("p (h w) -> p h w", h=4)
        cp("act", mktok[:, :, 0:64], kvp[:, :, 0:64], [pk(bb)], [("hbk", 0)])
        cp("dve", mV[:, kt, :, 0:64], kvp[:, :, 64:128], [pk(bb)], [("mV", kt)])
        cp("dve", mktok[:, :, 64:96], kr_src.unsqueeze(1).to_broadcast([128, 4, 32]), kr_keys + [("hbk", 0)], [("hbk", 0)])
        bt = tr_rot.next()
        trs([(psb(bt)[0:96, h * 128:(h + 1) * 128], hb[:, 576 + h * 96:576 + (h + 1) * 96], ident[:, :]) for h in range(4)],
            [("hbk", 0), "ident"], [pk(bt)])
        cp("act", mkT[0:96, :, kcol], psb(bt)[0:96, 0:512].rearrange("p (h t) -> p h t", h=4), [pk(bt)], [("mkT", kt)])

    def in_proj_tile(job, l, i, kt):
        rp = job["rope"]
        OW = "own"
        ow = own[:, 0, :]
        bounds = [0, 512, 1024, 1536, IN_COLS]
        banks = []
        for c in range(4):
            b_ = mm_rot.next()
            banks.append(b_)
            n0, n1 = bounds[c], bounds[c + 1]
            mm(psf(b_)[:, 0:n1 - n0], [(actT[:, kc, i * 128:(i + 1) * 128], w_in_sb[:, kc, n0:n1]) for kc in range(8)],
               AK(i) + ["w_in"], [pk(b_)])
        b0, b1, b2, b3 = banks
        tcol = slice(i * 128, (i + 1) * 128)
        kcol = slice(kt * 128, (kt + 1) * 128)
        if rp:
            rope(psf(b0)[:, 0:256], [pk(b0)], tokb[:, 0:256], [("tokb", 0)], 8, 8, cos32, sin32, i)
            rope(psf(b0)[:, 256:512], [pk(b0)], tokb[:, 256:512], [("tokb", 1)], 8, 8, cos32, sin32, i)
        else:
            cp("act", tokb[:, 0:256], psf(b0)[:, 0:256], [pk(b0)], [("tokb", 0)])
            cp("dve", tokb[:, 256:512], psf(b0)[:, 256:512], [pk(b0)], [("tokb", 1)])
        if job["caches_out"]:
            cp("act", ow[:, 0:256], psf(b0)[:, 256:512], [pk(b0)], [OW])
        bt = tr_rot.next()
        trs([(psb(bt)[0:64, j * 128:(j + 1) * 128], tokb[:, j * 64:(j + 1) * 64], ident[:, :]) for j in range(8)],
            [("tokb", 0), ("tokb", 1), "ident"], [pk(bt)])
        cp("act", dqT[0:64, :, tcol], psb(bt)[0:64, 0:512].rearrange("p (h t) -> p h t", h=4), [pk(bt)], [("dqT", i)])
        cp("act", dkT[0:64, :, kcol], psb(bt)[0:64, 512:1024].rearrange("p (h t) -> p h t", h=4), [pk(bt)], [("dkT", kt)])
        if job["caches_out"]:
            cp("act", ow[:, 256:512], psf(b1)[:, 0:256], [pk(b1)], [OW])
        cp("act", dV[:, kt, :, 0:64], psf(b1)[:, 0:256].rearrange("p (h d) -> p h d", h=4), [pk(b1)], [("dV", kt)])
        head_rms(psf(b1)[:, 256:512], 4, 64, gq_g, "gqa_qn_g", tokf[:, 256:512], [pk(b1)], [("tokf", 1)], 8)
        if rp:
            rope(tokf[:, 256:512], [("tokf", 1)], tokb[:, 512:768], [("tokb", 2)], 4, 16, cos64, sin64, i)
        else:
            cp("dve", tokb[:, 512:768], tokf[:, 256:512], [("tokf", 1)], [("tokb", 2)])
        head_rms(psf(b2)[:, 0:128], 2, 64, gk_g, "gqa_kn_g", ow[:, 512:640], [pk(b2)], [OW], 24)
        if rp:
            rope(ow[:, 512:640], [OW], tokb[:, 768:896], [("tokb", 3)], 2, 16, cos64, sin64, i)
        else:
            cp("dve", tokb[:, 768:896], ow[:, 512:640], [OW], [("tokb", 3)])
        if job["caches_out"]:
            cp("act", ow[:, 640:768], psf(b2)[:, 128:256], [pk(b2)], [OW])
        cp("act", gV[:, kt, :, 0:64], psf(b2)[:, 128:256].rearrange("p (h d) -> p h d", h=2), [pk(b2)], [("gV", kt)])
        bt = tr_rot.next()
        trs([(psb(bt)[0:64, j * 128:(j + 1) * 128], tokb[:, 512 + j * 64:512 + (j + 1) * 64], ident[:, :]) for j in range(6)],
            [("tokb", 2), ("tokb", 3), "ident"], [pk(bt)])
        cp("act", gqT[0:64, :, tcol], psb(bt)[0:64, 0:512].rearrange("p (h t) -> p h t", h=4), [pk(bt)], [("gqT", i)])
        cp("act", gkT[0:64, :, kcol], psb(bt)[0:64, 512:768].rearrange("p (h t) -> p h t", h=2), [pk(bt)], [("gkT", kt)])
        cp("act", hb[:, 0:256], psf(b2)[:, 256:512], [pk(b2)], [("hb", 0)])
        bt = tr_rot.next()
        trs([(psb(bt)[:, j * 128:(j + 1) * 128], hb[:, j * 128:(j + 1) * 128], ident[:, :]) for j in range(2)],
            [("hb", 0), "ident"], [pk(bt)])
        cp("act", uT[:, :, tcol], psb(bt)[:, 0:256].rearrange("p (k t) -> p k t", k=2), [pk(bt)], [UK])
        actf(junk[:, 0:192], psf(b3)[:, 0:192], AF.Square, [pk(b3)], [("st", 40)], accum_out=st_(40))
        rstd_chain(st_(40), st_(41), 1.0 / 192, ("st", 40), ("st", 41))
        stt(hb[:, 256:448], psf(b3)[:, 0:192], st_(41), mq_g[:, 0:192], ALU.mult, ALU.mult,
            [pk(b3), ("st", 41), "mla_qn_g"], [("hbq", 0)])
        actf(junk[:, 256:384], psf(b3)[:, 192:320], AF.Square, [pk(b3)], [("st", 42)], accum_out=st_(42))
        rstd_chain(st_(42), st_(43), 1.0 / 128, ("st", 42), ("st", 43))
        stt(ow[:, 768:896], psf(b3)[:, 192:320], st_(43), mkv_g[:, 0:128], ALU.mult, ALU.mult,
            [pk(b3), ("st", 43), "mla_kvn_g"], [OW])
        cp("dve", hb[:, 448:576], ow[:, 768:896], [OW], [("hbq", 1)])
        cp("act", ow[:, 896:928], psf(b3)[:, 320:352], [pk(b3)], [OW])
        bt = tr_rot.next()
        trs([(psb(bt)[:, 0:128], hb[:, 256:384], ident[:, :]), (psb(bt)[0:64, 128:256], hb[:, 384:448], ident[:, :]),
             (psb(bt)[:, 256:384], hb[:, 448:576], ident[:, :])], [("hbq", 0), ("hbq", 1), "ident"], [pk(bt)])
        cp("act", tokb[:, 0:128], psb(bt)[:, 0:128], [pk(bt)], [("tokb", 0)])
        cp("act", tokb[0:64, 128:256], psb(bt)[0:64, 128:256], [pk(bt)], [("tokb", 0)])
        cp("dve", tokb[:, 256:384], psb(bt)[:, 256:384], [pk(bt)], [("tokb", 1)])
        ba = aux_rot.next()
        mm(psf(ba)[:, 0:384], [(tokb[:, 0:128], w_uq[:, 0, :]), (tokb[0:64, 128:256], w_uq[0:64, 1, :])],
           [("tokb", 0), "w_uq"], [pk(ba)])
        mqtok = tokb[:, 512:896].rearrange("p (h w) -> p h w", h=4)
        mqp = psf(ba)[:, 0:384].rearrange("p (h w) -> p h w", h=4)
        cp("act", mqtok[:, :, 0:64], mqp[:, :, 0:64], [pk(ba)], [("tokb", 2), ("tokb", 3)])
        if rp:
            cp("dve", tokf[:, 768:896].rearrange("p (h w) -> p h w", h=4), mqp[:, :, 64:96], [pk(ba)], [("tokf", 3)])
            rope(tokf[:, 768:896], [("tokf", 3)], tokf[:, 896:1024], [("tokf", 4)], 4, 8, cos32, sin32, i)
            cp("dve", mqtok[:, :, 64:96], tokf[:, 896:1024].rearrange("p (h w) -> p h w", h=4), [("tokf", 4)],
               [("tokb", 2), ("tokb", 3)])
        else:
            cp("dve", mqtok[:, :, 64:96], mqp[:, :, 64:96], [pk(ba)], [("tokb", 2), ("tokb", 3)])
        bt = tr_rot.next()
        trs([(psb(bt)[0:96, h * 128:(h + 1) * 128], tokb[:, 512 + h * 96:512 + (h + 1) * 96], ident[:, :]) for h in range(4)],
            [("tokb", 2), ("tokb", 3), "ident"], [pk(bt)])
        cp("act", mqT[0:96, :, tcol], psb(bt)[0:96, 0:512].rearrange("p (h t) -> p h t", h=4), [pk(bt)], [("mqT", i)])
        if rp:
            rope(ow[:, 896:928], [OW], tokf[:, 896:928], [("tokf", 4)], 1, 8, cos32, sin32, i)
            kv_expand(kt, tokf[:, 896:928], [("tokf", 4)])
        else:
            kv_expand(kt, ow[:, 896:928], [OW])
        if job["caches_out"]:
            sq = i // 2
            t0 = (i % 2) * 128
            for (dst, c0, c1) in ((ndk_d, 0, 256), (ndv_d, 256, 512), (ngk_d, 512, 640), (ngv_d, 640, 768),
                                  (nckv_d, 768, 896), (nkr_d, 896, 928)):
                S.dma("sp", dst.ap()[sq, l, t0:t0 + 128, :], ow[:, c0:c1], reads=[OW])

    def past_tiles(job, l):
        for half in range(2):
            for (src, c0, c1) in ((cdk_d, 0, 256), (cdv_d, 256, 512), (cgk_d, 512, 640), (cgv_d, 640, 768),
                                  (cckv_d, 768, 896), (ckr_d, 896, 928)):
                S.dma("pool", cache_sb[:, :, c0:c1], src.ap()[l, half * 256:(half + 1) * 256, :].rearrange("(j p) n -> p j n", p=128),
                      writes=[UK])
            for kk in range(2):
                kt = half * 2 + kk
                kcol = slice(kt * 128, (kt + 1) * 128)
                cs_ = cache_sb[:, kk, :]
                bt = tr_rot.next()
                trs([(psb(bt)[0:64, j * 128:(j + 1) * 128], cs_[:, j * 64:(j + 1) * 64], ident[:, :]) for j in range(4)] +
                    [(psb(bt)[0:64, (4 + j) * 128:(5 + j) * 128], cs_[:, 512 + j * 64:512 + (j + 1) * 64], ident[:, :]) for j in range(2)] +
                    [(psb(bt)[:, 768:896], cs_[:, 768:896], ident[:, :])], [UK, "ident"], [pk(bt)])
                cp("act", dkT[0:64, :, kcol], psb(bt)[0:64, 0:512].rearrange("p (h t) -> p h t", h=4), [pk(bt)], [("dkT", kt)])
                cp("dve", gkT[0:64, :, kcol], psb(bt)[0:64, 512:768].rearrange("p (h t) -> p h t", h=2), [pk(bt)], [("gkT", kt)])
                cp("act", tokb[:, 256:384], psb(bt)[:, 768:896], [pk(bt)], [("tokb", 1)])
                cp("dve", dV[:, kt, :, 0:64], cs_[:, 256:512].rearrange("p (h d) -> p h d", h=4), [UK], [("dV", kt)])
                cp("dve", gV[:, kt, :, 0:64], cs_[:, 640:768].rearrange("p (h d) -> p h d", h=2), [UK], [("gV", kt)])
                kv_expand(kt, cs_[:, 896:928], [UK])

    def attention(job, l):
        nseq, Ts, past = job["nseq"], job["Ts"], job["past"]
        nk = (past + Ts) // 128
        qb = min(512, Ts)
        nqb = Ts // qb
        nsub = qb // 128
        o1 = tokf[:, 0:256].rearrange("p (s d) -> p s d", s=4)
        o2 = tokf[:, 256:512].rearrange("p (s d) -> p s d", s=4)
        att_s_rot = Rot([0, 1])
        att_o_rot = Rot([2, 3])
        for sq in range(nseq):
            tile0 = sq * (Ts // 128)
            kt0 = 0 if past else tile0
            items = []
            for h_ in range(4):
                for qi_ in range(nqb):
                    for j_ in range(2):
                        items.append((2 * h_ + j_, qi_))
            for hs_ in range(8, 16):
                for qi_ in range(nqb):
                    items.append((hs_, qi_))
            for (hs, qi) in items:
                if hs < 8:
                    kind, h, j = "d", hs // 2, hs % 2
                    QT, KT, V, vh, r0, r1, scale = dqT, dkT, dV, h, 32 * j, 32 * j + 32, 32 ** -0.5
                    qh, kh = h, h
                elif hs < 12:
                    kind, h, j = "g", hs - 8, 0
                    QT, KT, V, vh, r0, r1, scale = gqT, gkT, gV, h // 2, 0, 64, 64 ** -0.5
                    qh, kh = h, h // 2
                else:
                    kind, h, j = "m", hs - 12, 0
                    QT, KT, V, vh, r0, r1, scale = mqT, mkT, mV, h, 0, 96, 96 ** -0.5
                    qh, kh = h, h
                qkey = {"d": "dqT", "g": "gqT", "m": "mqT"}[kind]
                kkey = {"d": "dkT", "g": "gkT", "m": "mkT"}[kind]
                vkey = {"d": "dV", "g": "gV", "m": "mV"}[kind]
                if True:
                    q0 = tile0 * 128 + qi * qb
                    bo = att_o_rot.next()
                    ops_ = psf(bo)[:, 0:nsub * 65].rearrange("p (s d) -> p s d", s=nsub)
                    qtiles = [tile0 + qi * nsub + s_ for s_ in range(nsub)]
                    for kk in range(nk):
                        kt = kt0 + kk
                        bs = att_s_rot.next()
                        mm(psf(bs)[:, 0:qb], [(KT[r0:r1, kh, kt * 128:(kt + 1) * 128], QT[r0:r1, qh, q0:q0 + qb])],
                           [(kkey, kt)] + [(qkey, t_) for t_ in qtiles], [pk(bs)])
                        pi = pT_rot.next()
                        actf(pT[pi][:, 0:qb], psf(bs)[:, 0:qb], AF.Exp, [pk(bs)], [("pT", pi)], scale=scale)

                        def pv(hh, pi=pi, kt=kt, kk=kk, ops_=ops_, V=V, vh=vh):
                            ins = None
                            for s_ in range(nsub):
                                ins = hh.matmul(ops_[:, s_, :], lhsT=pT[pi][:, s_ * 128:(s_ + 1) * 128], rhs=V[:, kt, vh, :],
                                                start=(kk == 0 and s_ == 0), stop=(kk == nk - 1), skip_group_check=True)
                            return ins
                        E("pe", pv, [("pT", pi), (vkey, kt)], [pk(bo)], cost=0.1 + nsub * 0.09)
                    E("dve", lambda h_, ops_=ops_: h_.reciprocal(st_(48, 48 + nsub), ops_[:, :, 64]), [pk(bo)], [("st", 48)])
                    rec = st_(48, 48 + nsub).unsqueeze(2).to_broadcast([128, nsub, 64])
                    mk_ = [("mixed", t_) for t_ in qtiles]
                    if kind == "d":
                        dst = o1 if j == 0 else o2
                        tt("dve", dst[:, 0:nsub, :], ops_[:, :, 0:64], rec, ALU.mult, [pk(bo), ("st", 48)], [("tokf", j)])
                        if j == 1:
                            dif = tokf[:, 512:768].rearrange("p (s d) -> p s d", s=4)[:, 0:nsub, :]
                            stt(dif, o2[:, 0:nsub, :], lams[:, 5:6], o1[:, 0:nsub, :], ALU.mult, ALU.add,
                                [("tokf", 0), ("tokf", 1), "lams"], [("tokf", 2)])
                            sqv = tmpf[:, 0:nsub * 64].rearrange("p (s d) -> p s d", s=nsub)
                            tt("pool", sqv, dif, dif, ALU.mult, [("tokf", 2)], ["tmpf"])
                            E("dve", lambda h_, sqv=sqv: h_.tensor_reduce(out=st_(52, 52 + nsub), in_=sqv, axis=AX.X, op=ALU.add),
                              ["tmpf"], [("st", 52)])
                            rstd_chain(st_(52, 52 + nsub), st_(56, 56 + nsub), 1.0 / 64, ("st", 52), ("st", 56))
                            tt("dve", dif, dif, st_(56, 56 + nsub).unsqueeze(2).to_broadcast([128, nsub, 64]), ALU.mult,
                               [("tokf", 2), ("st", 56)], [("tokf", 2)])
                            mdst = mixed[:, qtiles[0]:qtiles[0] + nsub, h * 64:(h + 1) * 64]
                            tt("dve", mdst, dif, sub_g[:, 0:64].unsqueeze(1).to_broadcast([128, nsub, 64]), ALU.mult,
                               [("tokf", 2), "diff_subln_g"], mk_)
                    else:
                        c0 = (256 if kind == "g" else 768) + h * 64
                        mdst = mixed[:, qtiles[0]:qtiles[0] + nsub, c0:c0 + 64]
                        tt("dve", mdst, ops_[:, :, 0:64], rec, ALU.mult, [pk(bo), ("st", 48)], mk_)

    def ssm_run(job, l):
        nm = job["name"]
        nseq, Ts = job["nseq"], job["Ts"]
        nch = Ts // 8
        P = f"sr{l}{nm}_"
        RK = P + "region"
        o_ = [16384]
        o2_ = [A_ATT_END + 4096]

        def cv(shape, dt=F32, tail=False):
            n = 1
            for s_ in shape[1:]:
                n *= s_
            esz = 2 if dt == BF16 else 4
            sz = (n * esz + 3) // 4 * 4
            if tail:
                oo = o2_[0]
                o2_[0] += sz
                assert o2_[0] <= ARENA_BYTES, o2_[0]
            else:
                oo = o_[0]
                o_[0] += sz
                assert o_[0] <= 30208, o_[0]
            return carve(oo, shape, dt)

        prm = []
        for i_ in range(2):
            prm.append(dict(M=cv([128, 2, 128], BF16), PS=cv([128, 2, 128], BF16), PSW=cv([128, 2, 128], BF16),
                            Q=cv([128, 2, 128], BF16), CC=cv([128, 2, 128]), SS=cv([128, 2, 128])))
        Ug = [cv([128, 128], BF16) for _ in range(2)]
        Wt = [cv([128, 128]) for _ in range(2)]
        T1 = [cv([128, 128]) for _ in range(2)]
        Gt = [cv([128, 128]) for _ in range(2)]
        Hb = [cv([128, nseq, nch + 1], BF16) for _ in range(4)]
        Yg = [cv([128, 128], BF16) for _ in range(2)]
        Hfin = cv([128, nseq, 32], tail=True)
        g0 = cv([128, 32], tail=True)
        g0t = cv([128, 32], tail=True)
        r8 = cv([128, 32], tail=True)
        c1 = cv([128, 2, 32], tail=True)
        h0b = cv([128, 32], BF16, tail=True)
        ygT = [tokb[:, :], hb[:, :]]
        YGK = [[("tokb", k) for k in range(4)], HB]
        sgl = tmpf[:, 0:256]
        S.dma("sp", r8, ssm_scr_r.ap()[l], reads=[("scr_r", l), RK], writes=[P + "r8"])

        def load_prm(g):
            gs = g % 2
            pr = prm[gs]
            for idx, kk_ in enumerate(("M", "PS", "PSW", "Q")):
                S.dma("sp", pr[kk_], ssm_scr_b.ap()[l, idx].rearrange("p (d g m) -> p d g m", d=2, g=16)[:, :, g, :],
                      reads=[("scr_b", l), RK], writes=[(P + "prm" + kk_, gs)])
            for idx, kk_ in enumerate(("CC", "SS")):
                S.dma("sp", pr[kk_], ssm_scr_f.ap()[l, idx].rearrange("p (d g m) -> p d g m", d=2, g=16)[:, :, g, :],
                      reads=[("scr_f", l), RK], writes=[(P + "prm" + kk_, gs)])

        if job["past"]:
            ld = tokf[0:32, 0:256].rearrange("p (r q) -> p r q", r=2)
            LDK = [("tokf", 0)]
            S.dma("sp", ld[:, 0, 0:64], h0re_d.ap()[l], writes=LDK)
            S.dma("sp", ld[:, 0, 64:128], h0im_d.ap()[l], writes=LDK)
            S.dma("sp", ld[:, 1, 0:64], h0im_d.ap()[l], writes=LDK)
            S.dma("sp", ld[:, 1, 64:128], h0re_d.ap()[l], writes=LDK)
            S.dma("sp", c1, ssm_scr_c1.ap()[l].rearrange("w p g -> p w g"), reads=[("scr_c1", l), RK], writes=[P + "c1"])
            bT_ = 4
            trs([(psf(bT_)[:, r * 32:(r + 1) * 32], ld[:, r, :], identf[0:32, 0:32]) for r in range(2)],
                LDK + ["identf", RK], [pk(bT_)])
            cp("dve", h0t[:].rearrange("p r g -> p (r g)"), psf(bT_)[:, 0:64], [pk(bT_)], ["h0t"])
            cp("dve", h0b, h0t[:, 0, :], ["h0t", RK], [P + "h0b"])
            tt("dve", g0, c1[:, 0, :], h0t[:, 0, :], ALU.mult, [P + "c1", "h0t", RK], [P + "g0"])
            tt("dve", g0t, c1[:, 1, :], h0t[:, 1, :], ALU.mult, [P + "c1", "h0t", RK], [P + "g0t"])
            tt("dve", g0, g0, g0t, ALU.add, [P + "g0", P + "g0t"], [P + "g0"])
        NC_ = nseq * nch
        ybanks = {0: (6, 7), 1: (6, 7)}
        s_rots = {0: Rot([4]), 1: Rot([4])}
        y_rot = Rot([5])
        v3 = lambda a: a[:, 0:NC_].rearrange("p (q k) -> p q k", q=nseq)
        load_prm(0)
        for g in range(16):
            ch, gl = g // 8, g % 8
            ui = g % 2
            gs = g % 2
            pr = prm[gs]
            PKf = lambda nm_, gs=gs: (P + "prm" + nm_, gs)
            s_rot = s_rots[ch]
            if g + 1 < 16:
                load_prm(g + 1)
            bu = s_rot.next()
            mm(psf(bu)[:, 0:NC_], [(rpad[:, gl, 112 - 16 * s_:240 - 16 * s_],
                                    uT[:, ch, :].rearrange("p (k s) -> p s k", s=8)[:, s_, :]) for s_ in range(8)],
               [UK, "rpad"], [pk(bu)])
            cp("act", Ug[ui], psf(bu)[:, 0:NC_], [pk(bu), RK], [(P + "Ug", ui)])
            by = y_rot.next()
            for d_ in range(2):
                dg = d_ * 16 + g
                wi = d_
                bs_ = s_rot.next()

                def ssw(hh, bs_=bs_, d_=d_, ui=ui, pr=pr):
                    hh.matmul(psf(bs_)[:, 0:128], lhsT=pr["PS"][:, d_, :], rhs=Ug[ui], start=True, stop=True)
                    return hh.matmul(psf(bs_)[:, 128:256], lhsT=pr["PSW"][:, d_, :], rhs=Ug[ui], start=True, stop=True)
                E("pe", ssw, [PKf("PS"), PKf("PSW"), (P + "Ug", ui)], [pk(bs_)], cost=0.35)

                def tabv(tab, d_=d_):
                    t_ = tab[:, d_, 0:nch]
                    if d_ == 1:
                        t_ = t_[:, ::-1]
                    return t_.unsqueeze(1).to_broadcast([128, nseq, nch])

                tt("dve", v3(Wt[wi]), v3(psf(bs_)), tabv(pr["CC"]), ALU.mult, [pk(bs_), PKf("CC"), RK], [(P + "Wt", wi)])
                tt("dve", v3(T1[wi]), psf(bs_)[:, 128:256].rearrange("p (q k) -> p q k", q=nseq), tabv(pr["SS"]), ALU.mult,
                   [pk(bs_), PKf("SS"), RK], [(P + "T1", wi)])
                tt("pool", Wt[wi][:, 0:NC_], Wt[wi][:, 0:NC_], T1[wi][:, 0:NC_], ALU.subtract,
                   [(P + "Wt", wi), (P + "T1", wi)], [(P + "Wt", wi)])
                for sq in range(nseq):
                    wv = Wt[wi][:, sq * nch:(sq + 1) * nch]
                    gv = Gt[wi][:, sq * nch:(sq + 1) * nch]
                    if d_ == 1:
                        wv = wv[:, ::-1]
                        gv = gv[:, ::-1]
                    init = g0[:, dg:dg + 1] if job["past"] else 0.0
                    rk = [(P + "Wt", wi), P + "r8", RK] + ([P + "g0"] if job["past"] else [])
                    E("dve", lambda h_, wv=wv, gv=gv, init=init, dg=dg:
                      h_.tensor_tensor_scan(out=gv, data0=r8[:, dg:dg + 1].to_broadcast([128, nch]), data1=wv,
                                            initial=init, op0=ALU.mult, op1=ALU.add), rk, [(P + "Gt", wi)],
                      cost=0.15 + 2 * nch / 960.0)
                bg = s_rot.next()
                mm(psf(bg)[:, 0:NC_], [(j2[:, :], Gt[wi][:, 0:NC_])], [(P + "Gt", wi), "j2"], [pk(bg)])
                tt("dve", v3(Wt[wi]), v3(Gt[wi]), tabv(pr["CC"]), ALU.mult, [(P + "Gt", wi), PKf("CC")], [(P + "Wt", wi)])
                tt("dve", v3(T1[wi]), v3(psf(bg)), tabv(pr["SS"]), ALU.mult, [pk(bg), PKf("SS")], [(P + "T1", wi)])
                hbi = d_ * 2 + (g % 2)
                HBt = Hb[hbi]
                if d_ == 0:
                    hdst, hinit, hrhs = HBt[:, :, 1:nch + 1], HBt[:, :, 0], HBt[:, :, 0:nch]
                else:
                    hdst, hinit, hrhs = HBt[:, :, 0:nch], HBt[:, :, nch], HBt[:, :, 1:nch + 1]
                tt("pool", hdst, v3(Wt[wi]), v3(T1[wi]), ALU.add, [(P + "Wt", wi), (P + "T1", wi), RK], [(P + "Hb", hbi)])
                if job["past"]:
                    cp("dve", hinit, h0b[:, dg:dg + 1].to_broadcast([128, nseq]), [P + "h0b", RK], [(P + "Hb", hbi)])
                else:
                    memset("dve", hinit, 0.0, [(P + "Hb", hbi)])
                if job["caches_out"]:
                    col = nch - 1 if d_ == 0 else 0
                    fv = v3(Wt[wi])[:, :, col]
                    tv = v3(T1[wi])[:, :, col]
                    tt("dve", Hfin[:, :, dg], fv, tv, ALU.add, [(P + "Wt", wi), (P + "T1", wi), RK], [P + "Hfin"])
                mm1(psf(by)[:, 0:NC_], pr["M"][:, d_, :], Ug[ui], d_ == 0, False, [PKf("M"), (P + "Ug", ui)], [pk(by)])
                mm1(v3(psf(by)), pr["Q"][:, d_, :], hrhs, False, d_ == 1, [PKf("Q"), (P + "Hb", hbi)], [pk(by)])
            actf(Yg[ui], psf(by)[:, 0:NC_], AF.Gelu_apprx_tanh, [pk(by), RK], [(P + "Yg", ui)])
            yb = ybanks[ch]
            for tau in range(8):
                pb_ = yb[tau // 4]
                outp = psf(pb_)[:, (tau % 4) * 128:(tau % 4 + 1) * 128]
                mm1(outp, rpad[:, tau, 112 - 16 * gl:240 - 16 * gl], Yg[ui], gl == 0 and tau % 4 == 0, gl == 7,
                    [(P + "Yg", ui), "rpad"], [pk(pb_)], skip=True)
            if gl == 7:
                for half in range(2):
                    pb_ = yb[half]
                    dstv = ygT[ch].rearrange("p (k s) -> p s k", s=8)[:, half * 4:(half + 1) * 4, :]
                    cp("act", dstv, psf(pb_)[:, :].rearrange("p (s k) -> p s k", s=4), [pk(pb_)], YGK[ch])
        for i in range(NT):
            bz = [4, 5][i % 2]
            mm(psf(bz)[:, 0:512], [(ygT[kc][:, i * 128:(i + 1) * 128], w_glu[:, kc, :]) for kc in range(2)],
               YGK[0] + YGK[1] + ["w_glu"], [pk(bz)])
            actf(sgl, psf(bz)[:, 256:512], AF.Sigmoid, [pk(bz)], ["tmpf"])
            tt("dve", mixed[:, i, 512:768], psf(bz)[:, 0:256], sgl, ALU.mult, [pk(bz), "tmpf", RK], [("mixed_s", i)])
        if job["caches_out"]:
            bT_ = 4
            trs([(psf(bT_)[:, 0:128], Hfin.rearrange("p q g -> p (q g)"), identf[:, :])], [P + "Hfin", "identf"], [pk(bT_)])
            cp("dve", tokf[:, 0:128], psf(bT_)[:, 0:128], [pk(bT_)], [("tokf", 0)])
            for sq in range(nseq):
                S.dma("sp", nsre_d.ap()[sq, l], tokf[sq * 32:(sq + 1) * 32, 0:64], reads=[("tokf", 0)])
                S.dma("sp", nsim_d.ap()[sq, l], tokf[sq * 32:(sq + 1) * 32, 64:128], reads=[("tokf", 0)])
        return [RK] + [(P + "prm" + n_, i_) for n_ in ("M", "PS", "PSW", "Q", "CC", "SS") for i_ in range(2)] + [P + "r8", P + "c1", P + "Hfin", P + "h0b", P + "g0", P + "g0t"] + \
               [(P + k, i_) for k in ("Ug", "Wt", "T1", "Gt", "Yg") for i_ in range(2)] + [(P + "Hb", i_) for i_ in range(4)]

    def out_proj(job, l, ssm_keys):
        nm = job["name"]
        P = f"op{l}{nm}_"
        w_out_sb = carve(30208, [128, 8, 1024], BF16)
        S.alias(["w_out"], ATT_KEYS)
        S.dma("pool", w_out_sb, W["w_out"].ap()[l].rearrange("(k p) n -> p k n", p=128), writes=["w_out"])
        for i in range(NT):
            b_ = tr_rot.next()
            trs([(psb(b_)[:, kc * 128:(kc + 1) * 128], mixed[:, i, kc * 128:(kc + 1) * 128], ident[:, :]) for kc in range(8)],
                [("mixed", i), ("mixed_s", i), "ident"], [pk(b_)])
            cp("act", actT[:, :, i * 128:(i + 1) * 128], psb(b_)[:, :].rearrange("p (k t) -> p k t", k=8), [pk(b_)],
               AK(i))
        for i in range(NT):
            for nh in range(2):
                b_ = mm_rot.next()
                mm(psf(b_)[:, :], [(actT[:, kc, i * 128:(i + 1) * 128], w_out_sb[:, kc, nh * 512:(nh + 1) * 512]) for kc in range(8)],
                   AK(i) + ["w_out"], [pk(b_)])
                tb_, tk_ = [(tmpf, ["tmpf"]), (tokf, [("tokf", 0), ("tokf", 1)])][nh]
                tt("dve", tb_[:, 0:512], psf(b_)[:, :], modb[:, 0, nh * 512:(nh + 1) * 512], ALU.mult, [pk(b_), ("modb", 2)], tk_)
                tt("dve", x_sb[:, i, nh * 512:(nh + 1) * 512], x_sb[:, i, nh * 512:(nh + 1) * 512], tb_[:, 0:512], ALU.add,
                   tk_ + [("x", i)], [("x", i)])

    def mlp(job, l):
        nm = job["name"]
        P = f"ml{l}{nm}_"
        aT = carve(0, [128, 32, 1024], BF16)
        w1b = [carve(65536 + i * 4096, [128, 8, 256], BF16) for i in range(2)]
        w2b = [carve(65536 + 8192 + i * 16384, [128, 32, 256], BF16) for i in range(2)]
        RK = P + "region"
        S.alias([RK], ["w_out", "w_in", UK] + MIX_KEYS + ATT_KEYS + list(job.get("ssm_keys", [])))
        for jb in range(16):
            bi = jb % 2
            S.dma("pool", w1b[bi], W["mlp_w1"].ap()[l, :, jb * 256:(jb + 1) * 256].rearrange("(k p) n -> p k n", p=128),
                  reads=[RK], writes=[(P + "w1", bi)])
            for hc in range(2):
                j = jb * 2 + hc
                for tb in range(2):
                    b_ = mm_rot.next()
                    mm(psf(b_)[:, :], [(w1b[bi][:, kc, hc * 128:(hc + 1) * 128], actT[:, kc, tb * 512:(tb + 1) * 512]) for kc in range(8)],
                       [(P + "w1", bi)] + [k_ for t in range(tb * 4, tb * 4 + 4) for k_ in AK(t)], [pk(b_)])
                    pi = pT_rot.next()
                    actf(pT[pi][:, :], psf(b_)[:, :], AF.Relu, [pk(b_)], [("pT", pi)])
                    tt("dve", aT[:, j, tb * 512:(tb + 1) * 512], pT[pi][:, :], pT[pi][:, :], ALU.mult, [("pT", pi), RK], [(P + "aT", tb)])
        for q in range(4):
            bi = q % 2
            S.dma("pool", w2b[bi], W["mlp_w2"].ap()[l, :, q * 256:(q + 1) * 256].rearrange("(j p) n -> p j n", p=128),
                  reads=[RK], writes=[(P + "w2", bi)])
            for i in range(NT):
                b_ = mm_rot.next()
                mm(psf(b_)[:, 0:256], [(aT[:, j, i * 128:(i + 1) * 128], w2b[bi][:, j, :]) for j in range(32)],
                   [(P + "aT", i // 4), (P + "w2", bi)], [pk(b_)])
                tb_, tk_ = [(tmpf, ["tmpf"]), (tokf, [("tokf", 0)])][i % 2]
                tt("dve", tb_[:, 0:256], psf(b_)[:, 0:256], modb[:, 0, q * 256:(q + 1) * 256], ALU.mult, [pk(b_), ("modb", 2)], tk_)
                tt("dve", x_sb[:, i, q * 256:(q + 1) * 256], x_sb[:, i, q * 256:(q + 1) * 256], tb_[:, 0:256], ALU.add,
                   tk_ + [("x", i)], [("x", i)])
        return [RK, (P + "aT", 0), (P + "aT", 1), (P + "w1", 0), (P + "w1", 1), (P + "w2", 0), (P + "w2", 1)]

    def run_job(job, prev_keys):
        S.alias(["w_in", UK] + ATT_KEYS + MIX_KEYS, prev_keys)
        for i in range(NT):
            S.dma("sp", x_sb[:, i, :], job["x_d"].ap()[i * 128:(i + 1) * 128, :], writes=[("x", i)])
        mlp_keys = None
        for l in range(2):
            if mlp_keys is not None:
                S.alias(["w_in", UK] + ATT_KEYS + MIX_KEYS, mlp_keys)
            S.dma("pool", w_in_sb, W["w_in"].ap()[l].rearrange("(k p) n -> p k n", p=128), writes=["w_in"])
            load_layer_params(l, job)
            load_mod(l, job["cond"], 0)
            for (t_, key) in ((dV, "dV"), (gV, "gV"), (mV, "mV")):
                for kt in range(12):
                    memset("pool", t_[:, kt, :, 64:65], 1.0, [(key, kt)])
            for i in range(NT):
                norm_mod_transpose(i, 0)
            if job["past"]:
                past_tiles(job, l)
            for i in range(NT):
                kt = (job["past"] // 128 + i) if job["past"] else i
                in_proj_tile(job, l, i, kt)
            if stop_after == "inproj":
                return
            S.alias(MIX_KEYS, ["w_in"])
            S.alias([f"sr{l}{job['name']}_region"], ["w_in", UK] + ATT_KEYS)
            attention(job, l)
            if stop_after == "attn":
                return
            ssm_keys = ssm_run(job, l)
            dbg(f"mixed{job['name']}{l}", mixed.rearrange("p t c -> p (t c)"), [128, NT * 1024], MIX_KEYS, q="pool")
            if stop_after == "ssm":
                return
            job["ssm_keys"] = ssm_keys
            out_proj(job, l, ssm_keys)
            load_mod(l, job["cond"], 1)
            for i in range(NT):
                norm_mod_transpose(i, 1)
            mlp_keys = mlp(job, l)
        S.alias([UK], mlp_keys)
        S.dma("sp", n_g, W["final_norm_g"].ap().unsqueeze(0).to_broadcast([128, D]), writes=[UK])
        for i in range(NT):
            xk = ("x", i)
            actf(junk[:], x_sb[:, i, :], AF.Square, [xk], [("st", 0)], accum_out=st_(0))
            rstd_chain(st_(0), st_(1), 1.0 / D, ("st", 0), ("st", 1))
            stt(tmpf[:], x_sb[:, i, :], st_(1), n_g, ALU.mult, ALU.mult, [xk, ("st", 1), UK], ["tmpf"])
            S.dma("sp", job["y_d"].ap()[i * 128:(i + 1) * 128, :], tmpf[:], reads=["tmpf"])
        return mlp_keys + [UK]

    jobP = dict(name="P", nseq=4, Ts=256, past=0, rope=False, cond=0, x_d=xp_d, y_d=yp_d, caches_out=True)
    jobS = dict(name="S", nseq=1, Ts=1024, past=512, rope=True, cond=1, x_d=xs_d, y_d=ys_d, caches_out=False)
    arena_setup_keys = [k for k in setup_keys if (isinstance(k, str) and k.startswith("sb")) or (isinstance(k, tuple) and k[0] == "wada")]
    kP = run_job(jobP, arena_setup_keys)
    if stop_after is None or stop_after == "all":
        run_job(jobS, kP)
    elif stop_after == "sample_inproj":
        pass
    st = S.emit()
    return nc, dbg_outs, st


_CACHE = {}


def _get_program():
    if "nc" not in _CACHE:
        nc, _, st = build_program()
        _CACHE["nc"] = nc
    return _CACHE["nc"]


def make_in_maps(inputs):
    f = lambda a: np.ascontiguousarray(np.asarray(a, dtype=np.float32))
    x_prompt = f(inputs["x_prompt"])
    x_sample = f(inputs["x_sample"])
    wnames = ["norm1_g", "norm2_g", "w_ada", "b_ada", "w_in", "w_out", "diff_lq1", "diff_lk1", "diff_lq2", "diff_lk2",
              "diff_subln_g", "gqa_qn_g", "gqa_kn_g", "ssm_log_dt", "ssm_d", "ssm_w_glu", "mla_qn_g", "mla_kvn_g",
              "mla_w_uq", "mla_w_ukv", "mlp_w1", "mlp_w2", "final_norm_g"]
    shared = {n: f(inputs[n]) for n in wnames}
    shared["ssm_a_re"] = f(inputs["ssm_a_re"]).reshape(2, 32, 64)
    shared["ssm_a_im"] = f(inputs["ssm_a_im"]).reshape(2, 32, 64)
    shared["ssm_log_dt"] = f(inputs["ssm_log_dt"]).reshape(2, 32)
    shared["ssm_b_re"] = f(inputs["ssm_b_re"]).reshape(2, 32, 1024)
    shared["ssm_b_im"] = f(inputs["ssm_b_im"]).reshape(2, 32, 1024)
    shared["ssm_c_re"] = f(inputs["ssm_c_re"]).reshape(2, 512, 64)
    shared["ssm_c_im"] = f(inputs["ssm_c_im"]).reshape(2, 512, 64)
    c = f(inputs["c"])
    c_ctx = f(inputs["c_ctx"])
    maps = []
    for core in range(8):
        b = core // 2
        m = dict(shared)
        m["xp"] = x_prompt[4 * core:4 * core + 4].reshape(1024, D)
        m["xs"] = x_sample[b]
        m["cvec"] = np.stack([c_ctx, c[b]], axis=0)
        m["cdk"] = f(inputs["cache_diff_k"])[b].reshape(2, 512, 256)
        m["cdv"] = f(inputs["cache_diff_v"])[b].reshape(2, 512, 256)
        m["cgk"] = f(inputs["cache_gqa_k"])[b].reshape(2, 512, 128)
        m["cgv"] = f(inputs["cache_gqa_v"])[b].reshape(2, 512, 128)
        m["cckv"] = f(inputs["cache_mla_ckv"])[b].reshape(2, 512, 128)
        m["ckr"] = f(inputs["cache_mla_krope"])[b].reshape(2, 512, 32)
        m["h0re"] = f(inputs["state_ssm_re"])[b].reshape(2, 32, 64)
        m["h0im"] = f(inputs["state_ssm_im"])[b].reshape(2, 32, 64)
        maps.append(m)
    return maps


def assemble(results):
    y_prompt = np.concatenate([r["yp"].reshape(4, 256, D) for r in results], axis=0)
    y_sample = np.stack([np.concatenate([results[2 * b]["ys"][0:512], results[2 * b + 1]["ys"][512:1024]], axis=0)
                         for b in range(4)], axis=0)
    cat = lambda k, shp: np.concatenate([r[k].reshape(shp) for r in results], axis=0)
    ndk = cat("ndk", (4, 2, 256, 4, 64))
    ndv = cat("ndv", (4, 2, 256, 4, 64))
    ngk = cat("ngk", (4, 2, 256, 2, 64))
    ngv = cat("ngv", (4, 2, 256, 2, 64))
    nckv = cat("nckv", (4, 2, 256, 128))
    nkr = cat("nkr", (4, 2, 256, 32))
    nsre = cat("nsre", (4, 2, 2, 16, 64))
    nsim = cat("nsim", (4, 2, 2, 16, 64))
    outs = (y_prompt, y_sample, ndk, ndv, ngk, ngv, nckv, nkr, nsre, nsim)
    return tuple(np.ascontiguousarray(o, dtype=np.float32) for o in outs)


def kernel(**inputs):
    nc = _get_program()
    in_maps = make_in_maps(inputs)
    res = run_bass_kernel_spmd(nc, in_maps, core_ids=list(range(8)))
    return assemble(res.results)
```

```python
import math
import numpy as np
import concourse.bass as bass
import concourse.mybir as mybir
from concourse.bass_utils import run_bass_kernel_spmd

F32 = mybir.dt.float32
BF16 = mybir.dt.bfloat16
I32 = mybir.dt.int32
AF = mybir.ActivationFunctionType
ALU = mybir.AluOpType
AX = mybir.AxisListType

ENGS = ("pe", "act", "dve", "pool", "sp")
D = 1024
NT = 8
EPS = 1e-6
TWO_PI = 2.0 * math.pi
IN_COLS = 1888
LAM_INIT = [0.8 - 0.6 * math.exp(-0.3 * l) for l in range(2)]


class Sched:
    def __init__(self, nc, n_dma_sems=24):
        self.nc = nc
        self.eng = dict(pe=nc.tensor, act=nc.scalar, dve=nc.vector, pool=nc.gpsimd, sp=nc.sync)
        self.ops = []
        self.last_w = {}
        self.readers = {}
        self.extra = {}
        self.window = 64
        self.reorder = True
        self.n_dma_sems = n_dma_sems

    def add(self, eng, fn, reads=(), writes=(), dma=False, cost=None, lat=0.0):
        idx = len(self.ops)
        writes = list(writes) + [k for k in reads if isinstance(k, tuple) and k and k[0] == "ps" and k not in writes]
        deps = set()
        for k in list(reads) + list(writes):
            deps |= self.extra.get(k, set())
        for k in reads:
            w = self.last_w.get(k)
            if w is not None:
                deps.add(w)
        for k in writes:
            w = self.last_w.get(k)
            if w is not None:
                deps.add(w)
            for r in self.readers.get(k, ()):
                deps.add(r)
        for k in reads:
            self.readers.setdefault(k, []).append(idx)
        for k in writes:
            self.last_w[k] = idx
            self.readers[k] = []
        deps.discard(idx)
        if cost is None:
            cost = {'pe': 0.3, 'act': 0.4, 'dve': 0.4, 'pool': 0.6, 'sp': 0.1}[eng]
        self.ops.append(dict(eng=eng, fn=fn, deps=deps, dma=dma, sig=False, cost=cost, lat=lat))
        return idx

    def alias(self, new_keys, old_keys):
        acc = set()
        for k in old_keys:
            w = self.last_w.get(k)
            if w is not None:
                acc.add(w)
            acc.update(self.readers.get(k, ()))
        for k in new_keys:
            self.extra[k] = self.extra.get(k, set()) | acc

    def dma(self, q, out, in_, reads=(), writes=(), **kw):
        n = 1
        for s_ in out.shape:
            n *= s_
        nbytes = n * 4
        cost = 0.15 if q == "sp" else 0.8
        return self.add(q, lambda h: h.dma_start(out=out, in_=in_, **kw), reads, writes, dma=True, cost=cost,
                        lat=2.0 + nbytes / 150e3)

    def schedule(self):
        import heapq
        ops = self.ops
        n = len(ops)
        succ = [[] for _ in range(n)]
        indeg = [0] * n
        for i, op in enumerate(ops):
            indeg[i] = len(op["deps"])
            for j in op["deps"]:
                succ[j].append(i)
        ready_t = [0.0] * n
        fin = [0.0] * n
        bl = [0.0] * n
        for i in range(n - 1, -1, -1):
            m = 0.0
            for k in succ[i]:
                if bl[k] > m:
                    m = bl[k]
            bl[i] = ops[i]["cost"] + ops[i]["lat"] + 0.07 + m
        heaps = {e: [] for e in ENGS}
        free = {e: 0.0 for e in ENGS}
        for i in range(n):
            if indeg[i] == 0:
                heapq.heappush(heaps[ops[i]["eng"]], (0.0, i))
        order = []
        WINDOW = self.window
        while len(order) < n:
            best = None
            for e in ENGS:
                h = heaps[e]
                if not h:
                    continue
                rt, i = h[0]
                start = max(rt, free[e])
                cand = (start, i, e)
                if best is None or cand < best:
                    best = cand
            start, i, e = best
            h = heaps[e]
            pool_ = []
            while h and h[0][0] <= start and len(pool_) < WINDOW:
                pool_.append(heapq.heappop(h))
            pick = max(pool_, key=lambda t: (bl[t[1]], -t[1]))
            for t in pool_:
                if t is not pick:
                    heapq.heappush(h, t)
            i = pick[1]
            op = ops[i]
            st_ = max(pick[0], free[e])
            op["t0"] = st_
            free[e] = st_ + op["cost"]
            fin[i] = st_ + op["cost"] + op["lat"]
            order.append(i)
            for k in succ[i]:
                same = (ops[k]["eng"] == e)
                t_ = fin[i] + (0.05 if same and e == "pe" else 0.07)
                if t_ > ready_t[k]:
                    ready_t[k] = t_
                indeg[k] -= 1
                if indeg[k] == 0:
                    heapq.heappush(heaps[ops[k]["eng"]], (ready_t[k], k))
        self.est_time = max(fin) if fin else 0.0
        return order

    def emit(self):
        nc = self.nc
        ops = self.ops
        for i, op in enumerate(ops):
            for j in op["deps"]:
                pj = ops[j]
                if pj["dma"]:
                    continue
                if pj["eng"] == "pe" and op["eng"] == "pe" and not op["dma"]:
                    continue
                pj["sig"] = True
        esem = {e: nc.alloc_semaphore(f"es_{e}") for e in ENGS}
        nd = self.n_dma_sems
        dsem = {q: [nc.alloc_semaphore(f"ds_{q}_{i}") for i in range(nd)] for q in ("sp", "pool", "act")}
        ecount = {e: 0 for e in ENGS}
        dcount = {q: [0] * nd for q in dsem}
        known = {e: {} for e in ENGS}
        dma_i = {q: 0 for q in dsem}
        nwaits = 0
        final = {}
        order = self.schedule() if self.reorder else list(range(len(ops)))
        for i in order:
            op = ops[i]
            e = op["eng"]
            h = self.eng[e]
            waits = {}
            for j in op["deps"]:
                pj = ops[j]
                if pj["dma"]:
                    key, sem, val = ("d",) + pj["dslot"], dsem[pj["dslot"][0]][pj["dslot"][1]], pj["dval"]
                elif pj["eng"] == "pe" and e == "pe" and not op["dma"]:
                    continue
                else:
                    key, sem, val = ("e", pj["eng"]), esem[pj["eng"]], pj["sigval"]
                if key not in waits or waits[key][1] < val:
                    waits[key] = (sem, val)
            if op["dma"]:
                s = dma_i[e] % nd
                dma_i[e] += 1
                if dcount[e][s] > 0:
                    key = ("d", e, s)
                    if key not in waits or waits[key][1] < dcount[e][s]:
                        waits[key] = (dsem[e][s], dcount[e][s])
            for key, (sem, val) in waits.items():
                if known[e].get(key, 0) >= val:
                    continue
                h.wait_ge(sem, val)
                known[e][key] = val
                nwaits += 1
            ins = op["fn"](h)
            if op["dma"]:
                dcount[e][s] += 16
                ins.then_inc(dsem[e][s], 16)
                op["dslot"] = (e, s)
                op["dval"] = dcount[e][s]
                final[("d", e, s)] = (dsem[e][s], dcount[e][s])
            elif op["sig"]:
                ecount[e] += 1
                ins.then_inc(esem[e], 1)
                op["sigval"] = ecount[e]
        h = self.eng["sp"]
        for key, (sem, val) in final.items():
            if known["sp"].get(key, 0) >= val:
                continue
            h.wait_ge(sem, val)
        return dict(n_ops=len(ops), n_waits=nwaits, counts=dict(ecount))


class Rot:
    def __init__(self, items):
        self.items = list(items)
        self.i = 0

    def next(self):
        v = self.items[self.i % len(self.items)]
        self.i += 1
        return v


class _Stop(Exception):
    pass


def build_program(debug=None, stop_after=None):
    try:
        return _build_program(debug, stop_after)
    except _Stop as e:
        return e.args[0]


def _build_program(debug=None, stop_after=None):
    nc = bass.Bass("TRN2", target_bir_lowering=False)
    S = Sched(nc)
    dbg_outs = {}

    def din(name, shape):
        return nc.dram_tensor(name, list(shape), F32, kind="ExternalInput")

    def dout(name, shape):
        return nc.dram_tensor(name, list(shape), F32, kind="ExternalOutput")

    xp_d = din("xp", [1024, D])
    xs_d = din("xs", [1024, D])
    cvec_d = din("cvec", [2, D])
    cdk_d = din("cdk", [2, 512, 256])
    cdv_d = din("cdv", [2, 512, 256])
    cgk_d = din("cgk", [2, 512, 128])
    cgv_d = din("cgv", [2, 512, 128])
    cckv_d = din("cckv", [2, 512, 128])
    ckr_d = din("ckr", [2, 512, 32])
    h0re_d = din("h0re", [2, 32, 64])
    h0im_d = din("h0im", [2, 32, 64])
    W = {}
    for name, shape in [
        ("norm1_g", [2, D]), ("norm2_g", [2, D]), ("w_ada", [2, D, 6 * D]), ("b_ada", [2, 6 * D]),
        ("w_in", [2, D, IN_COLS]), ("w_out", [2, D, D]),
        ("diff_lq1", [2, 32]), ("diff_lk1", [2, 32]), ("diff_lq2", [2, 32]), ("diff_lk2", [2, 32]),
        ("diff_subln_g", [2, 64]), ("gqa_qn_g", [2, 64]), ("gqa_kn_g", [2, 64]),
        ("ssm_a_re", [2, 32, 64]), ("ssm_a_im", [2, 32, 64]), ("ssm_log_dt", [2, 32]),
        ("ssm_b_re", [2, 32, 1024]), ("ssm_b_im", [2, 32, 1024]),
        ("ssm_c_re", [2, 512, 64]), ("ssm_c_im", [2, 512, 64]),
        ("ssm_d", [2, 256]), ("ssm_w_glu", [2, 256, 512]),
        ("mla_qn_g", [2, 192]), ("mla_kvn_g", [2, 128]),
        ("mla_w_uq", [2, 192, 384]), ("mla_w_ukv", [2, 128, 512]),
        ("mlp_w1", [2, D, 4 * D]), ("mlp_w2", [2, 4 * D, D]), ("final_norm_g", [D]),
    ]:
        W[name] = din(name, shape)
    yp_d = dout("yp", [1024, D])
    ys_d = dout("ys", [1024, D])
    ndk_d = dout("ndk", [4, 2, 256, 256])
    ndv_d = dout("ndv", [4, 2, 256, 256])
    ngk_d = dout("ngk", [4, 2, 256, 128])
    ngv_d = dout("ngv", [4, 2, 256, 128])
    nckv_d = dout("nckv", [4, 2, 256, 128])
    nkr_d = dout("nkr", [4, 2, 256, 32])
    nsre_d = dout("nsre", [4, 2, 32, 64])
    nsim_d = dout("nsim", [4, 2, 32, 64])
    modscr = nc.dram_tensor("modscr", [2, 2, 6 * D], F32)
    ssm_scr_b = nc.dram_tensor("ssm_scr_b", [2, 4, 128, 32 * 128], BF16)
    ssm_scr_f = nc.dram_tensor("ssm_scr_f", [2, 2, 128, 32 * 128], F32)
    ssm_scr_r = nc.dram_tensor("ssm_scr_r", [2, 128, 32], F32)
    ssm_scr_c1 = nc.dram_tensor("ssm_scr_c1", [2, 2, 128, 32], F32)

    def sb(name, shape, dt=F32):
        return nc.alloc_sbuf_tensor(name, list(shape), dt)

    PS = [nc.alloc_psum_tensor(f"ps{b}", [128, 512], F32) for b in range(8)]
    mm_rot = Rot([0, 1, 2, 3])
    tr_rot = Rot([4, 5])
    aux_rot = Rot([6, 7])

    def psf(b):
        return PS[b][:, :]

    def psb(b):
        return PS[b][:, :].bitcast(BF16)

    def pk(b):
        return ("ps", b)

    x_sb = sb("x_sb", [128, NT, D])
    modb = sb("modb", [128, 1, D])
    modcol = sb("modcol", [128, 2, 3, 8])
    hb2 = sb("hb2", [128, D], BF16)
    actT = sb("actT", [128, 8, 1024], BF16)
    tmpf = sb("tmpf", [128, D])
    hb = sb("hb", [128, D], BF16)
    junk = sb("junk", [128, D], BF16)
    stat = sb("stat", [128, 64])
    own = sb("own", [128, 1, 928])
    ident = sb("ident", [128, 128], BF16)
    identf = sb("identf", [128, 128])
    ARENA_BYTES = 104 * 1024
    arena = sb("arena", [128, ARENA_BYTES // 2], BF16)

    def carve(off, shape, dt):
        n = 1
        for s_ in shape[1:]:
            n *= s_
        esz = 2 if dt == BF16 else 4
        assert off % 4 == 0
        assert off + n * esz <= ARENA_BYTES, (off, n * esz)
        ap = arena[:, off // 2: off // 2 + n * esz // 2]
        if dt != BF16:
            ap = ap.bitcast(dt)
        if len(shape) == 2:
            return ap
        names = " ".join(f"d{i}" for i in range(len(shape) - 1))
        kw = {f"d{i}": shape[i + 1] for i in range(len(shape) - 1)}
        return ap.rearrange(f"p ({names}) -> p {names}", **kw)

    def fsz(ap):
        n = 1
        for s_ in ap.shape[1:]:
            n *= s_
        return n

    def ecost(eng, n):
        if eng == "dve":
            return 0.12 + n / 960.0
        if eng == "act":
            return 0.2 + n / 1200.0
        if eng == "pool":
            return 0.3 + n / 450.0
        return 0.3

    def E(eng, fn, reads=(), writes=(), cost=None):
        return S.add(eng, fn, reads, writes, cost=cost)

    def tt(eng, out, in0, in1, op, reads, writes):
        return E(eng, lambda h: h.tensor_tensor(out=out, in0=in0, in1=in1, op=op), reads, writes, cost=ecost(eng, fsz(out)))

    def ts(eng, out, in0, s1, s2, op0, op1, reads, writes):
        c = ecost(eng, fsz(out))
        if op1 is None:
            return E(eng, lambda h: h.tensor_scalar(out, in0, s1, None, op0=op0), reads, writes, cost=c)
        return E(eng, lambda h: h.tensor_scalar(out, in0, s1, s2, op0=op0, op1=op1), reads, writes, cost=c)

    def stt(out, in0, scalar, in1, op0, op1, reads, writes, eng="dve"):
        return E(eng, lambda h: h.scalar_tensor_tensor(out=out, in0=in0, scalar=scalar, in1=in1, op0=op0, op1=op1),
                 reads, writes, cost=ecost(eng, fsz(out)))

    def actf(out, in_, func, reads, writes, scale=None, bias=None, accum_out=None):
        kw = {}
        if scale is not None:
            kw["scale"] = scale
        if bias is not None:
            kw["bias"] = bias
        if accum_out is not None:
            kw["accum_out"] = accum_out
        return E("act", lambda h: h.activation(out=out, in_=in_, func=func, **kw), reads, writes,
                 cost=ecost("act", fsz(out)) + (0.1 if accum_out is not None else 0.0))

    def cp(eng, out, in_, reads, writes):
        c = ecost(eng, fsz(out))
        if eng == "act":
            return E("act", lambda h: h.copy(out, in_), reads, writes, cost=c)
        return E(eng, lambda h: h.tensor_copy(out, in_), reads, writes, cost=c)

    def mmcost(l, r):
        nn = fsz(r)
        f32 = (r.dtype == F32)
        return 0.035 + (nn * (4.0 if f32 else 1.0)) / 2400.0

    def mm(out, pairs, reads, writes):
        def fn(h):
            ins = None
            n = len(pairs)
            for i, (l, r) in enumerate(pairs):
                ins = h.matmul(out, lhsT=l, rhs=r, start=(i == 0), stop=(i == n - 1))
            return ins
        return E("pe", fn, reads, writes, cost=sum(mmcost(l, r) for (l, r) in pairs) + 0.1)

    def mm1(out, l, r, start, stop, reads, writes, skip=False):
        return E("pe", lambda h: h.matmul(out, lhsT=l, rhs=r, start=start, stop=stop, skip_group_check=skip), reads, writes,
                 cost=mmcost(l, r) + 0.1)

    def trs(items, reads, writes):
        def fn(h):
            ins = None
            for (o, i_, idn) in items:
                ins = h.transpose(o, i_, idn)
            return ins
        return E("pe", fn, reads, writes, cost=0.1 + sum(0.06 + (fsz(i_) if False else 128) * (2.0 if i_.dtype == F32 else 1.0) / 2400.0 for (o, i_, idn) in items))

    def memset(eng, ap, val, writes):
        return E(eng, lambda h: h.memset(ap, val), (), writes, cost=ecost(eng, fsz(ap)))

    def dbg(name, ap, shape, reads, q="sp"):
        if debug is None or name not in debug:
            return
        t = dout("dbg_" + name, shape)
        S.dma(q, t.ap(), ap, reads=reads)
        dbg_outs[name] = t

    _uid = [0]

    def uid(p):
        _uid[0] += 1
        return f"{p}{_uid[0]}"

    def rstd_chain(src, dst, mul, rk, wk):
        ts("dve", dst, src, mul, EPS, ALU.mult, ALU.add, [rk], [wk])
        actf(dst, dst, AF.Sqrt, [wk], [wk])
        E("dve", lambda h: h.reciprocal(dst, dst), [wk], [wk])

    MAGIC = 12582912.0

    def range_reduce(eng, ap, key, itmp, ikey, ftmp, fkey):
        actf(ftmp, ap, AF.Identity, [key], [fkey], scale=1.0 / TWO_PI, bias=MAGIC)
        actf(ftmp, ftmp, AF.Identity, [fkey], [fkey], bias=-MAGIC)
        stt(ap, ftmp, -TWO_PI, ap, ALU.mult, ALU.add, [fkey, key], [key], eng=eng)
        ts(eng, ap, ap, 3.1415925, -3.1415925, ALU.min, ALU.max, [key], [key])

    memset("pool", identf[:], 0.0, ["identf"])
    E("pool", lambda h: h.affine_select(out=identf[:], in_=identf[:], compare_op=ALU.not_equal, fill=1.0,
                                        base=0, pattern=[[-1, 128]], channel_multiplier=1), ["identf"], ["identf"])
    cp("dve", ident[:], identf[:], ["identf"], ["ident"])
    rpad = sb("rpad", [128, 8, 240], BF16)
    rpadf = sb("rpadf", [128, 16])
    memset("pool", rpad[:], 0.0, ["rpad"])
    for b in range(8):
        memset("pool", rpadf[:], 0.0, ["rpadf"])
        E("pool", lambda h, b=b: h.affine_select(out=rpadf[:], in_=rpadf[:], compare_op=ALU.not_equal, fill=1.0,
                                                 base=-16 * b, pattern=[[-1, 16]], channel_multiplier=1),
          ["rpadf"], ["rpadf"])
        cp("pool", rpad[:, b, 112:128], rpadf[:], ["rpadf"], ["rpad"])
    j2 = sb("j2", [128, 128])
    memset("pool", j2[:], 0.0, ["j2"])
    E("pool", lambda h: h.affine_select(out=j2[:, 64:128], in_=j2[:, 64:128], compare_op=ALU.not_equal, fill=1.0,
                                        base=0, pattern=[[-1, 64]], channel_multiplier=1), ["j2"], ["j2"])
    E("pool", lambda h: h.affine_select(out=j2[:, 0:64], in_=j2[:, 0:64], compare_op=ALU.not_equal, fill=1.0,
                                        base=-64, pattern=[[-1, 64]], channel_multiplier=1), ["j2"], ["j2"])
    epad = sb("epad", [128, 192])
    memset("pool", epad[:], 0.0, ["epad"])
    E("pool", lambda h: h.affine_select(out=epad[:, 64:128], in_=epad[:, 64:128], compare_op=ALU.not_equal, fill=1.0,
                                        base=0, pattern=[[-1, 64]], channel_multiplier=1), ["epad"], ["epad"])
    pidx_i = sb("pidx_i", [128, 4], I32)
    pidx_f = sb("pidx_f", [128, 8])
    E("pool", lambda h: h.iota(pidx_i[:, 0:1], pattern=[[0, 1]], base=0, channel_multiplier=1), (), ["pidx_i"])
    E("dve", lambda h: h.tensor_single_scalar(pidx_i[:, 1:2], pidx_i[:, 0:1], 63, ALU.bitwise_and), ["pidx_i"], ["pidx_i1"])
    E("dve", lambda h: h.tensor_single_scalar(pidx_i[:, 2:3], pidx_i[:, 0:1], 6, ALU.arith_shift_right), ["pidx_i"], ["pidx_i2"])
    E("dve", lambda h: h.tensor_single_scalar(pidx_i[:, 3:4], pidx_i[:, 0:1], 4, ALU.arith_shift_right), ["pidx_i"], ["pidx_i3"])
    cp("dve", pidx_f[:, 0:4], pidx_i[:, 0:4], ["pidx_i", "pidx_i1", "pidx_i2", "pidx_i3"], ["pidx_f"])
    ts("dve", pidx_f[:, 4:5], pidx_f[:, 2:3], 2.0, -1.0, ALU.mult, ALU.add, ["pidx_f"], ["sgnv"])
    sgnv = pidx_f[:, 4:5]

    if stop_after == "const":
        S.dma("sp", ndk_d.ap()[0, 0, 0:128, 0:128], identf[:, :], reads=["identf"])
        S.dma("sp", ndk_d.ap()[0, 0, 128:256, 0:128], j2[:, :], reads=["j2"])
        S.dma("sp", ndv_d.ap()[0, 0, 0:128, 0:8], pidx_f[:, :], reads=["pidx_f", "sgnv"])
        st = S.emit()
        return nc, dbg_outs, st
    csT = sb("csT", [128, 8, 2], BF16)
    tokf = sb("tokf", [128, 1024])
    WADA_OFF = ARENA_BYTES - 16384

    def mod_phase():
        cs = x_sb[0:2, 0, :]
        S.dma("sp", cs, cvec_d.ap(), writes=[("x", 0)])
        actf(cs, cs, AF.Silu, [("x", 0)], [("x", 0)])
        bT = tr_rot.next()
        trs([(psf(bT)[:, 2 * kc:2 * kc + 2], cs[:, kc * 128:(kc + 1) * 128], identf[0:2, 0:2]) for kc in range(8)],
            [("x", 0), "identf"], [pk(bT)])
        cp("dve", csT[:], psf(bT)[:, 0:16].rearrange("p (k t) -> p k t", t=2), [pk(bT)], ["csT"])
        wada = [carve(WADA_OFF + i * 8192, [128, 8, 512], BF16) for i in range(2)]
        bada = tokf[0:2, :].rearrange("p (b n) -> p b n", b=2)
        modrow = x_sb[0:2, 4:6, 0:512]
        BK_ = [[("tokf", 0), ("tokf", 1)], [("tokf", 2), ("tokf", 3), ("tokf", 4)]]
        ci = 0
        for l in range(2):
            for nchunk in range(12):
                bi = ci % 2
                ci += 1
                wk = ("wada", bi)
                S.dma("pool", wada[bi], W["w_ada"].ap()[l, :, nchunk * 512:(nchunk + 1) * 512].rearrange("(k p) n -> p k n", p=128),
                      writes=[wk])
                S.dma("sp", bada[:, bi, :], W["b_ada"].ap()[l:l + 1, nchunk * 512:(nchunk + 1) * 512].to_broadcast([2, 512]),
                      writes=BK_[bi])
                b = mm_rot.next()
                mm(psf(b)[0:2, :], [(csT[:, kc, :], wada[bi][:, kc, :]) for kc in range(8)], ["csT", wk], [pk(b)])
                tt("dve", modrow[:, bi, :], psf(b)[0:2, :], bada[:, bi, :], ALU.add,
                   [pk(b)] + BK_[bi], [("x", 4 + bi)])
                S.dma("sp", modscr.ap()[l, :, nchunk * 512:(nchunk + 1) * 512], modrow[:, bi, :], reads=[("x", 4 + bi)],
                      writes=["modscr"])
                yield

    modgen = mod_phase()

    def mod_step(n=1):
        for _ in range(n):
            next(modgen, None)

    cos32 = sb("cos32", [128, NT, 32])
    sin32 = sb("sin32", [128, NT, 32])
    cos64 = sb("cos64", [128, NT, 64])
    sin64 = sb("sin64", [128, NT, 64])
    rt_i = x_sb[:, 1, 0:256].bitcast(I32)
    rt_f = x_sb[:, 2, 0:256]
    rt_a = x_sb[:, 3, 0:256]
    rowf = sb("rowf", [128, NT])
    rowi = sb("rowi", [128, NT], I32)
    E("pool", lambda h: h.iota(rowi[:], pattern=[[2, NT]], base=0, channel_multiplier=0), (), ["rowi"])
    cp("dve", rowf[:], rowi[:], ["rowi"], ["rowf"])
    ts("dve", rowf[:], rowf[:], pidx_f[:, 2:3], None, ALU.add, None, ["rowf", "pidx_f"], ["rowf"])
    for (n, cs_t, sn_t) in ((8, cos32, sin32), (16, cos64, sin64)):
        fr_i = sb(f"fr_i{n}", [128, n], I32)
        fr = sb(f"fr{n}", [128, n])
        E("pool", lambda h, fr_i=fr_i, n=n: h.iota(fr_i[:], pattern=[[1, n]], base=0, channel_multiplier=0), (), [f"fr_i{n}"])
        cp("dve", fr[:], fr_i[:], [f"fr_i{n}"], [f"fr{n}"])
        actf(fr[:], fr[:], AF.Exp, [f"fr{n}"], [f"fr{n}"], scale=-math.log(10000.0) / n)
        W4 = 4 * n
        for which in ("sin", "cos"):
            ang = rt_a[:, 0:NT * 2 * n].rearrange("p (t a n) -> p t a n", t=NT, a=2)
            key = ("x", 3)
            tt("dve", ang[:, :, 0, :], rowf[:].unsqueeze(2).to_broadcast([128, NT, n]),
               fr[:].unsqueeze(1).to_broadcast([128, NT, n]), ALU.mult, ["rowf", f"fr{n}"], [key])
            tt("dve", ang[:, :, 1, :], pidx_f[:, 1:2].unsqueeze(2).to_broadcast([128, NT, n]),
               fr[:].unsqueeze(1).to_broadcast([128, NT, n]), ALU.mult, ["pidx_f", f"fr{n}", key], [key])
            flat = rt_a[:, 0:NT * 2 * n]
            if which == "cos":
                ts("dve", flat, flat, math.pi / 2, None, ALU.add, None, [key], [key])
            range_reduce("dve", flat, key, rt_i[:, 0:NT * 2 * n], ("x", 1), rt_f[:, 0:NT * 2 * n], ("x", 2))
            actf(flat, flat, AF.Sin, [key], [key])
            if which == "cos":
                dst = cs_t[:].rearrange("p t (a two n) -> p t a two n", a=2, two=2)
                for two in range(2):
                    cp("dve", dst[:, :, :, two, :], ang, [key], [f"cos{W4}"])
            else:
                dst = sn_t[:].rearrange("p t (a two n) -> p t a two n", a=2, two=2)
                cp("dve", dst[:, :, :, 0, :], ang, [key], [f"sin{W4}"])
                ts("dve", dst[:, :, :, 1, :], ang, -1.0, None, ALU.mult, None, [key], [f"sin{W4}"])

    if stop_after == "rope":
        S.dma("sp", ndk_d.ap()[0, 0, 0:128, 0:256], cos32[:].rearrange("p t w -> p (t w)"), reads=["cos32"])
        S.dma("sp", ndk_d.ap()[0, 1, 0:128, 0:256], sin32[:].rearrange("p t w -> p (t w)"), reads=["sin32"])
        st = S.emit()
        return nc, dbg_outs, st
    if stop_after == "mod":
        mod_step(100)
        st = S.emit()
        return nc, dbg_outs, st
    def _dma_nc(h, out, in_):
        return h.dma_start(out=out, in_=in_, allow_slow_non_contiguous=True)

    def ckpt(name):
        if stop_after == name:
            st = S.emit()
            raise _Stop((nc, dbg_outs, st))

    def ssm_build(l):
        P = f"sb{l}_"
        off = [0]

        def cv(shape, dt=F32):
            n = 1
            for s_ in shape[1:]:
                n *= s_
            esz = 2 if dt == BF16 else 4
            o = off[0]
            off[0] += (n * esz + 3) // 4 * 4
            assert off[0] <= WADA_OFF, off[0]
            return carve(o, shape, dt)

        if l > 0:
            S.alias([P + "all"], [k for k in S.last_w.keys() if isinstance(k, str) and k.startswith(f"sb{l-1}_")])
        ALL = [P + "all"] if l > 0 else []
        aRI = cv([128, 2, 32])
        dtt = cv([128, 32]); adt = cv([128, 32]); th = cv([128, 32])
        pim = cv([128, 5, 32, 8]); pre = cv([128, 5, 32, 8])
        cf = cv([128, 8, 32]); r8 = cv([128, 32])
        Bb = cv([128, 2, 32, 16]); Cri = cv([128, 2, 32, 16])
        mask = cv([128, 2, 128]); dcol = cv([128, 16]); phi = cv([128, 32])
        R1 = off[0]
        ld = cv([32, 2, 128])
        e5i = cv([128, 5, 32, 8], I32); e5 = cv([128, 5, 32, 8])
        mag = cv([128, 5, 32, 8]); ang = cv([128, 5, 32, 8])
        NE = 5 * 32 * 8
        itmp = cv([128, NE], I32); ftmp = cv([128, NE])
        Bsb = cv([32, 2, 1024]); Bri = cv([128, 2, 32, 16]); t1 = cv([128, 32, 16]); t2 = cv([128, 32, 16])
        Csb = cv([128, 2, 4, 64])
        mski = cv([128, 128], I32); mskf = cv([128, 128])
        early_keys = [P + k for k in ("ld", "e5i", "e5", "mag", "ang", "itmp", "ftmp", "Bsb", "Bri", "t1", "t2", "Csb", "mski", "mskf")]

        for half in range(2):
            S.dma("sp", ld[0:32, 0, half * 64:(half + 1) * 64], W["ssm_a_re"].ap()[l], reads=ALL, writes=[P + "ld"])
            S.dma("sp", ld[0:32, 1, half * 64:(half + 1) * 64], W["ssm_a_im"].ap()[l], reads=ALL, writes=[P + "ld"])
        bT_ = tr_rot.next()
        trs([(psf(bT_)[:, ri * 32:(ri + 1) * 32], ld[0:32, ri, :], identf[0:32, 0:32]) for ri in range(2)],
            [P + "ld", "identf"], [pk(bT_)])
        cp("dve", aRI.rearrange("p r g -> p (r g)"), psf(bT_)[:, 0:64], [pk(bT_)] + ALL, [P + "aRI"])
        aRe = aRI[:, 0, :]
        aIm = aRI[:, 1, :]
        S.dma("sp", dtt, W["ssm_log_dt"].ap()[l:l + 1, :].to_broadcast([128, 32]), reads=ALL, writes=[P + "dt"])
        actf(dtt, dtt, AF.Exp, [P + "dt"], [P + "dt"])
        tt("dve", adt, aRe, dtt, ALU.mult, [P + "aRI", P + "dt"] + ALL, [P + "adt"])
        tt("dve", th, aIm, dtt, ALU.mult, [P + "aRI", P + "dt"] + ALL, [P + "th"])
        specs = [(0, 0, 0, -1), (0, 1, 0, 1), (1, 0, 0, 1), (1, 1, 0, -1),
                 (2, 0, 7, -1), (2, 1, 0, 1), (3, 0, 1, 1), (3, 1, 8, -1), (4, 0, 1, 0), (4, 1, 1, 0)]
        for (slot, d_, base, step) in specs:
            E("pool", lambda h, slot=slot, d_=d_, base=base, step=step:
              h.iota(e5i[:, slot, d_ * 16:(d_ + 1) * 16, :], pattern=[[0, 16], [step, 8]], base=base, channel_multiplier=0),
              ALL, [P + "e5i"])
        fl = lambda a: a.rearrange("p a g s -> p (a g s)")
        cp("dve", fl(e5), fl(e5i), [P + "e5i"] + ALL, [P + "e5"])
        for a in range(5):
            tt("dve", mag[:, a], adt.unsqueeze(2).to_broadcast([128, 32, 8]), e5[:, a], ALU.mult,
               [P + "adt", P + "e5"] + ALL, [P + "mag"])
            tt("dve", ang[:, a], th.unsqueeze(2).to_broadcast([128, 32, 8]), e5[:, a], ALU.mult,
               [P + "th", P + "e5"] + ALL, [P + "ang"])
        actf(fl(mag), fl(mag), AF.Exp, [P + "mag"], [P + "mag"])
        ts("dve", fl(pre), fl(ang), math.pi / 2, None, ALU.add, None, [P + "ang"] + ALL, [P + "pre"])
        range_reduce("dve", fl(ang), P + "ang", itmp, P + "itmp", ftmp, P + "ftmp")
        actf(fl(pim), fl(ang), AF.Sin, [P + "ang"] + ALL, [P + "pim"])
        range_reduce("dve", fl(pre), P + "pre", itmp, P + "itmp", ftmp, P + "ftmp")
        actf(fl(pre), fl(pre), AF.Sin, [P + "pre"], [P + "pre"])
        tt("dve", fl(pim), fl(pim), fl(mag), ALU.mult, [P + "pim", P + "mag"], [P + "pim"])
        tt("dve", fl(pre), fl(pre), fl(mag), ALU.mult, [P + "pre", P + "mag"], [P + "pre"])
        dbg(f"pow_re{l}", fl(pre), [128, NE], [P + "pre"])
        dbg(f"pow_im{l}", fl(pim), [128, NE], [P + "pim"])
        ckpt("sb1")
        mod_step(2)
        abr = pre[:, 4, :, 0]
        abi = pim[:, 4, :, 0]
        K = P + "cf"
        ts("dve", cf[:, 0], abr, -1.0, None, ALU.add, None, [P + "pre"] + ALL, [K + "0"])
        tt("dve", cf[:, 1], cf[:, 0], aRe, ALU.mult, [K + "0", P + "aRI"] + ALL, [K + "1"])
        tt("dve", cf[:, 2], abi, aIm, ALU.mult, [P + "pim", P + "aRI"] + ALL, [K + "2"])
        tt("dve", cf[:, 1], cf[:, 1], cf[:, 2], ALU.add, [K + "1", K + "2"], [K + "1"])
        tt("dve", cf[:, 2], abi, aRe, ALU.mult, [P + "pim", P + "aRI", K + "1"], [K + "2"])
        tt("dve", cf[:, 3], cf[:, 0], aIm, ALU.mult, [K + "0", P + "aRI"] + ALL, [K + "3"])
        tt("dve", cf[:, 2], cf[:, 2], cf[:, 3], ALU.subtract, [K + "2", K + "3"], [K + "2"])
        tt("dve", cf[:, 3], aRe, aRe, ALU.mult, [P + "aRI", K + "2"], [K + "3"])
        tt("dve", cf[:, 4], aIm, aIm, ALU.mult, [P + "aRI"] + ALL, [K + "4"])
        tt("dve", cf[:, 3], cf[:, 3], cf[:, 4], ALU.add, [K + "3", K + "4"], [K + "3"])
        E("dve", lambda h: h.reciprocal(cf[:, 3], cf[:, 3]), [K + "3"], [K + "3"])
        tt("dve", cf[:, 5], cf[:, 1], cf[:, 3], ALU.mult, [K + "1", K + "3"] + ALL, [K + "5"])
        tt("dve", cf[:, 6], cf[:, 2], cf[:, 3], ALU.mult, [K + "2", K + "3"] + ALL, [K + "6"])
        actf(r8, adt, AF.Exp, [P + "adt"] + ALL, [P + "r8"], scale=8.0)
        S.dma("sp", ssm_scr_r.ap()[l], r8, reads=[P + "r8"], writes=[("scr_r", l)])
        ckpt("sb2")
        S.dma("sp", Bsb[0:32, 0, :], W["ssm_b_re"].ap()[l], reads=ALL, writes=[P + "Bsb"])
        S.dma("sp", Bsb[0:32, 1, :], W["ssm_b_im"].ap()[l], reads=ALL, writes=[P + "Bsb"])
        for ri in range(2):
            b_ = mm_rot.next()
            trs([(psf(b_)[0:64, c * 32:(c + 1) * 32], Bsb[0:32, ri, :].rearrange("g (p c) -> g c p", c=16)[:, c, :],
                  identf[0:32, 0:32]) for c in range(16)], [P + "Bsb", "identf"], [pk(b_)])
            cp("dve", Bri[0:64, ri].rearrange("p g c -> p c g"), psf(b_)[0:64, :].rearrange("p (c g) -> p c g", c=16),
               [pk(b_)] + ALL, [P + "Bri"])
        cre = cf[0:64, 5, :].unsqueeze(2).to_broadcast([64, 32, 16])
        cim = cf[0:64, 6, :].unsqueeze(2).to_broadcast([64, 32, 16])
        tt("dve", t1[0:64], cre, Bri[0:64, 0], ALU.mult, [K + "5", P + "Bri"] + ALL, [P + "t1"])
        tt("dve", t2[0:64], cim, Bri[0:64, 1], ALU.mult, [K + "6", P + "Bri"] + ALL, [P + "t2"])
        tt("dve", Bb[0:64, 0], t1[0:64], t2[0:64], ALU.subtract, [P + "t1", P + "t2"] + ALL, [P + "Bb0"])
        tt("dve", t1[0:64], cre, Bri[0:64, 1], ALU.mult, [K + "5", P + "Bri", P + "Bb0"], [P + "t1"])
        tt("dve", t2[0:64], cim, Bri[0:64, 0], ALU.mult, [K + "6", P + "Bri", P + "Bb0"], [P + "t2"])
        tt("dve", Bb[0:64, 1], t1[0:64], t2[0:64], ALU.add, [P + "t1", P + "t2"] + ALL, [P + "Bb1"])
        ckpt("sb3")
        S.dma("sp", Csb[:, 0], W["ssm_c_re"].ap()[l].rearrange("(j r) p -> r j p", r=128), reads=ALL, writes=[P + "Csb"])
        S.dma("sp", Csb[:, 1], W["ssm_c_im"].ap()[l].rearrange("(j r) p -> r j p", r=128), reads=ALL, writes=[P + "Csb"])
        for ri in range(2):
            b_ = mm_rot.next()
            trs([(psf(b_)[0:64, j * 128:(j + 1) * 128], Csb[:, ri, j, :], identf[:, :]) for j in range(4)],
                [P + "Csb", "identf"], [pk(b_)])
            cp("dve", Cri[0:64, ri].rearrange("p g c -> p (g c)"), psf(b_)[0:64, :], [pk(b_)] + ALL, [P + "Cri"])
        ckpt("sb4")
        E("pool", lambda h: h.iota(mski, pattern=[[1, 128]], base=0, channel_multiplier=0), ALL, [P + "mski"])
        E("dve", lambda h: h.tensor_single_scalar(mski, mski, 4, ALU.arith_shift_right), [P + "mski"], [P + "mski"])
        cp("dve", mskf, mski, [P + "mski"] + ALL, [P + "mskf"])
        ts("dve", mask[:, 0, :], mskf, pidx_f[:, 3:4], None, ALU.is_ge, None, [P + "mskf", "pidx_f"] + ALL, [P + "mask"])
        ts("dve", mask[:, 1, :], mskf, pidx_f[:, 3:4], None, ALU.is_le, None, [P + "mskf", "pidx_f"], [P + "mask"])
        for s_ in range(8):
            S.add("sp", lambda h, s_=s_: _dma_nc(h, dcol[s_ * 16:(s_ + 1) * 16, :],
                                                W["ssm_d"].ap()[l].rearrange("(g c) -> c g", c=16)),
                  ALL, [P + "dcol"], dma=True)
        ckpt("sb5")
        ts("dve", phi, th, 8.0, None, ALU.mult, None, [P + "th"] + ALL, [P + "phi"])
        range_reduce("dve", phi, P + "phi", itmp[:, 0:32], P + "itmp", ftmp[:, 0:32], P + "ftmp")
        mod_step(2)
        ckpt("sb6")
        S.alias([P + "late"], early_keys)
        LATE = [P + "late"]
        off[0] = R1
        GB = 8
        Mst = cv([128, GB, 128], BF16); PSst = cv([128, GB, 128], BF16)
        PSWst = cv([128, GB, 128], BF16); Qst = cv([128, GB, 128], BF16)
        Xre = cv([128, GB, 8, 16]); nXim = cv([128, GB, 8, 16]); Zre = cv([128, GB, 8, 16]); Zim = cv([128, GB, 8, 16])
        Pre = cv([128, GB, 8, 16]); Pim = cv([128, GB, 8, 16]); Qre = cv([128, GB, 8, 16]); nQim = cv([128, GB, 8, 16])
        ta = cv([128, GB, 8, 16]); tb = cv([128, GB, 8, 16])
        mtmp = cv([128, 128])
        kidx_i = cv([128, 128], I32); kidx = cv([128, 128])
        tab = cv([128, 8, 128]); tabo = cv([128, 8, 128])
        it2 = cv([128, 1024], I32); ft2 = cv([128, 1024])

        def cmul(slot, mat, mkeys, blk, out_re, out_im, neg_im, kout):
            g0, g1 = blk * GB, blk * GB + GB
            shp = [64, GB, 8, 16]
            pr = pre[0:64, slot, g0:g1, :].unsqueeze(3).to_broadcast(shp)
            pi = pim[0:64, slot, g0:g1, :].unsqueeze(3).to_broadcast(shp)
            mr = mat[0:64, 0, g0:g1, :].unsqueeze(2).to_broadcast(shp)
            mi = mat[0:64, 1, g0:g1, :].unsqueeze(2).to_broadcast(shp)
            rk = [P + "pre", P + "pim"] + mkeys + LATE
            tt("dve", ta[0:64], pr, mr, ALU.mult, rk, [P + "ta"])
            tt("pool", tb[0:64], pi, mi, ALU.mult, rk, [P + "tb"])
            tt("dve", out_re[0:64], ta[0:64], tb[0:64], ALU.subtract, [P + "ta", P + "tb"] + LATE, [kout + "r"])
            tt("dve", ta[0:64], pr, mi, ALU.mult, rk, [P + "ta"])
            tt("pool", tb[0:64], pi, mr, ALU.mult, rk, [P + "tb"])
            if neg_im:
                stt(out_im[0:64], ta[0:64], -1.0, tb[0:64], ALU.mult, ALU.subtract, [P + "ta", P + "tb"] + LATE, [kout + "i"])
            else:
                tt("dve", out_im[0:64], ta[0:64], tb[0:64], ALU.add, [P + "ta", P + "tb"] + LATE, [kout + "i"])

        BK = [P + "Bb0", P + "Bb1"]
        CK = [P + "Cri"]
        for blk in range(32 // GB):
            d_ = (blk * GB) // 16
            cmul(0, Bb, BK, blk, Xre, nXim, True, P + "X")
            cmul(1, Cri, CK, blk, Zre, Zim, False, P + "Z")
            cmul(2, Bb, BK, blk, Pre, Pim, False, P + "P")
            cmul(3, Cri, CK, blk, Qre, nQim, True, P + "Q")
            ckpt("sb6a")
            for gi in range(GB):
                g = (blk * GB + gi) % 16
                f = lambda a, gi=gi: a[0:64, gi].rearrange("p s c -> p (s c)")
                b_ = mm_rot.next()
                mm(psf(b_)[:, 0:128], [(f(Xre), f(Zre)), (f(nXim), f(Zim))],
                   [P + "Xr", P + "Xi", P + "Zr", P + "Zi"], [pk(b_)])
                if d_ == 0:
                    tt("dve", mtmp, psf(b_)[:, 0:128], mask[:, 0, :], ALU.mult, [pk(b_), P + "mask"] + LATE, [P + "mtmp"])
                    stt(Mst[:, gi, :], identf[:, :], dcol[:, g:g + 1], mtmp, ALU.mult, ALU.add,
                        [P + "mtmp", P + "dcol", "identf"] + LATE, [P + "Mst"])
                else:
                    tt("dve", Mst[:, gi, :], psf(b_)[:, 0:128], mask[:, 1, :], ALU.mult, [pk(b_), P + "mask"] + LATE, [P + "Mst"])
                ckpt("sb6b")
                b_ = mm_rot.next()
                trs([(psf(b_)[:, 0:64], f(Pre), identf[0:64, 0:64]), (psf(b_)[:, 64:128], f(Pim), identf[0:64, 0:64])],
                    [P + "Pr", P + "Pi", "identf"], [pk(b_)])
                cp("act", PSst[:, gi, :], psf(b_)[:, 0:128], [pk(b_)] + LATE, [P + "PSst"])
                cp("dve", PSWst[:, gi, :].rearrange("p (r q) -> p r q", r=2),
                   psf(b_)[:, 0:128].rearrange("p (r q) -> p r q", r=2)[:, ::-1, :], [pk(b_)] + LATE, [P + "PSWst"])
                ckpt("sb6c")
                b_ = mm_rot.next()
                mm(psf(b_)[:, 0:128], [(epad[0:64, 64:192], f(Qre)), (epad[0:64, 0:128], f(nQim))],
                   [P + "Qr", P + "Qi", "epad"], [pk(b_)])
                cp("act", Qst[:, gi, :], psf(b_)[:, 0:128], [pk(b_)] + LATE, [P + "Qst"])
                ckpt("sb6d")
            for idx, st_ in enumerate((Mst, PSst, PSWst, Qst)):
                S.dma("sp", ssm_scr_b.ap()[l, idx, :, blk * GB * 128:(blk + 1) * GB * 128], st_.rearrange("p g m -> p (g m)"),
                      reads=[P + ["Mst", "PSst", "PSWst", "Qst"][idx]], writes=[("scr_b", l)])
            ckpt("sb6e%d" % blk)
            mod_step(1)
        ckpt("sb7")
        E("pool", lambda h: h.iota(kidx_i, pattern=[[1, 128]], base=0, channel_multiplier=0), LATE, [P + "kidx_i"])
        cp("dve", kidx, kidx_i, [P + "kidx_i"] + LATE, [P + "kidx"])
        for blk in range(4):
            for which in range(2):
                tk = P + "tab"
                tt("dve", tab, phi[:, blk * 8:(blk + 1) * 8].unsqueeze(2).to_broadcast([128, 8, 128]),
                   kidx.unsqueeze(1).to_broadcast([128, 8, 128]), ALU.mult, [P + "phi", P + "kidx"] + LATE, [tk])
                tv = tab.rearrange("p g k -> p (g k)")
                tvo = tabo.rearrange("p g k -> p (g k)")
                if which == 0:
                    ts("dve", tv, tv, math.pi / 2, None, ALU.add, None, [tk], [tk])
                range_reduce("dve", tv, tk, it2, P + "it2", ft2, P + "ft2")
                if which == 0:
                    actf(tvo, tv, AF.Sin, [tk] + LATE, [P + "tabo"])
                else:
                    actf(tv, tv, AF.Sin, [tk], [tk])
                    ts("dve", tvo, tv, sgnv, None, ALU.mult, None, [tk, "sgnv"] + LATE, [P + "tabo"])
                S.dma("sp", ssm_scr_f.ap()[l, which, :, blk * 1024:(blk + 1) * 1024], tvo,
                      reads=[P + "tabo"], writes=[("scr_f", l)])
                S.add("sp", lambda h, blk=blk, which=which: _dma_nc(h, ssm_scr_c1.ap()[l, which, :, blk * 8:(blk + 1) * 8], tabo[:, :, 1]),
                      [P + "tabo"], [("scr_c1", l)], dma=True, cost=0.15, lat=2.5)
            mod_step(1)

    for l in range(2):
        ssm_build(l)
    mod_step(100)
    setup_keys = list(S.last_w.keys())

    if stop_after == "setup":
        st = S.emit()
        return nc, dbg_outs, st

    A_WIN = 0
    o = 30208
    dqT = carve(o, [128, 4, 1024], BF16); o += 4 * 1024 * 2
    dkT = carve(o, [128, 4, 1536], BF16); o += 4 * 1536 * 2
    dV = carve(o, [128, 12, 4, 65], BF16); o += 12 * 4 * 65 * 2
    o = (o + 3) // 4 * 4
    gqT = carve(o, [128, 4, 1024], BF16); o += 4 * 1024 * 2
    gkT = carve(o, [128, 2, 1536], BF16); o += 2 * 1536 * 2
    gV = carve(o, [128, 12, 2, 65], BF16); o += 12 * 2 * 65 * 2
    o = (o + 3) // 4 * 4
    mqT = carve(o, [128, 4, 1024], BF16); o += 4 * 1024 * 2
    mkT = carve(o, [128, 4, 1536], BF16); o += 4 * 1536 * 2
    mV = carve(o, [128, 12, 4, 65], BF16); o += 12 * 4 * 65 * 2
    o = (o + 3) // 4 * 4
    A_ATT_END = o
    uT = carve(o, [128, 2, 1024], BF16)
    cache_sb = carve(o, [128, 2, 928], BF16)
    n_g = carve(o, [128, D], F32)
    o += 4096
    assert o <= ARENA_BYTES, o
    mixed = carve(0, [128, NT, 1024], BF16)
    w_in_sb = carve(A_WIN, [128, 8, IN_COLS], BF16)
    UK = "uslot"
    ATT_KEYS = ([("dqT", t) for t in range(NT)] + [("dkT", t) for t in range(12)] + [("dV", t) for t in range(12)] +
                [("gqT", t) for t in range(NT)] + [("gkT", t) for t in range(12)] + [("gV", t) for t in range(12)] +
                [("mqT", t) for t in range(NT)] + [("mkT", t) for t in range(12)] + [("mV", t) for t in range(12)])
    MIX_KEYS = [("mixed", t) for t in range(NT)] + [("mixed_s", t) for t in range(NT)]
    TOKF = [("tokf", k) for k in range(5)]
    HB = [("hb", 0), ("hbq", 0), ("hbq", 1), ("hbk", 0), ("hb", 4)]

    gq_g = sb("gq_g", [128, 64]); gk_g = sb("gk_g", [128, 64]); sub_g = sb("sub_g", [128, 64])
    mq_g = sb("mq_g", [128, 192]); mkv_g = sb("mkv_g", [128, 128])
    lamt = sb("lamt", [128, 4, 32]); lams = sb("lams", [128, 8])
    w_uq = sb("w_uq", [128, 2, 384], BF16)
    w_ukv = sb("w_ukv", [128, 512], BF16)
    w_glu = sb("w_glu", [128, 2, 512], BF16)
    pT = [sb(f"pT{i}", [128, 512], BF16) for i in range(4)]
    pT_rot = Rot([0, 1, 2, 3])
    tokb = sb("tokb", [128, 1024], BF16)
    h0t = sb("h0t", [128, 2, 32])

    def st_(c0, c1=None):
        return stat[:, c0:(c1 if c1 is not None else c0 + 1)]

    def load_layer_params(l, job):
        for (t, nm, n) in ((gq_g, "gqa_qn_g", 64), (gk_g, "gqa_kn_g", 64), (sub_g, "diff_subln_g", 64),
                           (mq_g, "mla_qn_g", 192), (mkv_g, "mla_kvn_g", 128)):
            S.dma("sp", t[:, 0:n], W[nm].ap()[l:l + 1, :].to_broadcast([128, n]), writes=[nm])
        ts("dve", sub_g[:], sub_g[:], 1.0 - LAM_INIT[l], None, ALU.mult, None, ["diff_subln_g"], ["diff_subln_g"])
        for i, nm in enumerate(("diff_lq1", "diff_lk1", "diff_lq2", "diff_lk2")):
            S.dma("sp", lamt[:, i, :], W[nm].ap()[l:l + 1, :].to_broadcast([128, 32]), writes=[("lamt", i)])
        tt("dve", lamt[:, 0, :], lamt[:, 0, :], lamt[:, 1, :], ALU.mult, [("lamt", 0), ("lamt", 1)], [("lamt", 0)])
        tt("dve", lamt[:, 2, :], lamt[:, 2, :], lamt[:, 3, :], ALU.mult, [("lamt", 2), ("lamt", 3)], [("lamt", 2)])
        E("dve", lambda h: h.tensor_reduce(out=lams[:, 0:1], in_=lamt[:, 0, :], axis=AX.X, op=ALU.add), [("lamt", 0)], ["lams"])
        E("dve", lambda h: h.tensor_reduce(out=lams[:, 1:2], in_=lamt[:, 2, :], axis=AX.X, op=ALU.add), [("lamt", 2)], ["lams"])
        actf(lams[:, 2:4], lams[:, 0:2], AF.Exp, ["lams"], ["lams"])
        tt("dve", lams[:, 4:5], lams[:, 2:3], lams[:, 3:4], ALU.subtract, ["lams"], ["lams"])
        ts("dve", lams[:, 5:6], lams[:, 4:5], -1.0, -LAM_INIT[l], ALU.mult, ALU.add, ["lams"], ["lams"])
        S.dma("pool", w_uq[:, 0, :], W["mla_w_uq"].ap()[l, 0:128, :], writes=["w_uq"])
        S.dma("pool", w_uq[0:64, 1, :], W["mla_w_uq"].ap()[l, 128:192, :], writes=["w_uq"])
        S.dma("pool", w_ukv[:], W["mla_w_ukv"].ap()[l], writes=["w_ukv"])
        S.dma("pool", w_glu[:], W["ssm_w_glu"].ap()[l].rearrange("(k p) n -> p k n", p=128), writes=["w_glu"])

    def AK(i):
        return [("actT", i, kc) for kc in range(8)]

    def load_mod(l, cond, which):
        mc = modcol[:, which]
        MK = ("modcol", which)
        S.dma("sp", modb[:, 0, :],
              modscr.ap()[l, cond:cond + 1, (which * 3 + 2) * D:(which * 3 + 3) * D].to_broadcast([128, D]),
              reads=["modscr"], writes=[("modb", 2)])
        nm = "norm1_g" if which == 0 else "norm2_g"
        srcs = [modscr.ap()[l, cond, (which * 3 + 0) * D:(which * 3 + 1) * D],
                modscr.ap()[l, cond, (which * 3 + 1) * D:(which * 3 + 2) * D],
                W[nm].ap()[l, :]]
        for i_, src in enumerate(srcs):
            S.add("sp", lambda h, i_=i_, src=src: _dma_nc(h, mc[:, i_, :], src.rearrange("(k p) -> p k", p=128)),
                  ["modscr"], [(MK, i_)], dma=True, cost=0.15, lat=4.0)
        stt(mc[:, 1, :], mc[:, 1, :], 1.0, mc[:, 2, :], ALU.add, ALU.mult, [(MK, 1), (MK, 2)], [(MK, 1)])

    def norm_mod_transpose(i, which):
        xk = ("x", i)
        mc = modcol[:, which]
        MK = ("modcol", which)
        par = i % 2
        hbx = [hb, hb2][par]
        HK = HB if par == 0 else ["hb2"]
        sc0, sc1 = 60 + 2 * par, 61 + 2 * par
        actf(hbx[:], x_sb[:, i, :], AF.Square, [xk], [("st", sc0)] + HK, accum_out=st_(sc0))
        rstd_chain(st_(sc0), st_(sc1), 1.0 / D, ("st", sc0), ("st", sc1))
        ts("dve", hbx[:], x_sb[:, i, :], st_(sc1), None, ALU.mult, None, [xk, ("st", sc1)], HK)
        b_ = tr_rot.next()
        trs([(psb(b_)[:, kc * 128:(kc + 1) * 128], hbx[:, kc * 128:(kc + 1) * 128], ident[:, :]) for kc in range(8)],
            HK + ["ident"], [pk(b_)])
        for kc in range(8):
            o_ap = actT[:, kc, i * 128:(i + 1) * 128]
            i_ap = psb(b_)[:, kc * 128:(kc + 1) * 128]
            if kc % 2 == 0:
                E("act", lambda h, o_ap=o_ap, i_ap=i_ap, kc=kc: h.activation(out=o_ap, in_=i_ap, func=AF.Identity,
                                                                             scale=mc[:, 1, kc:kc + 1], bias=mc[:, 0, kc:kc + 1]),
                  [pk(b_), (MK, 0), (MK, 1)], [("actT", i, kc)], cost=0.5)
            else:
                ts("dve", o_ap, i_ap, mc[:, 1, kc:kc + 1], mc[:, 0, kc:kc + 1], ALU.mult, ALU.add,
                   [pk(b_), (MK, 0), (MK, 1)], [("actT", i, kc)])

    def rope(src, src_keys, dst, dst_keys, nh, n, cos_t, sin_t, i):
        Wd = 4 * n
        t1 = tokf[:, 0:nh * Wd]
        t2 = tmpf[:, 0:nh * Wd]
        tt("dve", t1.rearrange("p (h w) -> p h w", h=nh), src.rearrange("p (h w) -> p h w", h=nh),
           cos_t[:, i, :].unsqueeze(1).to_broadcast([128, nh, Wd]), ALU.mult, src_keys + ["cos%d" % Wd], [("tokf", 0)])
        tt("dve", t2.rearrange("p (h w) -> p h w", h=nh), src.rearrange("p (h w) -> p h w", h=nh),
           sin_t[:, i, :].unsqueeze(1).to_broadcast([128, nh, Wd]), ALU.mult, src_keys + ["sin%d" % Wd], ["tmpf"])
        v = lambda a: a.rearrange("p (ha two n) -> p ha two n", two=2, n=n)
        tt("dve", v(dst), v(t1), v(t2)[:, :, ::-1, :], ALU.add, [("tokf", 0), "tmpf"], dst_keys)

    def head_rms(src, nh, hd, g_tile, gkey, dst, rkeys, wkeys, scol):
        actf(tmpf[:, 0:nh * hd], src, AF.Square, rkeys, ["tmpf"])
        E("dve", lambda h: h.tensor_reduce(out=st_(scol, scol + nh), in_=tmpf[:, 0:nh * hd].rearrange("p (h d) -> p h d", h=nh),
                                           axis=AX.X, op=ALU.add), ["tmpf"], [("st", scol)])
        rstd_chain(st_(scol, scol + nh), st_(scol + 8, scol + 8 + nh), 1.0 / hd, ("st", scol), ("st", scol + 8))
        tt("dve", dst.rearrange("p (h d) -> p h d", h=nh), src.rearrange("p (h d) -> p h d", h=nh),
           st_(scol + 8, scol + 8 + nh).unsqueeze(2).to_broadcast([128, nh, hd]), ALU.mult, rkeys + [("st", scol + 8)], wkeys)
        tt("dve", dst.rearrange("p (h d) -> p h d", h=nh), dst.rearrange("p (h d) -> p h d", h=nh),
           g_tile[:, 0:hd].unsqueeze(1).to_broadcast([128, nh, hd]), ALU.mult, wkeys + [gkey], wkeys)

    def kv_expand(kt, kr_src, kr_keys):
        kcol = slice(kt * 128, (kt + 1) * 128)
        bb = aux_rot.next()
        mm(psf(bb)[:, 0:512], [(tokb[:, 256:384], w_ukv[:, :])], [("tokb", 1), "w_ukv"], [pk(bb)])
        mktok = hb[:, 576:960].rearrange("p (h w) -> p h w", h=4)
        kvp = psf(bb)[:, 0:512].rearrange("p (h w) -> p h w", h=4)
        cp("act", mktok[:, :, 0:64], kvp[:, :, 0:64], [pk(bb)], [("hbk", 0)])
        cp("dve", mV[:, kt, :, 0:64], kvp[:, :, 64:128], [pk(bb)], [("mV", kt)])
        cp("dve", mktok[:, :, 64:96], kr_src.unsqueeze(1).to_broadcast([128, 4, 32]), kr_keys + [("hbk", 0)], [("hbk", 0)])
        bt = tr_rot.next()
        trs([(psb(bt)[0:96, h * 128:(h + 1) * 128], hb[:, 576 + h * 96:576 + (h + 1) * 96], ident[:, :]) for h in range(4)],
            [("hbk", 0), "ident"], [pk(bt)])
        cp("act", mkT[0:96, :, kcol], psb(bt)[0:96, 0:512].rearrange("p (h t) -> p h t", h=4), [pk(bt)], [("mkT", kt)])

    def in_proj_tile(job, l, i, kt):
        rp = job["rope"]
        OW = "own"
        ow = own[:, 0, :]
        bounds = [0, 512, 1024, 1536, IN_COLS]
        banks = []
        for c in range(4):
            b_ = mm_rot.next()
            banks.append(b_)
            n0, n1 = bounds[c], bounds[c + 1]
            mm(psf(b_)[:, 0:n1 - n0], [(actT[:, kc, i * 128:(i + 1) * 128], w_in_sb[:, kc, n0:n1]) for kc in range(8)],
               AK(i) + ["w_in"], [pk(b_)])
        b0, b1, b2, b3 = banks
        tcol = slice(i * 128, (i + 1) * 128)
        kcol = slice(kt * 128, (kt + 1) * 128)
        if rp:
            rope(psf(b0)[:, 0:256], [pk(b0)], tokb[:, 0:256], [("tokb", 0)], 8, 8, cos32, sin32, i)
            rope(psf(b0)[:, 256:512], [pk(b0)], tokb[:, 256:512], [("tokb", 1)], 8, 8, cos32, sin32, i)
        else:
            cp("act", tokb[:, 0:256], psf(b0)[:, 0:256], [pk(b0)], [("tokb", 0)])
            cp("dve", tokb[:, 256:512], psf(b0)[:, 256:512], [pk(b0)], [("tokb", 1)])
        if job["caches_out"]:
            cp("act", ow[:, 0:256], psf(b0)[:, 256:512], [pk(b0)], [OW])
        bt = tr_rot.next()
        trs([(psb(bt)[0:64, j * 128:(j + 1) * 128], tokb[:, j * 64:(j + 1) * 64], ident[:, :]) for j in range(8)],
            [("tokb", 0), ("tokb", 1), "ident"], [pk(bt)])
        cp("act", dqT[0:64, :, tcol], psb(bt)[0:64, 0:512].rearrange("p (h t) -> p h t", h=4), [pk(bt)], [("dqT", i)])
        cp("act", dkT[0:64, :, kcol], psb(bt)[0:64, 512:1024].rearrange("p (h t) -> p h t", h=4), [pk(bt)], [("dkT", kt)])
        if job["caches_out"]:
            cp("act", ow[:, 256:512], psf(b1)[:, 0:256], [pk(b1)], [OW])
        cp("act", dV[:, kt, :, 0:64], psf(b1)[:, 0:256].rearrange("p (h d) -> p h d", h=4), [pk(b1)], [("dV", kt)])
        head_rms(psf(b1)[:, 256:512], 4, 64, gq_g, "gqa_qn_g", tokf[:, 256:512], [pk(b1)], [("tokf", 1)], 8)
        if rp:
            rope(tokf[:, 256:512], [("tokf", 1)], tokb[:, 512:768], [("tokb", 2)], 4, 16, cos64, sin64, i)
        else:
            cp("dve", tokb[:, 512:768], tokf[:, 256:512], [("tokf", 1)], [("tokb", 2)])
        head_rms(psf(b2)[:, 0:128], 2, 64, gk_g, "gqa_kn_g", ow[:, 512:640], [pk(b2)], [OW], 24)
        if rp:
            rope(ow[:, 512:640], [OW], tokb[:, 768:896], [("tokb", 3)], 2, 16, cos64, sin64, i)
        else:
            cp("dve", tokb[:, 768:896], ow[:, 512:640], [OW], [("tokb", 3)])
        if job["caches_out"]:
            cp("act", ow[:, 640:768], psf(b2)[:, 128:256], [pk(b2)], [OW])
        cp("act", gV[:, kt, :, 0:64], psf(b2)[:, 128:256].rearrange("p (h d) -> p h d", h=2), [pk(b2)], [("gV", kt)])
        bt = tr_rot.next()
        trs([(psb(bt)[0:64, j * 128:(j + 1) * 128], tokb[:, 512 + j * 64:512 + (j + 1) * 64], ident[:, :]) for j in range(6)],
            [("tokb", 2), ("tokb", 3), "ident"], [pk(bt)])
        cp("act", gqT[0:64, :, tcol], psb(bt)[0:64, 0:512].rearrange("p (h t) -> p h t", h=4), [pk(bt)], [("gqT", i)])
        cp("act", gkT[0:64, :, kcol], psb(bt)[0:64, 512:768].rearrange("p (h t) -> p h t", h=2), [pk(bt)], [("gkT", kt)])
        cp("act", hb[:, 0:256], psf(b2)[:, 256:512], [pk(b2)], [("hb", 0)])
        bt = tr_rot.next()
        trs([(psb(bt)[:, j * 128:(j + 1) * 128], hb[:, j * 128:(j + 1) * 128], ident[:, :]) for j in range(2)],
            [("hb", 0), "ident"], [pk(bt)])
        cp("act", uT[:, :, tcol], psb(bt)[:, 0:256].rearrange("p (k t) -> p k t", k=2), [pk(bt)], [UK])
        actf(junk[:, 0:192], psf(b3)[:, 0:192], AF.Square, [pk(b3)], [("st", 40)], accum_out=st_(40))
        rstd_chain(st_(40), st_(41), 1.0 / 192, ("st", 40), ("st", 41))
        stt(hb[:, 256:448], psf(b3)[:, 0:192], st_(41), mq_g[:, 0:192], ALU.mult, ALU.mult,
            [pk(b3), ("st", 41), "mla_qn_g"], [("hbq", 0)])
        actf(junk[:, 256:384], psf(b3)[:, 192:320], AF.Square, [pk(b3)], [("st", 42)], accum_out=st_(42))
        rstd_chain(st_(42), st_(43), 1.0 / 128, ("st", 42), ("st", 43))
        stt(ow[:, 768:896], psf(b3)[:, 192:320], st_(43), mkv_g[:, 0:128], ALU.mult, ALU.mult,
            [pk(b3), ("st", 43), "mla_kvn_g"], [OW])
        cp("dve", hb[:, 448:576], ow[:, 768:896], [OW], [("hbq", 1)])
        cp("act", ow[:, 896:928], psf(b3)[:, 320:352], [pk(b3)], [OW])
        bt = tr_rot.next()
        trs([(psb(bt)[:, 0:128], hb[:, 256:384], ident[:, :]), (psb(bt)[0:64, 128:256], hb[:, 384:448], ident[:, :]),
             (psb(bt)[:, 256:384], hb[:, 448:576], ident[:, :])], [("hbq", 0), ("hbq", 1), "ident"], [pk(bt)])
        cp("act", tokb[:, 0:128], psb(bt)[:, 0:128], [pk(bt)], [("tokb", 0)])
        cp("act", tokb[0:64, 128:256], psb(bt)[0:64, 128:256], [pk(bt)], [("tokb", 0)])
        cp("dve", tokb[:, 256:384], psb(bt)[:, 256:384], [pk(bt)], [("tokb", 1)])
        ba = aux_rot.next()
        mm(psf(ba)[:, 0:384], [(tokb[:, 0:128], w_uq[:, 0, :]), (tokb[0:64, 128:256], w_uq[0:64, 1, :])],
           [("tokb", 0), "w_uq"], [pk(ba)])
        mqtok = tokb[:, 512:896].rearrange("p (h w) -> p h w", h=4)
        mqp = psf(ba)[:, 0:384].rearrange("p (h w) -> p h w", h=4)
        cp("act", mqtok[:, :, 0:64], mqp[:, :, 0:64], [pk(ba)], [("tokb", 2), ("tokb", 3)])
        if rp:
            cp("dve", tokf[:, 768:896].rearrange("p (h w) -> p h w", h=4), mqp[:, :, 64:96], [pk(ba)], [("tokf", 3)])
            rope(tokf[:, 768:896], [("tokf", 3)], tokf[:, 896:1024], [("tokf", 4)], 4, 8, cos32, sin32, i)
            cp("dve", mqtok[:, :, 64:96], tokf[:, 896:1024].rearrange("p (h w) -> p h w", h=4), [("tokf", 4)],
               [("tokb", 2), ("tokb", 3)])
        else:
            cp("dve", mqtok[:, :, 64:96], mqp[:, :, 64:96], [pk(ba)], [("tokb", 2), ("tokb", 3)])
        bt = tr_rot.next()
        trs([(psb(bt)[0:96, h * 128:(h + 1) * 128], tokb[:, 512 + h * 96:512 + (h + 1) * 96], ident[:, :]) for h in range(4)],
            [("tokb", 2), ("tokb", 3), "ident"], [pk(bt)])
        cp("act", mqT[0:96, :, tcol], psb(bt)[0:96, 0:512].rearrange("p (h t) -> p h t", h=4), [pk(bt)], [("mqT", i)])
        if rp:
            rope(ow[:, 896:928], [OW], tokf[:, 896:928], [("tokf", 4)], 1, 8, cos32, sin32, i)
            kv_expand(kt, tokf[:, 896:928], [("tokf", 4)])
        else:
            kv_expand(kt, ow[:, 896:928], [OW])
        if job["caches_out"]:
            sq = i // 2
            t0 = (i % 2) * 128
            for (dst, c0, c1) in ((ndk_d, 0, 256), (ndv_d, 256, 512), (ngk_d, 512, 640), (ngv_d, 640, 768),
                                  (nckv_d, 768, 896), (nkr_d, 896, 928)):
                S.dma("sp", dst.ap()[sq, l, t0:t0 + 128, :], ow[:, c0:c1], reads=[OW])

    def past_tiles(job, l):
        for half in range(2):
            for (src, c0, c1) in ((cdk_d, 0, 256), (cdv_d, 256, 512), (cgk_d, 512, 640), (cgv_d, 640, 768),
                                  (cckv_d, 768, 896), (ckr_d, 896, 928)):
                S.dma("pool", cache_sb[:, :, c0:c1], src.ap()[l, half * 256:(half + 1) * 256, :].rearrange("(j p) n -> p j n", p=128),
                      writes=[UK])
            for kk in range(2):
                kt = half * 2 + kk
                kcol = slice(kt * 128, (kt + 1) * 128)
                cs_ = cache_sb[:, kk, :]
                bt = tr_rot.next()
                trs([(psb(bt)[0:64, j * 128:(j + 1) * 128], cs_[:, j * 64:(j + 1) * 64], ident[:, :]) for j in range(4)] +
                    [(psb(bt)[0:64, (4 + j) * 128:(5 + j) * 128], cs_[:, 512 + j * 64:512 + (j + 1) * 64], ident[:, :]) for j in range(2)] +
                    [(psb(bt)[:, 768:896], cs_[:, 768:896], ident[:, :])], [UK, "ident"], [pk(bt)])
                cp("act", dkT[0:64, :, kcol], psb(bt)[0:64, 0:512].rearrange("p (h t) -> p h t", h=4), [pk(bt)], [("dkT", kt)])
                cp("dve", gkT[0:64, :, kcol], psb(bt)[0:64, 512:768].rearrange("p (h t) -> p h t", h=2), [pk(bt)], [("gkT", kt)])
                cp("act", tokb[:, 256:384], psb(bt)[:, 768:896], [pk(bt)], [("tokb", 1)])
                cp("dve", dV[:, kt, :, 0:64], cs_[:, 256:512].rearrange("p (h d) -> p h d", h=4), [UK], [("dV", kt)])
                cp("dve", gV[:, kt, :, 0:64], cs_[:, 640:768].rearrange("p (h d) -> p h d", h=2), [UK], [("gV", kt)])
                kv_expand(kt, cs_[:, 896:928], [UK])

    def attention(job, l):
        nseq, Ts, past = job["nseq"], job["Ts"], job["past"]
        nk = (past + Ts) // 128
        qb = min(512, Ts)
        nqb = Ts // qb
        nsub = qb // 128
        o1 = tokf[:, 0:256].rearrange("p (s d) -> p s d", s=4)
        o2 = tokf[:, 256:512].rearrange("p (s d) -> p s d", s=4)
        att_s_rot = Rot([0, 1])
        att_o_rot = Rot([2, 3])
        for sq in range(nseq):
            tile0 = sq * (Ts // 128)
            kt0 = 0 if past else tile0
            items = []
            for h_ in range(4):
                for qi_ in range(nqb):
                    for j_ in range(2):
                        items.append((2 * h_ + j_, qi_))
            for hs_ in range(8, 16):
                for qi_ in range(nqb):
                    items.append((hs_, qi_))
            for (hs, qi) in items:
                if hs < 8:
                    kind, h, j = "d", hs // 2, hs % 2
                    QT, KT, V, vh, r0, r1, scale = dqT, dkT, dV, h, 32 * j, 32 * j + 32, 32 ** -0.5
                    qh, kh = h, h
                elif hs < 12:
                    kind, h, j = "g", hs - 8, 0
                    QT, KT, V, vh, r0, r1, scale = gqT, gkT, gV, h // 2, 0, 64, 64 ** -0.5
                    qh, kh = h, h // 2
                else:
                    kind, h, j = "m", hs - 12, 0
                    QT, KT, V, vh, r0, r1, scale = mqT, mkT, mV, h, 0, 96, 96 ** -0.5
                    qh, kh = h, h
                qkey = {"d": "dqT", "g": "gqT", "m": "mqT"}[kind]
                kkey = {"d": "dkT", "g": "gkT", "m": "mkT"}[kind]
                vkey = {"d": "dV", "g": "gV", "m": "mV"}[kind]
                if True:
                    q0 = tile0 * 128 + qi * qb
                    bo = att_o_rot.next()
                    ops_ = psf(bo)[:, 0:nsub * 65].rearrange("p (s d) -> p s d", s=nsub)
                    qtiles = [tile0 + qi * nsub + s_ for s_ in range(nsub)]
                    for kk in range(nk):
                        kt = kt0 + kk
                        bs = att_s_rot.next()
                        mm(psf(bs)[:, 0:qb], [(KT[r0:r1, kh, kt * 128:(kt + 1) * 128], QT[r0:r1, qh, q0:q0 + qb])],
                           [(kkey, kt)] + [(qkey, t_) for t_ in qtiles], [pk(bs)])
                        pi = pT_rot.next()
                        actf(pT[pi][:, 0:qb], psf(bs)[:, 0:qb], AF.Exp, [pk(bs)], [("pT", pi)], scale=scale)

                        def pv(hh, pi=pi, kt=kt, kk=kk, ops_=ops_, V=V, vh=vh):
                            ins = None
                            for s_ in range(nsub):
                                ins = hh.matmul(ops_[:, s_, :], lhsT=pT[pi][:, s_ * 128:(s_ + 1) * 128], rhs=V[:, kt, vh, :],
                                                start=(kk == 0 and s_ == 0), stop=(kk == nk - 1), skip_group_check=True)
                            return ins
                        E("pe", pv, [("pT", pi), (vkey, kt)], [pk(bo)], cost=0.1 + nsub * 0.09)
                    E("dve", lambda h_, ops_=ops_: h_.reciprocal(st_(48, 48 + nsub), ops_[:, :, 64]), [pk(bo)], [("st", 48)])
                    rec = st_(48, 48 + nsub).unsqueeze(2).to_broadcast([128, nsub, 64])
                    mk_ = [("mixed", t_) for t_ in qtiles]
                    if kind == "d":
                        dst = o1 if j == 0 else o2
                        tt("dve", dst[:, 0:nsub, :], ops_[:, :, 0:64], rec, ALU.mult, [pk(bo), ("st", 48)], [("tokf", j)])
                        if j == 1:
                            dif = tokf[:, 512:768].rearrange("p (s d) -> p s d", s=4)[:, 0:nsub, :]
                            stt(dif, o2[:, 0:nsub, :], lams[:, 5:6], o1[:, 0:nsub, :], ALU.mult, ALU.add,
                                [("tokf", 0), ("tokf", 1), "lams"], [("tokf", 2)])
                            sqv = tmpf[:, 0:nsub * 64].rearrange("p (s d) -> p s d", s=nsub)
                            tt("pool", sqv, dif, dif, ALU.mult, [("tokf", 2)], ["tmpf"])
                            E("dve", lambda h_, sqv=sqv: h_.tensor_reduce(out=st_(52, 52 + nsub), in_=sqv, axis=AX.X, op=ALU.add),
                              ["tmpf"], [("st", 52)])
                            rstd_chain(st_(52, 52 + nsub), st_(56, 56 + nsub), 1.0 / 64, ("st", 52), ("st", 56))
                            tt("dve", dif, dif, st_(56, 56 + nsub).unsqueeze(2).to_broadcast([128, nsub, 64]), ALU.mult,
                               [("tokf", 2), ("st", 56)], [("tokf", 2)])
                            mdst = mixed[:, qtiles[0]:qtiles[0] + nsub, h * 64:(h + 1) * 64]
                            tt("dve", mdst, dif, sub_g[:, 0:64].unsqueeze(1).to_broadcast([128, nsub, 64]), ALU.mult,
                               [("tokf", 2), "diff_subln_g"], mk_)
                    else:
                        c0 = (256 if kind == "g" else 768) + h * 64
                        mdst = mixed[:, qtiles[0]:qtiles[0] + nsub, c0:c0 + 64]
                        tt("dve", mdst, ops_[:, :, 0:64], rec, ALU.mult, [pk(bo), ("st", 48)], mk_)

    def ssm_run(job, l):
        nm = job["name"]
        nseq, Ts = job["nseq"], job["Ts"]
        nch = Ts // 8
        P = f"sr{l}{nm}_"
        RK = P + "region"
        o_ = [16384]
        o2_ = [A_ATT_END + 4096]

        def cv(shape, dt=F32, tail=False):
            n = 1
            for s_ in shape[1:]:
                n *= s_
            esz = 2 if dt == BF16 else 4
            sz = (n * esz + 3) // 4 * 4
            if tail:
                oo = o2_[0]
                o2_[0] += sz
                assert o2_[0] <= ARENA_BYTES, o2_[0]
            else:
                oo = o_[0]
                o_[0] += sz
                assert o_[0] <= 30208, o_[0]
            return carve(oo, shape, dt)

        prm = []
        for i_ in range(2):
            prm.append(dict(M=cv([128, 2, 128], BF16), PS=cv([128, 2, 128], BF16), PSW=cv([128, 2, 128], BF16),
                            Q=cv([128, 2, 128], BF16), CC=cv([128, 2, 128]), SS=cv([128, 2, 128])))
        Ug = [cv([128, 128], BF16) for _ in range(2)]
        Wt = [cv([128, 128]) for _ in range(2)]
        T1 = [cv([128, 128]) for _ in range(2)]
        Gt = [cv([128, 128]) for _ in range(2)]
        Hb = [cv([128, nseq, nch + 1], BF16) for _ in range(4)]
        Yg = [cv([128, 128], BF16) for _ in range(2)]
        Hfin = cv([128, nseq, 32], tail=True)
        g0 = cv([128, 32], tail=True)
        g0t = cv([128, 32], tail=True)
        r8 = cv([128, 32], tail=True)
        c1 = cv([128, 2, 32], tail=True)
        h0b = cv([128, 32], BF16, tail=True)
        ygT = [tokb[:, :], hb[:, :]]
        YGK = [[("tokb", k) for k in range(4)], HB]
        sgl = tmpf[:, 0:256]
        S.dma("sp", r8, ssm_scr_r.ap()[l], reads=[("scr_r", l), RK], writes=[P + "r8"])

        def load_prm(g):
            gs = g % 2
            pr = prm[gs]
            for idx, kk_ in enumerate(("M", "PS", "PSW", "Q")):
                S.dma("sp", pr[kk_], ssm_scr_b.ap()[l, idx].rearrange("p (d g m) -> p d g m", d=2, g=16)[:, :, g, :],
                      reads=[("scr_b", l), RK], writes=[(P + "prm" + kk_, gs)])
            for idx, kk_ in enumerate(("CC", "SS")):
                S.dma("sp", pr[kk_], ssm_scr_f.ap()[l, idx].rearrange("p (d g m) -> p d g m", d=2, g=16)[:, :, g, :],
                      reads=[("scr_f", l), RK], writes=[(P + "prm" + kk_, gs)])

        if job["past"]:
            ld = tokf[0:32, 0:256].rearrange("p (r q) -> p r q", r=2)
            LDK = [("tokf", 0)]
            S.dma("sp", ld[:, 0, 0:64], h0re_d.ap()[l], writes=LDK)
            S.dma("sp", ld[:, 0, 64:128], h0im_d.ap()[l], writes=LDK)
            S.dma("sp", ld[:, 1, 0:64], h0im_d.ap()[l], writes=LDK)
            S.dma("sp", ld[:, 1, 64:128], h0re_d.ap()[l], writes=LDK)
            S.dma("sp", c1, ssm_scr_c1.ap()[l].rearrange("w p g -> p w g"), reads=[("scr_c1", l), RK], writes=[P + "c1"])
            bT_ = 4
            trs([(psf(bT_)[:, r * 32:(r + 1) * 32], ld[:, r, :], identf[0:32, 0:32]) for r in range(2)],
                LDK + ["identf", RK], [pk(bT_)])
            cp("dve", h0t[:].rearrange("p r g -> p (r g)"), psf(bT_)[:, 0:64], [pk(bT_)], ["h0t"])
            cp("dve", h0b, h0t[:, 0, :], ["h0t", RK], [P + "h0b"])
            tt("dve", g0, c1[:, 0, :], h0t[:, 0, :], ALU.mult, [P + "c1", "h0t", RK], [P + "g0"])
            tt("dve", g0t, c1[:, 1, :], h0t[:, 1, :], ALU.mult, [P + "c1", "h0t", RK], [P + "g0t"])
            tt("dve", g0, g0, g0t, ALU.add, [P + "g0", P + "g0t"], [P + "g0"])
        NC_ = nseq * nch
        ybanks = {0: (6, 7), 1: (6, 7)}
        s_rots = {0: Rot([4]), 1: Rot([4])}
        y_rot = Rot([5])
        v3 = lambda a: a[:, 0:NC_].rearrange("p (q k) -> p q k", q=nseq)
        load_prm(0)
        for g in range(16):
            ch, gl = g // 8, g % 8
            ui = g % 2
            gs = g % 2
            pr = prm[gs]
            PKf = lambda nm_, gs=gs: (P + "prm" + nm_, gs)
            s_rot = s_rots[ch]
            if g + 1 < 16:
                load_prm(g + 1)
            bu = s_rot.next()
            mm(psf(bu)[:, 0:NC_], [(rpad[:, gl, 112 - 16 * s_:240 - 16 * s_],
                                    uT[:, ch, :].rearrange("p (k s) -> p s k", s=8)[:, s_, :]) for s_ in range(8)],
               [UK, "rpad"], [pk(bu)])
            cp("act", Ug[ui], psf(bu)[:, 0:NC_], [pk(bu), RK], [(P + "Ug", ui)])
            by = y_rot.next()
            for d_ in range(2):
                dg = d_ * 16 + g
                wi = d_
                bs_ = s_rot.next()

                def ssw(hh, bs_=bs_, d_=d_, ui=ui, pr=pr):
                    hh.matmul(psf(bs_)[:, 0:128], lhsT=pr["PS"][:, d_, :], rhs=Ug[ui], start=True, stop=True)
                    return hh.matmul(psf(bs_)[:, 128:256], lhsT=pr["PSW"][:, d_, :], rhs=Ug[ui], start=True, stop=True)
                E("pe", ssw, [PKf("PS"), PKf("PSW"), (P + "Ug", ui)], [pk(bs_)], cost=0.35)

                def tabv(tab, d_=d_):
                    t_ = tab[:, d_, 0:nch]
                    if d_ == 1:
                        t_ = t_[:, ::-1]
                    return t_.unsqueeze(1).to_broadcast([128, nseq, nch])

                tt("dve", v3(Wt[wi]), v3(psf(bs_)), tabv(pr["CC"]), ALU.mult, [pk(bs_), PKf("CC"), RK], [(P + "Wt", wi)])
                tt("dve", v3(T1[wi]), psf(bs_)[:, 128:256].rearrange("p (q k) -> p q k", q=nseq), tabv(pr["SS"]), ALU.mult,
                   [pk(bs_), PKf("SS"), RK], [(P + "T1", wi)])
                tt("pool", Wt[wi][:, 0:NC_], Wt[wi][:, 0:NC_], T1[wi][:, 0:NC_], ALU.subtract,
                   [(P + "Wt", wi), (P + "T1", wi)], [(P + "Wt", wi)])
                for sq in range(nseq):
                    wv = Wt[wi][:, sq * nch:(sq + 1) * nch]
                    gv = Gt[wi][:, sq * nch:(sq + 1) * nch]
                    if d_ == 1:
                        wv = wv[:, ::-1]
                        gv = gv[:, ::-1]
                    init = g0[:, dg:dg + 1] if job["past"] else 0.0
                    rk = [(P + "Wt", wi), P + "r8", RK] + ([P + "g0"] if job["past"] else [])
                    E("dve", lambda h_, wv=wv, gv=gv, init=init, dg=dg:
                      h_.tensor_tensor_scan(out=gv, data0=r8[:, dg:dg + 1].to_broadcast([128, nch]), data1=wv,
                                            initial=init, op0=ALU.mult, op1=ALU.add), rk, [(P + "Gt", wi)],
                      cost=0.15 + 2 * nch / 960.0)
                bg = s_rot.next()
                mm(psf(bg)[:, 0:NC_], [(j2[:, :], Gt[wi][:, 0:NC_])], [(P + "Gt", wi), "j2"], [pk(bg)])
                tt("dve", v3(Wt[wi]), v3(Gt[wi]), tabv(pr["CC"]), ALU.mult, [(P + "Gt", wi), PKf("CC")], [(P + "Wt", wi)])
                tt("dve", v3(T1[wi]), v3(psf(bg)), tabv(pr["SS"]), ALU.mult, [pk(bg), PKf("SS")], [(P + "T1", wi)])
                hbi = d_ * 2 + (g % 2)
                HBt = Hb[hbi]
                if d_ == 0:
                    hdst, hinit, hrhs = HBt[:, :, 1:nch + 1], HBt[:, :, 0], HBt[:, :, 0:nch]
                else:
                    hdst, hinit, hrhs = HBt[:, :, 0:nch], HBt[:, :, nch], HBt[:, :, 1:nch + 1]
                tt("pool", hdst, v3(Wt[wi]), v3(T1[wi]), ALU.add, [(P + "Wt", wi), (P + "T1", wi), RK], [(P + "Hb", hbi)])
                if job["past"]:
                    cp("dve", hinit, h0b[:, dg:dg + 1].to_broadcast([128, nseq]), [P + "h0b", RK], [(P + "Hb", hbi)])
                else:
                    memset("dve", hinit, 0.0, [(P + "Hb", hbi)])
                if job["caches_out"]:
                    col = nch - 1 if d_ == 0 else 0
                    fv = v3(Wt[wi])[:, :, col]
                    tv = v3(T1[wi])[:, :, col]
                    tt("dve", Hfin[:, :, dg], fv, tv, ALU.add, [(P + "Wt", wi), (P + "T1", wi), RK], [P + "Hfin"])
                mm1(psf(by)[:, 0:NC_], pr["M"][:, d_, :], Ug[ui], d_ == 0, False, [PKf("M"), (P + "Ug", ui)], [pk(by)])
                mm1(v3(psf(by)), pr["Q"][:, d_, :], hrhs, False, d_ == 1, [PKf("Q"), (P + "Hb", hbi)], [pk(by)])
            actf(Yg[ui], psf(by)[:, 0:NC_], AF.Gelu_apprx_tanh, [pk(by), RK], [(P + "Yg", ui)])
            yb = ybanks[ch]
            for tau in range(8):
                pb_ = yb[tau // 4]
                outp = psf(pb_)[:, (tau % 4) * 128:(tau % 4 + 1) * 128]
                mm1(outp, rpad[:, tau, 112 - 16 * gl:240 - 16 * gl], Yg[ui], gl == 0 and tau % 4 == 0, gl == 7,
                    [(P + "Yg", ui), "rpad"], [pk(pb_)], skip=True)
            if gl == 7:
                for half in range(2):
                    pb_ = yb[half]
                    dstv = ygT[ch].rearrange("p (k s) -> p s k", s=8)[:, half * 4:(half + 1) * 4, :]
                    cp("act", dstv, psf(pb_)[:, :].rearrange("p (s k) -> p s k", s=4), [pk(pb_)], YGK[ch])
        for i in range(NT):
            bz = [4, 5][i % 2]
            mm(psf(bz)[:, 0:512], [(ygT[kc][:, i * 128:(i + 1) * 128], w_glu[:, kc, :]) for kc in range(2)],
               YGK[0] + YGK[1] + ["w_glu"], [pk(bz)])
            actf(sgl, psf(bz)[:, 256:512], AF.Sigmoid, [pk(bz)], ["tmpf"])
            tt("dve", mixed[:, i, 512:768], psf(bz)[:, 0:256], sgl, ALU.mult, [pk(bz), "tmpf", RK], [("mixed_s", i)])
        if job["caches_out"]:
            bT_ = 4
            trs([(psf(bT_)[:, 0:128], Hfin.rearrange("p q g -> p (q g)"), identf[:, :])], [P + "Hfin", "identf"], [pk(bT_)])
            cp("dve", tokf[:, 0:128], psf(bT_)[:, 0:128], [pk(bT_)], [("tokf", 0)])
            for sq in range(nseq):
                S.dma("sp", nsre_d.ap()[sq, l], tokf[sq * 32:(sq + 1) * 32, 0:64], reads=[("tokf", 0)])
                S.dma("sp", nsim_d.ap()[sq, l], tokf[sq * 32:(sq + 1) * 32, 64:128], reads=[("tokf", 0)])
        return [RK] + [(P + "prm" + n_, i_) for n_ in ("M", "PS", "PSW", "Q", "CC", "SS") for i_ in range(2)] + [P + "r8", P + "c1", P + "Hfin", P + "h0b", P + "g0", P + "g0t"] + \
               [(P + k, i_) for k in ("Ug", "Wt", "T1", "Gt", "Yg") for i_ in range(2)] + [(P + "Hb", i_) for i_ in range(4)]

    def out_proj(job, l, ssm_keys):
        nm = job["name"]
        P = f"op{l}{nm}_"
        w_out_sb = carve(30208, [128, 8, 1024], BF16)
        S.alias(["w_out"], ATT_KEYS)
        S.dma("pool", w_out_sb, W["w_out"].ap()[l].rearrange("(k p) n -> p k n", p=128), writes=["w_out"])
        for i in range(NT):
            b_ = tr_rot.next()
            trs([(psb(b_)[:, kc * 128:(kc + 1) * 128], mixed[:, i, kc * 128:(kc + 1) * 128], ident[:, :]) for kc in range(8)],
                [("mixed", i), ("mixed_s", i), "ident"], [pk(b_)])
            cp("act", actT[:, :, i * 128:(i + 1) * 128], psb(b_)[:, :].rearrange("p (k t) -> p k t", k=8), [pk(b_)],
               AK(i))
        for i in range(NT):
            for nh in range(2):
                b_ = mm_rot.next()
                mm(psf(b_)[:, :], [(actT[:, kc, i * 128:(i + 1) * 128], w_out_sb[:, kc, nh * 512:(nh + 1) * 512]) for kc in range(8)],
                   AK(i) + ["w_out"], [pk(b_)])
                tb_, tk_ = [(tmpf, ["tmpf"]), (tokf, [("tokf", 0), ("tokf", 1)])][nh]
                tt("dve", tb_[:, 0:512], psf(b_)[:, :], modb[:, 0, nh * 512:(nh + 1) * 512], ALU.mult, [pk(b_), ("modb", 2)], tk_)
                tt("dve", x_sb[:, i, nh * 512:(nh + 1) * 512], x_sb[:, i, nh * 512:(nh + 1) * 512], tb_[:, 0:512], ALU.add,
                   tk_ + [("x", i)], [("x", i)])

    def mlp(job, l):
        nm = job["name"]
        P = f"ml{l}{nm}_"
        aT = carve(0, [128, 32, 1024], BF16)
        w1b = [carve(65536 + i * 4096, [128, 8, 256], BF16) for i in range(2)]
        w2b = [carve(65536 + 8192 + i * 16384, [128, 32, 256], BF16) for i in range(2)]
        RK = P + "region"
        S.alias([RK], ["w_out", "w_in", UK] + MIX_KEYS + ATT_KEYS + list(job.get("ssm_keys", [])))
        for jb in range(16):
            bi = jb % 2
            S.dma("pool", w1b[bi], W["mlp_w1"].ap()[l, :, jb * 256:(jb + 1) * 256].rearrange("(k p) n -> p k n", p=128),
                  reads=[RK], writes=[(P + "w1", bi)])
            for hc in range(2):
                j = jb * 2 + hc
                for tb in range(2):
                    b_ = mm_rot.next()
                    mm(psf(b_)[:, :], [(w1b[bi][:, kc, hc * 128:(hc + 1) * 128], actT[:, kc, tb * 512:(tb + 1) * 512]) for kc in range(8)],
                       [(P + "w1", bi)] + [k_ for t in range(tb * 4, tb * 4 + 4) for k_ in AK(t)], [pk(b_)])
                    pi = pT_rot.next()
                    actf(pT[pi][:, :], psf(b_)[:, :], AF.Relu, [pk(b_)], [("pT", pi)])
                    tt("dve", aT[:, j, tb * 512:(tb + 1) * 512], pT[pi][:, :], pT[pi][:, :], ALU.mult, [("pT", pi), RK], [(P + "aT", tb)])
        for q in range(4):
            bi = q % 2
            S.dma("pool", w2b[bi], W["mlp_w2"].ap()[l, :, q * 256:(q + 1) * 256].rearrange("(j p) n -> p j n", p=128),
                  reads=[RK], writes=[(P + "w2", bi)])
            for i in range(NT):
                b_ = mm_rot.next()
                mm(psf(b_)[:, 0:256], [(aT[:, j, i * 128:(i + 1) * 128], w2b[bi][:, j, :]) for j in range(32)],
                   [(P + "aT", i // 4), (P + "w2", bi)], [pk(b_)])
                tb_, tk_ = [(tmpf, ["tmpf"]), (tokf, [("tokf", 0)])][i % 2]
                tt("dve", tb_[:, 0:256], psf(b_)[:, 0:256], modb[:, 0, q * 256:(q + 1) * 256], ALU.mult, [pk(b_), ("modb", 2)], tk_)
                tt("dve", x_sb[:, i, q * 256:(q + 1) * 256], x_sb[:, i, q * 256:(q + 1) * 256], tb_[:, 0:256], ALU.add,
                   tk_ + [("x", i)], [("x", i)])
        return [RK, (P + "aT", 0), (P + "aT", 1), (P + "w1", 0), (P + "w1", 1), (P + "w2", 0), (P + "w2", 1)]

    def run_job(job, prev_keys):
        S.alias(["w_in", UK] + ATT_KEYS + MIX_KEYS, prev_keys)
        for i in range(NT):
            S.dma("sp", x_sb[:, i, :], job["x_d"].ap()[i * 128:(i + 1) * 128, :], writes=[("x", i)])
        mlp_keys = None
        for l in range(2):
            if mlp_keys is not None:
                S.alias(["w_in", UK] + ATT_KEYS + MIX_KEYS, mlp_keys)
            S.dma("pool", w_in_sb, W["w_in"].ap()[l].rearrange("(k p) n -> p k n", p=128), writes=["w_in"])
            load_layer_params(l, job)
            load_mod(l, job["cond"], 0)
            for (t_, key) in ((dV, "dV"), (gV, "gV"), (mV, "mV")):
                for kt in range(12):
                    memset("pool", t_[:, kt, :, 64:65], 1.0, [(key, kt)])
            for i in range(NT):
                norm_mod_transpose(i, 0)
            if job["past"]:
                past_tiles(job, l)
            for i in range(NT):
                kt = (job["past"] // 128 + i) if job["past"] else i
                in_proj_tile(job, l, i, kt)
            if stop_after == "inproj":
                return
            S.alias(MIX_KEYS, ["w_in"])
            S.alias([f"sr{l}{job['name']}_region"], ["w_in", UK] + ATT_KEYS)
            attention(job, l)
            if stop_after == "attn":
                return
            ssm_keys = ssm_run(job, l)
            dbg(f"mixed{job['name']}{l}", mixed.rearrange("p t c -> p (t c)"), [128, NT * 1024], MIX_KEYS, q="pool")
            if stop_after == "ssm":
                return
            job["ssm_keys"] = ssm_keys
            out_proj(job, l, ssm_keys)
            load_mod(l, job["cond"], 1)
            for i in range(NT):
                norm_mod_transpose(i, 1)
            mlp_keys = mlp(job, l)
        S.alias([UK], mlp_keys)
        S.dma("sp", n_g, W["final_norm_g"].ap().unsqueeze(0).to_broadcast([128, D]), writes=[UK])
        for i in range(NT):
            xk = ("x", i)
            actf(junk[:], x_sb[:, i, :], AF.Square, [xk], [("st", 0)], accum_out=st_(0))
            rstd_chain(st_(0), st_(1), 1.0 / D, ("st", 0), ("st", 1))
            stt(tmpf[:], x_sb[:, i, :], st_(1), n_g, ALU.mult, ALU.mult, [xk, ("st", 1), UK], ["tmpf"])
            S.dma("sp", job["y_d"].ap()[i * 128:(i + 1) * 128, :], tmpf[:], reads=["tmpf"])
        return mlp_keys + [UK]

    jobP = dict(name="P", nseq=4, Ts=256, past=0, rope=False, cond=0, x_d=xp_d, y_d=yp_d, caches_out=True)
    jobS = dict(name="S", nseq=1, Ts=1024, past=512, rope=True, cond=1, x_d=xs_d, y_d=ys_d, caches_out=False)
    arena_setup_keys = [k for k in setup_keys if (isinstance(k, str) and k.startswith("sb")) or (isinstance(k, tuple) and k[0] == "wada")]
    kP = run_job(jobP, arena_setup_keys)
    if stop_after is None or stop_after == "all":
        run_job(jobS, kP)
    elif stop_after == "sample_inproj":
        pass
    st = S.emit()
    return nc, dbg_outs, st


_CACHE = {}


def _get_program():
    if "nc" not in _CACHE:
        nc, _, st = build_program()
        _CACHE["nc"] = nc
    return _CACHE["nc"]


def make_in_maps(inputs):
    f = lambda a: np.ascontiguousarray(np.asarray(a, dtype=np.float32))
    x_prompt = f(inputs["x_prompt"])
    x_sample = f(inputs["x_sample"])
    wnames = ["norm1_g", "norm2_g", "w_ada", "b_ada", "w_in", "w_out", "diff_lq1", "diff_lk1", "diff_lq2", "diff_lk2",
              "diff_subln_g", "gqa_qn_g", "gqa_kn_g", "ssm_log_dt", "ssm_d", "ssm_w_glu", "mla_qn_g", "mla_kvn_g",
              "mla_w_uq", "mla_w_ukv", "mlp_w1", "mlp_w2", "final_norm_g"]
    shared = {n: f(inputs[n]) for n in wnames}
    shared["ssm_a_re"] = f(inputs["ssm_a_re"]).reshape(2, 32, 64)
    shared["ssm_a_im"] = f(inputs["ssm_a_im"]).reshape(2, 32, 64)
    shared["ssm_log_dt"] = f(inputs["ssm_log_dt"]).reshape(2, 32)
    shared["ssm_b_re"] = f(inputs["ssm_b_re"]).reshape(2, 32, 1024)
    shared["ssm_b_im"] = f(inputs["ssm_b_im"]).reshape(2, 32, 1024)
    shared["ssm_c_re"] = f(inputs["ssm_c_re"]).reshape(2, 512, 64)
    shared["ssm_c_im"] = f(inputs["ssm_c_im"]).reshape(2, 512, 64)
    c = f(inputs["c"])
    c_ctx = f(inputs["c_ctx"])
    maps = []
    for core in range(8):
        b = core // 2
        m = dict(shared)
        m["xp"] = x_prompt[4 * core:4 * core + 4].reshape(1024, D)
        m["xs"] = x_sample[b]
        m["cvec"] = np.stack([c_ctx, c[b]], axis=0)
        m["cdk"] = f(inputs["cache_diff_k"])[b].reshape(2, 512, 256)
        m["cdv"] = f(inputs["cache_diff_v"])[b].reshape(2, 512, 256)
        m["cgk"] = f(inputs["cache_gqa_k"])[b].reshape(2, 512, 128)
        m["cgv"] = f(inputs["cache_gqa_v"])[b].reshape(2, 512, 128)
        m["cckv"] = f(inputs["cache_mla_ckv"])[b].reshape(2, 512, 128)
        m["ckr"] = f(inputs["cache_mla_krope"])[b].reshape(2, 512, 32)
        m["h0re"] = f(inputs["state_ssm_re"])[b].reshape(2, 32, 64)
        m["h0im"] = f(inputs["state_ssm_im"])[b].reshape(2, 32, 64)
        maps.append(m)
    return maps


def assemble(results):
    y_prompt = np.concatenate([r["yp"].reshape(4, 256, D) for r in results], axis=0)
    y_sample = np.stack([np.concatenate([results[2 * b]["ys"][0:512], results[2 * b + 1]["ys"][512:1024]], axis=0)
                         for b in range(4)], axis=0)
    cat = lambda k, shp: np.concatenate([r[k].reshape(shp) for r in results], axis=0)
    ndk = cat("ndk", (4, 2, 256, 4, 64))
    ndv = cat("ndv", (4, 2, 256, 4, 64))
    ngk = cat("ngk", (4, 2, 256, 2, 64))
    ngv = cat("ngv", (4, 2, 256, 2, 64))
    nckv = cat("nckv", (4, 2, 256, 128))
    nkr = cat("nkr", (4, 2, 256, 32))
    nsre = cat("nsre", (4, 2, 2, 16, 64))
    nsim = cat("nsim", (4, 2, 2, 16, 64))
    outs = (y_prompt, y_sample, ndk, ndv, ngk, ngv, nckv, nkr, nsre, nsim)
    return tuple(np.ascontiguousarray(o, dtype=np.float32) for o in outs)


def kernel(**inputs):
    nc = _get_program()
    in_maps = make_in_maps(inputs)
    res = run_bass_kernel_spmd(nc, in_maps, core_ids=list(range(8)))
    return assemble(res.results)
```
